# Optimizing a Trainium2 kernel written in Bass

```python
import jax
import jax.numpy as jnp
from jax import lax
import numpy as np

D_MODEL = 1024
BATCH = 2
SEQ = 8192
DEPTH = 4
DEC_BATCH = 128
DEC_SEQ = 8
PAST_LEN = 2048
PAGE_SIZE = 128

HEAD_DIM = 64
ROPE_THETA = 10000.0
NORM_EPS = 1e-6
N_MIXERS = 3

DIL_GROUPS = ((128, 1), (512, 4), (2048, 16))
N_GROUPS_A = len(DIL_GROUPS)
HQ_A = 8
HKV_A = 2
Q_A = N_GROUPS_A * HQ_A * HEAD_DIM
KV_A = N_GROUPS_A * HKV_A * HEAD_DIM
OUT_A = HQ_A * HEAD_DIM
IN_A = Q_A + 2 * KV_A + OUT_A
A_Q_BLOCK = 128

HQ_B = 16
HKV_B = 4
MOBA_BLOCK = 256
MOBA_TOPK = 3
MOBA_Q_BLOCK = 32
Q_B = HQ_B * HEAD_DIM
KV_B = HKV_B * HEAD_DIM
OUT_B = Q_B
IN_B = Q_B + 2 * KV_B + OUT_B

HQ_C = 16
HKV_C = 4
N_IDX_HEADS = 8
IDX_DIM = 64
DSA_TOPK = 256
DSA_Q_BLOCK = 128
Q_C = HQ_C * HEAD_DIM
KV_C = HKV_C * HEAD_DIM
OUT_C = Q_C
IN_C = Q_C + 2 * KV_C + OUT_C + N_IDX_HEADS * IDX_DIM + IDX_DIM + N_IDX_HEADS

IN_WIDTH = (IN_A, IN_B, IN_C)
OUT_WIDTH = (OUT_A, OUT_B, OUT_C)
F32 = jnp.float32

kernel_name = 'hybrid_dilated_moba_dsa_decoder_step'


def rms_norm(x, gain):
    xf = x.astype(F32)
    y = xf * lax.rsqrt(jnp.mean(xf * xf, axis=-1, keepdims=True) + NORM_EPS)
    return (y * gain.astype(F32)).astype(x.dtype)


def rope(x, pos):
    half = x.shape[-1] // 2
    inv_freq = ROPE_THETA ** (-jnp.arange(half, dtype=F32) / half)
    ang = pos.astype(F32)[:, None] * inv_freq[None, :]
    shape = (1, pos.shape[0]) + (1,) * (x.ndim - 3) + (half,)
    cos = jnp.cos(ang).reshape(shape)
    sin = jnp.sin(ang).reshape(shape)
    xf = x.astype(F32)
    x1, x2 = xf[..., :half], xf[..., half:]
    return jnp.concatenate([x1 * cos - x2 * sin, x2 * cos + x1 * sin], axis=-1).astype(x.dtype)


def split_cols(z, sizes):
    cuts = [int(c) for c in np.cumsum(sizes)[:-1]]
    return jnp.split(z, cuts, axis=-1)


def gather_pages(pool, page_table):
    rows = pool[page_table]
    return rows.reshape((page_table.shape[0], page_table.shape[1] * pool.shape[1]) + pool.shape[2:])


def to_blocks(a, block):
    b, t = a.shape[:2]
    return a.reshape((b, t // block, block) + a.shape[2:]).swapaxes(0, 1)


def from_blocks(a):
    nb, b, blk = a.shape[:3]
    return a.swapaxes(0, 1).reshape((b, nb * blk) + a.shape[3:])


def project_a(h, pos, w_in, q_gain, k_gain):
    b, t, _ = h.shape
    z = jnp.einsum('btd,de->bte', h, w_in)
    q, k, v, gate = split_cols(z, (Q_A, KV_A, KV_A, OUT_A))
    q = q.reshape(b, t, N_GROUPS_A, HKV_A, HQ_A // HKV_A, HEAD_DIM)
    k = k.reshape(b, t, N_GROUPS_A, HKV_A, HEAD_DIM)
    v = v.reshape(b, t, N_GROUPS_A, HKV_A, HEAD_DIM)
    q = rope(rms_norm(q, q_gain), pos)
    k = rope(rms_norm(k, k_gain), pos)
    return q, k, v, gate


def dilated_group_attend(q, kc, vc, qidx, window, dilation):
    offs = dilation * jnp.arange(window // dilation + 1)
    idx = qidx[:, None] - offs[None, :]
    valid = idx >= 0
    idx = jnp.maximum(idx, 0)
    kg = jnp.take(kc, idx, axis=1)
    vg = jnp.take(vc, idx, axis=1)
    s = jnp.einsum('btgrd,btjgd->btgrj', q, kg, preferred_element_type=F32) * (HEAD_DIM ** -0.5)
    s = jnp.where(valid[None, :, None, None, :], s, -jnp.inf)
    m = jnp.max(s, axis=-1, keepdims=True)
    p = jnp.exp(s - m)
    den = jnp.sum(p, axis=-1)
    o = jnp.einsum('btgrj,btjgd->btgrd', p, vg, preferred_element_type=F32) / den[..., None]
    return o, m[..., 0] + jnp.log(den)


def dilated_mix(q, k_ctx, v_ctx, q_idx):
    outs, lses = [], []
    for g, (window, dilation) in enumerate(DIL_GROUPS):
        o, lse = dilated_group_attend(q[:, :, g], k_ctx[g], v_ctx[g], q_idx[g], window, dilation)
        outs.append(o)
        lses.append(lse)
    wgt = jax.nn.softmax(jnp.stack(lses, 0), axis=0)
    o = jnp.einsum('nbtgr,nbtgrd->btgrd', wgt, jnp.stack(outs, 0))
    return o.reshape(o.shape[:2] + (OUT_A,))


def mixer_a(hp, hs, states, w_in, q_gain, k_gain):
    tp, ts = hp.shape[1], hs.shape[1]
    qp, kp, vp, gp = project_a(hp, jnp.arange(tp), w_in, q_gain, k_gain)
    qs, ks, vs, gs = project_a(hs, PAST_LEN + jnp.arange(ts), w_in, q_gain, k_gain)
    kp_g = [kp[:, :, g] for g in range(N_GROUPS_A)]
    vp_g = [vp[:, :, g] for g in range(N_GROUPS_A)]

    def step(args):
        qb, ib = args
        return dilated_mix(qb, kp_g, vp_g, (ib,) * N_GROUPS_A)

    o_p = from_blocks(lax.map(step, (to_blocks(qp, A_Q_BLOCK), jnp.arange(tp).reshape(-1, A_Q_BLOCK))))
    ctxs = [jnp.concatenate([states[g], jnp.stack([ks[:, :, g], vs[:, :, g]], axis=2)], axis=1)
            for g in range(N_GROUPS_A)]
    q_idx_s = [states[g].shape[1] + jnp.arange(ts) for g in range(N_GROUPS_A)]
    o_s = dilated_mix(qs, [c[:, :, 0] for c in ctxs], [c[:, :, 1] for c in ctxs], q_idx_s)
    new_state = []
    for g, (window, _) in enumerate(DIL_GROUPS):
        new_state.append(jnp.stack([kp[:, :, g], vp[:, :, g]], axis=2)[:, tp - min(window, tp):])
        lc = ctxs[g].shape[1]
        new_state.append(ctxs[g][:, lc - min(window, lc):])
    return (o_p.astype(hp.dtype) * jax.nn.silu(gp), o_s.astype(hs.dtype) * jax.nn.silu(gs), new_state)


def project_heads(h, pos, w_in, q_gain, k_gain, hq, hkv, extra_sizes):
    b, t, _ = h.shape
    z = jnp.einsum('btd,de->bte', h, w_in)
    q, k, v, gate, *rest = split_cols(z, (hq * HEAD_DIM, hkv * HEAD_DIM, hkv * HEAD_DIM, hq * HEAD_DIM) + tuple(extra_sizes))
    q = rope(rms_norm(q.reshape(b, t, hkv, hq // hkv, HEAD_DIM), q_gain), pos)
    k = rope(rms_norm(k.reshape(b, t, hkv, HEAD_DIM), k_gain), pos)
    v = v.reshape(b, t, hkv, HEAD_DIM)
    return q, k, v, gate, rest


def moba_blocks(kc, vc):
    b, l, g, dh = kc.shape
    nbk = -(-l // MOBA_BLOCK)
    pad = nbk * MOBA_BLOCK - l

    def blocks(a):
        a = jnp.pad(a, ((0, 0), (0, pad), (0, 0), (0, 0)))
        return a.reshape(b, nbk, MOBA_BLOCK, g, dh).transpose(0, 3, 1, 2, 4)

    kb, vb = blocks(kc), blocks(vc)
    return kb, vb, jnp.mean(kb.astype(F32), axis=3)


def moba_attend(q, qpos, kb, vb, kmean):
    b, tq, g, r, dh = q.shape
    nbk = kb.shape[2]
    scale = HEAD_DIM ** -0.5
    own = qpos // MOBA_BLOCK
    bs = jnp.einsum('btgrd,bgnd->btgrn', q.astype(F32), kmean)
    past = jnp.arange(nbk)[None, :] < own[:, None]
    bs = jnp.where(past[None, :, None, None, :], bs, -jnp.inf)
    n_sel = min(MOBA_TOPK, nbk)
    _, sel = lax.top_k(bs, n_sel)
    sel_ok = sel < own[None, :, None, None, None]
    b_ix = jnp.arange(b)[:, None, None, None, None]
    g_ix = jnp.arange(g)[None, None, :, None, None]
    ks = kb[b_ix, g_ix, sel]
    vs = vb[b_ix, g_ix, sel]
    s_sel = jnp.einsum('btgrd,btgrjkd->btgrjk', q, ks, preferred_element_type=F32) * scale
    s_sel = jnp.where(sel_ok[..., None], s_sel, -jnp.inf).reshape(b, tq, g, r, n_sel * MOBA_BLOCK)
    ko = jnp.take(kb, own, axis=2)
    vo = jnp.take(vb, own, axis=2)
    s_own = jnp.einsum('btgrd,bgtkd->btgrk', q, ko, preferred_element_type=F32) * scale
    own_pos = own[:, None] * MOBA_BLOCK + jnp.arange(MOBA_BLOCK)[None, :]
    s_own = jnp.where((own_pos <= qpos[:, None])[None, :, None, None, :], s_own, -jnp.inf)
    p = jax.nn.softmax(jnp.concatenate([s_sel, s_own], axis=-1), axis=-1)
    p_sel = p[..., :n_sel * MOBA_BLOCK].reshape(b, tq, g, r, n_sel, MOBA_BLOCK)
    p_own = p[..., n_sel * MOBA_BLOCK:]
    o = (jnp.einsum('btgrjk,btgrjkd->btgrd', p_sel, vs, preferred_element_type=F32)
         + jnp.einsum('btgrk,bgtkd->btgrd', p_own, vo, preferred_element_type=F32))
    return o.reshape(b, tq, g * r * dh)


def mixer_b(hp, hs, states, w_in, q_gain, k_gain, page_table):
    cache_k, cache_v = states
    tp, ts = hp.shape[1], hs.shape[1]
    qp, kp, vp, gp, _ = project_heads(hp, jnp.arange(tp), w_in, q_gain, k_gain, HQ_B, HKV_B, ())
    pos_s = PAST_LEN + jnp.arange(ts)
    qs, ks, vs, gs, _ = project_heads(hs, pos_s, w_in, q_gain, k_gain, HQ_B, HKV_B, ())
    kb, vb, km = moba_blocks(kp, vp)

    def step_p(args):
        qb, pb = args
        return moba_attend(qb, pb, kb, vb, km)

    o_p = from_blocks(lax.map(step_p, (to_blocks(qp, MOBA_Q_BLOCK), jnp.arange(tp).reshape(-1, MOBA_Q_BLOCK))))
    kc = jnp.concatenate([gather_pages(cache_k, page_table), ks], axis=1)
    vc = jnp.concatenate([gather_pages(cache_v, page_table), vs], axis=1)
    kbs, vbs, kms = moba_blocks(kc, vc)

    def step_s(args):
        qq, kk, vv, mm = args
        return moba_attend(qq[None], pos_s, kk[None], vv[None], mm[None])[0]

    o_s = lax.map(step_s, (qs, kbs, vbs, kms))
    new_state = [kp, ks, vp, vs]
    return (o_p.astype(hp.dtype) * jax.nn.silu(gp), o_s.astype(hs.dtype) * jax.nn.silu(gs), new_state)


def dsa_attend(q, q_idx, w_idx, qpos, kc, vc, kidx_c, n_sel):
    b = q.shape[0]
    l = kc.shape[1]
    dots = jnp.einsum('bthd,bsd->bths', q_idx, kidx_c, preferred_element_type=F32) * (IDX_DIM ** -0.5)
    score = jnp.einsum('bths,bth->bts', jax.nn.relu(dots), w_idx.astype(F32)) * (N_IDX_HEADS ** -0.5)
    admissible = jnp.arange(l)[None, :] <= qpos[:, None]
    score = jnp.where(admissible[None], score, -jnp.inf)
    _, sel = lax.top_k(score, n_sel)
    ok = sel <= qpos[None, :, None]
    b_ix = jnp.arange(b)[:, None, None]
    ks = kc[b_ix, sel]
    vs = vc[b_ix, sel]
    s = jnp.einsum('btgrd,btjgd->btgrj', q, ks, preferred_element_type=F32) * (HEAD_DIM ** -0.5)
    s = jnp.where(ok[:, :, None, None, :], s, -jnp.inf)
    p = jax.nn.softmax(s, axis=-1)
    o = jnp.einsum('btgrj,btjgd->btgrd', p, vs, preferred_element_type=F32)
    return o.reshape(o.shape[:2] + (-1,))


def project_c(h, pos, w_in, q_gain, k_gain):
    b, t, _ = h.shape
    q, k, v, gate, rest = project_heads(h, pos, w_in, q_gain, k_gain, HQ_C, HKV_C,
                                        (N_IDX_HEADS * IDX_DIM, IDX_DIM, N_IDX_HEADS))
    qi, ki, wi = rest
    qi = rope(qi.reshape(b, t, N_IDX_HEADS, IDX_DIM), pos)
    ki = rope(ki, pos)
    return q, k, v, gate, qi, ki, wi


def mixer_c(hp, hs, states, w_in, q_gain, k_gain, page_table):
    cache_k, cache_v, cache_kidx = states
    tp, ts = hp.shape[1], hs.shape[1]
    qp, kp, vp, gp, qip, kip, wip = project_c(hp, jnp.arange(tp), w_in, q_gain, k_gain)
    pos_s = PAST_LEN + jnp.arange(ts)
    qs, ks, vs, gs, qis, kis, wis = project_c(hs, pos_s, w_in, q_gain, k_gain)
    n_sel_p = min(DSA_TOPK, tp // 4)

    def step_p(args):
        qb, qib, wib, pb = args
        return dsa_attend(qb, qib, wib, pb, kp, vp, kip, n_sel_p)

    o_p = from_blocks(lax.map(step_p, (to_blocks(qp, DSA_Q_BLOCK), to_blocks(qip, DSA_Q_BLOCK),
                                       to_blocks(wip, DSA_Q_BLOCK), jnp.arange(tp).reshape(-1, DSA_Q_BLOCK))))
    kc = jnp.concatenate([gather_pages(cache_k, page_table), ks], axis=1)
    vc = jnp.concatenate([gather_pages(cache_v, page_table), vs], axis=1)
    kic = jnp.concatenate([gather_pages(cache_kidx, page_table), kis], axis=1)
    n_sel_s = min(DSA_TOPK, kc.shape[1] // 4)
    o_s = dsa_attend(qs, qis, wis, pos_s, kc, vc, kic, n_sel_s)
    new_state = [kp, ks, vp, vs, kip, kis]
    return (o_p.astype(hp.dtype) * jax.nn.silu(gp), o_s.astype(hs.dtype) * jax.nn.silu(gs), new_state)


def setup_inputs(seed: int = 0) -> dict:
    key = jax.random.key(seed)
    keys = iter(jax.random.split(key, 8 + 8 * DEPTH))

    def normal(shape, scale=1.0):
        return scale * jax.random.normal(next(keys), shape, F32)

    n_pages = PAST_LEN // PAGE_SIZE
    n_used = DEC_BATCH * n_pages
    n_pool = n_used + max(1, n_used // 4)
    inp = {'x_prompt': normal((BATCH, SEQ, D_MODEL)), 'x_sample': normal((DEC_BATCH, DEC_SEQ, D_MODEL))}
    for i in range(DEPTH):
        kind = i % N_MIXERS
        if kind == 0:
            for window, _ in DIL_GROUPS:
                inp[f'state_l{i}_kv_w{window}'] = normal((DEC_BATCH, min(window, PAST_LEN), 2, HKV_A, HEAD_DIM))
        elif kind == 1:
            inp[f'cache_l{i}_k'] = normal((n_pool, PAGE_SIZE, HKV_B, HEAD_DIM))
            inp[f'cache_l{i}_v'] = normal((n_pool, PAGE_SIZE, HKV_B, HEAD_DIM))
        else:
            inp[f'cache_l{i}_k'] = normal((n_pool, PAGE_SIZE, HKV_C, HEAD_DIM))
            inp[f'cache_l{i}_v'] = normal((n_pool, PAGE_SIZE, HKV_C, HEAD_DIM))
            inp[f'cache_l{i}_kidx'] = normal((n_pool, PAGE_SIZE, IDX_DIM))
    perm = jax.random.permutation(next(keys), n_pool)
    inp['page_table'] = perm[:n_used].reshape(DEC_BATCH, n_pages).astype(jnp.int32)
    for i in range(DEPTH):
        kind = i % N_MIXERS
        inp[f'l{i}_norm'] = 1.0 + normal((D_MODEL,), 0.02)
        inp[f'l{i}_w_in'] = normal((D_MODEL, IN_WIDTH[kind]), D_MODEL ** -0.5)
        inp[f'l{i}_q_norm'] = 1.0 + normal((HEAD_DIM,), 0.02)
        inp[f'l{i}_k_norm'] = 1.0 + normal((HEAD_DIM,), 0.02)
        inp[f'l{i}_w_out'] = normal((OUT_WIDTH[kind], D_MODEL), OUT_WIDTH[kind] ** -0.5)
    return inp


def reference(x_prompt, x_sample,
              state_l0_kv_w128, state_l0_kv_w512, state_l0_kv_w2048,
              cache_l1_k, cache_l1_v,
              cache_l2_k, cache_l2_v, cache_l2_kidx,
              state_l3_kv_w128, state_l3_kv_w512, state_l3_kv_w2048,
              page_table,
              l0_norm, l0_w_in, l0_q_norm, l0_k_norm, l0_w_out,
              l1_norm, l1_w_in, l1_q_norm, l1_k_norm, l1_w_out,
              l2_norm, l2_w_in, l2_q_norm, l2_k_norm, l2_w_out,
              l3_norm, l3_w_in, l3_q_norm, l3_k_norm, l3_w_out):
    layer_states = (
        (state_l0_kv_w128, state_l0_kv_w512, state_l0_kv_w2048),
        (cache_l1_k, cache_l1_v),
        (cache_l2_k, cache_l2_v, cache_l2_kidx),
        (state_l3_kv_w128, state_l3_kv_w512, state_l3_kv_w2048),
    )
    layer_params = (
        (l0_norm, l0_w_in, l0_q_norm, l0_k_norm, l0_w_out),
        (l1_norm, l1_w_in, l1_q_norm, l1_k_norm, l1_w_out),
        (l2_norm, l2_w_in, l2_q_norm, l2_k_norm, l2_w_out),
        (l3_norm, l3_w_in, l3_q_norm, l3_k_norm, l3_w_out),
    )
    xp, xs = x_prompt, x_sample
    new_state = []
    for i in range(DEPTH):
        norm_g, w_in, q_gain, k_gain, w_out = layer_params[i]
        hp = rms_norm(xp, norm_g)
        hs = rms_norm(xs, norm_g)
        kind = i % N_MIXERS
        if kind == 0:
            bp, bs, st = mixer_a(hp, hs, layer_states[i], w_in, q_gain, k_gain)
        elif kind == 1:
            bp, bs, st = mixer_b(hp, hs, layer_states[i], w_in, q_gain, k_gain, page_table)
        else:
            bp, bs, st = mixer_c(hp, hs, layer_states[i], w_in, q_gain, k_gain, page_table)
        xp = xp + jnp.einsum('btf,fd->btd', bp, w_out).astype(xp.dtype)
        xs = xs + jnp.einsum('btf,fd->btd', bs, w_out).astype(xs.dtype)
        new_state.extend(st)
    return (xp, xs, *new_state)
```

```python
import numpy as np
import ml_dtypes
import concourse.bass as bass
import concourse.mybir as mybir
from concourse.bass_utils import run_bass_kernel_spmd
from contextlib import ExitStack

F32 = mybir.dt.float32
BF16 = mybir.dt.bfloat16
I32 = mybir.dt.int32
ALU = mybir.AluOpType
AF = mybir.ActivationFunctionType
AX = mybir.AxisListType

PAST = 2048
NPG = 16
DIL = ((128, 1), (512, 4), (2048, 16))
NEG = -30000.0
BIGF = 1.0e30


class StopBuild(Exception):
    pass


class Prog:
    ENG = ('pe', 'act', 'dve', 'pool', 'sp')
    CH = 4096
    DMA_DEPTH = 4

    def __init__(self, nc, es, n_dma_sems=40):
        self.nc = nc
        self.es = es
        self.q = {e: [] for e in self.ENG}
        self.cnt = {e: 0 for e in self.ENG}
        self.sems = {}
        self.waited = {}
        self.buf_w = {}
        self.buf_r = {}
        self.nd = n_dma_sems
        for k in range(n_dma_sems):
            self.sems[('d', k)] = es.enter_context(nc.semaphore(f"dsem{k}"))
        self.dval = [0] * n_dma_sems
        self.dnext = 0
        self.n_ops = 0

    def _sem(self, key):
        if key not in self.sems:
            self.sems[key] = self.es.enter_context(self.nc.semaphore(f"s_{key[0]}_{key[1]}"))
        return self.sems[key]

    def _wait(self, e, tok):
        key, val, src = tok
        if src == 'pe' and e == 'pe':
            return
        if self.waited.get((e, key), 0) >= val:
            return
        self.waited[(e, key)] = val
        self._sem(key)
        self.q[e].append(('wait', key, val))

    def _deps(self, e, reads, writes):
        for b in reads:
            t = self.buf_w.get(b)
            if t is not None:
                self._wait(e, t)
        for b in writes:
            t = self.buf_w.get(b)
            if t is not None:
                self._wait(e, t)
            for key, (val, src) in self.buf_r.get(b, {}).items():
                self._wait(e, (key, val, src))

    def _mark(self, tok, reads, writes):
        key, val, src = tok
        for b in reads:
            d = self.buf_r.setdefault(b, {})
            if d.get(key, (0, None))[0] < val:
                d[key] = (val, src)
        for b in writes:
            self.buf_w[b] = tok
            self.buf_r[b] = {}

    limit = None

    def _chk(self):
        import sys as _s
        if not hasattr(self, 'log'):
            self.log = []
        f = _s._getframe(2)
        self.log.append((self.n_ops, f.f_lineno, f.f_code.co_name))
        if self.limit is not None and self.n_ops >= self.limit:
            raise StopBuild()

    def op(self, e, fn, reads=(), writes=()):
        self._chk()
        self._deps(e, reads, writes)
        idx = self.cnt[e]
        j, k = divmod(idx, self.CH)
        key = (e, j)
        self._sem(key)
        self.q[e].append(('op', fn, key))
        self.cnt[e] += 1
        self.n_ops += 1
        tok = (key, k + 1, e)
        self._mark(tok, reads, writes)
        return tok

    def dma(self, e, fn, reads=(), writes=()):
        self._chk()
        self._deps(e, reads, writes)
        k = self.dnext
        self.dnext = (k + 1) % self.nd
        key = ('d', k)
        if self.dval[k] > 0:
            self._wait(e, (key, self.dval[k], 'dma'))
        hist = self.__dict__.setdefault('dhist', {}).setdefault(e, [])
        if len(hist) >= self.DMA_DEPTH:
            self._wait(e, hist[-self.DMA_DEPTH])
        self.dval[k] += 16
        self.q[e].append(('dma', fn, key))
        self.n_ops += 1
        tok = (key, self.dval[k], 'dma')
        hist.append(tok)
        self._mark(tok, reads, writes)
        return tok

    def finish(self):
        for k in range(self.nd):
            if self.dval[k] > 0:
                self._wait('sp', (('d', k), self.dval[k], 'dma'))
        for e in ('pe', 'act', 'dve', 'pool'):
            if self.cnt[e] > 0:
                j, k = divmod(self.cnt[e] - 1, self.CH)
                self._wait('sp', ((e, j), k + 1, e))

    def _emit(self, eng, e):
        for item in self.q[e]:
            if item[0] == 'wait':
                eng.wait_ge(self.sems[item[1]], item[2])
            elif item[0] == 'op':
                item[1](eng).then_inc(self.sems[item[2]], 1)
            else:
                item[1](eng).then_inc(self.sems[item[2]], 16)

    def emit(self):
        self.finish()
        with self.nc.Block() as block:
            @block.tensor
            def _(eng):
                self._emit(eng, 'pe')

            @block.scalar
            def _(eng):
                self._emit(eng, 'act')

            @block.vector
            def _(eng):
                self._emit(eng, 'dve')

            @block.gpsimd
            def _(eng):
                self._emit(eng, 'pool')

            @block.sync
            def _(eng):
                self._emit(eng, 'sp')


class Cfg:
    def __init__(self, SEQ=8192, NS=128, NPOOL=2560, depth=4, stop=None, ag_delay=1200, pool_delay=2000):
        self.stop = stop
        self.ag_delay = ag_delay
        self.pool_delay = pool_delay
        self.SEQ, self.NS, self.NPOOL = SEQ, NS, NPOOL
        self.NT = SEQ // 512
        self.NSQ = NS // 8
        self.NTT = self.NT + 1
        self.PSH = NPOOL // 4
        self.depth = depth


def col_layout(kind):
    if kind == 0:
        return dict(q=(0, 1536), k=(1536, 384), v=(1920, 384), g=(2304, 512), IN=2816, OUT=512, HQ=24, HK=6)
    if kind == 1:
        return dict(q=(0, 1024), k=(1024, 256), v=(1280, 256), g=(1536, 1024), IN=2560, OUT=1024, HQ=16, HK=4)
    return dict(q=(0, 1024), k=(1024, 256), v=(1280, 256), g=(1536, 1024), qi=(2560, 512), ki=(3072, 64),
                wi=(3136, 8), IN=3144, OUT=1024, HQ=16, HK=4)


def a_tiles():
    lst = []
    for dj in range(4, -1, -1):
        for rp in range(4):
            for g in range(3):
                if g == 0 and not (dj == 0 or (dj == 1 and rp == 3)):
                    continue
                if g == 1 and dj > 1:
                    continue
                lst.append((g, dj, rp))
    return lst


A_TILES = a_tiles()


def host_tables(cfg, r):
    NT, NTT = cfg.NT, cfg.NTT
    t = {}
    half = 32
    inv = (10000.0 ** (-np.arange(half, dtype=np.float32) / half)).astype(np.float32)
    pos = np.zeros((128, NTT), np.float32)
    for j in range(NT):
        pos[:, j] = (4 * j + r) * 128 + np.arange(128)
    pos[:, NT] = PAST + (np.arange(128) % 8)
    ang = pos[:, :, None] * inv[None, None, :]
    t['cos'] = np.cos(ang).astype(np.float32).reshape(128, NTT * 32)
    t['sin'] = np.sin(ang).astype(np.float32).reshape(128, NTT * 32)
    t['identf'] = np.eye(128, dtype=np.float32)
    t['identb'] = np.eye(128, dtype=np.float32).astype(ml_dtypes.bfloat16)
    k = np.arange(128)[:, None]
    q = np.arange(128)[None, :]
    mA = np.zeros((128, len(A_TILES), 128), np.float32)
    for i, (g, dj, rp) in enumerate(A_TILES):
        W, d = DIL[g]
        delta = (4 * dj + r - rp) * 128 + q - k
        mA[:, i, :] = ((delta >= 0) & (delta <= W) & (delta % d == 0))
    t['maskA'] = mA.astype(ml_dtypes.bfloat16).reshape(128, -1)
    mC = np.zeros((128, 4, 128), np.float32)
    for rp in range(4):
        mC[:, rp, :] = ((rp * 128 + k) <= (r * 128 + q))
    t['maskC'] = mC.astype(ml_dtypes.bfloat16).reshape(128, -1)
    oh = np.zeros((32, NT, 4, 128), np.float32)
    for j in range(NT):
        for rp in range(4):
            n = 2 * j + rp // 2
            if n < 32:
                oh[n, j, rp, :] = 1.0
    t['onehotB'] = oh.astype(ml_dtypes.bfloat16).reshape(32, -1)
    pb = np.full((128, NT, 32), -BIGF, np.float32)
    ow = np.zeros((128, NT, 32), np.float32)
    for j in range(NT):
        own = 2 * j + r // 2
        pb[:, j, :own] = 0.0
        if own < 32:
            ow[:, j, own] = 1.0
    t['pastB'] = pb.reshape(128, -1)
    t['ownB'] = ow.reshape(128, -1)
    cb = np.zeros((128, 4, 128), np.float32)
    qq = np.arange(128)[:, None]
    kk = np.arange(128)[None, :]
    for rp in range(4):
        cb[:, rp, :] = np.where((rp * 128 + kk) <= (r * 128 + qq), 0.0, -BIGF)
    t['cbQ'] = cb.reshape(128, -1)
    tq = np.arange(8)[None, :]
    for g, (W, d) in enumerate(DIL):
        nkt = W // 128
        m = np.zeros((128, nkt + 1, 4, 8), np.float32)
        p = np.arange(128)[:, None]
        for kt in range(nkt):
            i = kt * 128 + p
            dlt = W + tq - i
            m[:, kt, :, :] = ((dlt >= 0) & (dlt <= W) & (dlt % d == 0))[:, None, :]
        dlt = tq - p
        m[:, nkt, :, :] = ((dlt >= 0) & (dlt % d == 0) & (p < 8))[:, None, :]
        t[f'maskAs{g}'] = m.astype(ml_dtypes.bfloat16).reshape(128, -1)
    mn = np.zeros((128, 4, 8), np.float32)
    p = np.arange(128)[:, None]
    mn[:, :, :] = ((p <= tq) & (p < 8))[:, None, :]
    t['maskNew'] = mn.astype(ml_dtypes.bfloat16).reshape(128, -1)
    ohs = np.zeros((32, 17 * 128), np.float32)
    for kt in range(16):
        ohs[kt // 2, kt * 128:(kt + 1) * 128] = 1.0
    ohs[8, 16 * 128:] = 1.0
    t['onehotS'] = ohs.astype(ml_dtypes.bfloat16)
    t['cbS'] = np.where(np.arange(8)[None, :] <= (np.arange(128) % 8)[:, None], 0.0, -BIGF).astype(np.float32)
    return t


TABLE_SPECS = None


def table_specs(cfg):
    t = host_tables(cfg, 0)
    return {k: (list(v.shape), BF16 if v.dtype == ml_dtypes.bfloat16 else F32) for k, v in t.items()}


class Builder:
    def __init__(self, cfg):
        self.cfg = cfg
        self.nc = bass.Bass("TRN2", target_bir_lowering=False, num_devices=8)
        self.din = {}
        self.dout = {}

    def inp(self, name, shape, dt=F32):
        self.din[name] = self.nc.dram_tensor(name, list(shape), dt, kind="ExternalInput").ap()
        return self.din[name]

    def outp(self, name, shape, dt=F32):
        self.dout[name] = self.nc.dram_tensor(name, list(shape), dt, kind="ExternalOutput").ap()
        return self.dout[name]

    def scratch(self, name, shape, dt):
        return self.nc.dram_tensor(name, list(shape), dt, kind="Internal").ap()

    def declare(self):
        cfg = self.cfg
        NT, NSQ = cfg.NT, cfg.NSQ
        self.inp("xp", [NT * 128, 1024])
        self.inp("xs", [128, 1024])
        for l in (0, 3):
            if l < cfg.depth:
                for W, _ in DIL:
                    self.inp(f"st{l}_{W}", [NSQ, W, 256])
        for nm, wd in (("c1k", 256), ("c1v", 256), ("c2k", 256), ("c2v", 256), ("c2i", 64)):
            self.inp(nm, [cfg.PSH * 128, wd])
        self.inp("pt", [1, NSQ * NPG], I32)
        for l in range(cfg.depth):
            L = col_layout(l % 3)
            self.inp(f"norm{l}", [1, 1024])
            self.inp(f"win{l}", [1024, L['IN']])
            self.inp(f"qn{l}", [1, 64])
            self.inp(f"kn{l}", [1, 64])
            self.inp(f"wout{l}", [L['OUT'], 1024])
        for k, (shape, dt) in table_specs(cfg).items():
            self.inp("t_" + k, shape, dt)
        self.outp("yp", [NT * 128, 1024])
        self.outp("ys", [128, 1024])
        for l in range(cfg.depth):
            kind = l % 3
            if kind == 0:
                self.outp(f"kvp{l}", [NT * 128, 768])
                for W, _ in DIL:
                    self.outp(f"sto{l}_{W}", [NSQ, W, 256])
            else:
                self.outp(f"kp{l}", [NT * 128, 256])
                self.outp(f"vp{l}", [NT * 128, 256])
                self.outp(f"ks{l}", [128, 256])
                self.outp(f"vs{l}", [128, 256])
                if kind == 2:
                    self.outp(f"kip{l}", [NT * 128, 64])
                    self.outp(f"kis{l}", [128, 64])
        self.Xd = self.scratch("Xd", [(NT + 1) * 128, 1024], F32)
        self.KT_in = [self.scratch(f"KT_in{c}", [128, NT * 128], BF16) for c in range(3)]
        self.KT_all = [self.scratch(f"KT_all{c}", [4 * 128, NT * 128], BF16) for c in range(3)]
        self.NVC = (NT + 3) // 4
        self.V_in = [self.scratch(f"V_in{c}", [128, 4 * 396], BF16) for c in range(self.NVC)]
        self.V_all = [self.scratch(f"V_all{c}", [4 * 128, 4 * 396], BF16) for c in range(self.NVC)]
        self.KI_in = self.scratch("KI_in", [64, NT * 128], BF16)
        self.KI_all = self.scratch("KI_all", [4 * 64, NT * 128], BF16)
        self.Wb = [self.scratch(f"Wb{l}", [128, 8 * col_layout(l % 3)['IN']], BF16) for l in range(cfg.depth)]
        self.Wob = [self.scratch(f"Wob{l}", [128, (col_layout(l % 3)['OUT'] // 128) * 1024], BF16) for l in range(cfg.depth)]
        self.pools = {}
        self.pool_in = {}
        for nm, wd in (("c1k", 256), ("c1v", 256), ("c2k", 256), ("c2v", 256), ("c2i", 64)):
            nh = 2 if wd == 256 else 1
            self.pools[nm] = [self.scratch(f"pool_{nm}{h}", [cfg.NPOOL * 128, wd // nh], F32) for h in range(nh)]
            self.pool_in[nm] = [self.scratch(f"pin_{nm}{h}", [cfg.PSH * 128, wd // nh], F32) for h in range(nh)]

    def build(self):
        cfg = self.cfg
        nc = self.nc
        self.declare()
        with ExitStack() as es:
            self.es = es
            self.P = P = Prog(nc, es)
            if isinstance(cfg.stop, str) and cfg.stop.startswith('n'):
                P.limit = int(cfg.stop[1:])
            self._uid = 0
            self.alloc_persistent()
            try:
                self.build_body()
            except StopBuild:
                pass
            P.emit()
        return nc

    def build_body(self):
        cfg, P = self.cfg, self.P
        if True:
            if cfg.depth > 1:
                for nm in ("c1k", "c1v", "c2k", "c2v", "c2i"):
                    if nm.startswith("c2") and cfg.depth < 3:
                        continue
                    nrows = cfg.PSH * 128
                    halves = self.pool_in[nm]
                    hw = 256 // 2 if len(halves) == 2 else 64
                    for h in range(len(halves)):
                        r0 = 0
                        while r0 < nrows:
                            n = min(2048, nrows - r0)
                            a = n // 128
                            P.dma('sp', lambda e, nm=nm, r0=r0, n=n, a=a, h=h, hw=hw: e.dma_start(
                                out=self.pool_in[nm][h][r0:r0 + n, :].rearrange("(p a) f -> p a f", a=a),
                                in_=self.din[nm][r0:r0 + n, h * hw:(h + 1) * hw].rearrange("(p a) f -> p a f", a=a)), writes=['pin_' + nm])
                            r0 += n
                        CP = 10 * 128
                        for k in range(cfg.PSH // 10):
                            P.op('pool', lambda e, nm=nm, k=k, h=h, CP=CP: e.collective_compute(
                                "AllGather", ALU.bypass, replica_groups=[[0, 1, 2, 3], [4, 5, 6, 7]],
                                ins=[self.pool_in[nm][h][k * CP:(k + 1) * CP, :]], outs=[self.pools[nm][h][k * 4 * CP:(k + 1) * 4 * CP, :]]),
                                reads=['pin_' + nm], writes=['pool_' + nm])
            self.load_tables()
            for l in range(cfg.depth):
                self.layer(l)
            for _ in range(cfg.pool_delay):
                P.op('pool', lambda e: e.memset(self.big2[:, 0:4096], 0), writes=['big2'])
            P.op('pool', lambda e: e.memset(self.big2[:, 0:4], 0), writes=['big2'] + ['pool_' + nm for nm in ("c1k", "c1v", "c2k", "c2v", "c2i")])
            for l in range(cfg.depth):
                self.layer_sample(l)

    def sb(self, name, shape, dt):
        return self.es.enter_context(self.nc.sbuf_tensor(name, list(shape), dt))

    def ps(self, name, shape, dt):
        return self.es.enter_context(self.nc.psum_tensor(name, list(shape), dt))

    def alloc_persistent(self):
        cfg = self.cfg
        NTT = cfg.NTT
        sb, ps = self.sb, self.ps
        self.cos = sb("cos", [128, NTT, 32], F32)
        self.sin = sb("sin", [128, NTT, 32], F32)
        self.identf = sb("identf", [128, 128], F32)
        self.identb = sb("identb", [128, 128], BF16)
        self.ltab = sb("ltab", [128, len(A_TILES) * 128], BF16)
        self.maskA = self.ltab[:, :].rearrange("p (a b) -> p a b", b=128)
        lt32 = self.ltab[:, :].bitcast(F32)
        self.pastB = lt32[:, 0:cfg.NT * 32].rearrange("p (a b) -> p a b", b=32)
        self.ownB = lt32[:, 512:512 + cfg.NT * 32].rearrange("p (a b) -> p a b", b=32)
        self.cbQ = lt32[:, 0:512]
        self.cposQ = lt32[:, 512:1024]
        self.maskC = sb("maskC", [128, 4, 128], BF16)
        self.maskAs = [sb(f"maskAs{g}", [128, DIL[g][0] // 128 + 1, 32], BF16) for g in range(3)]
        self.maskNew = sb("maskNew", [128, 32], BF16)
        self.cbS = sb("cbS", [128, 8], F32)
        self.ptb = sb("ptb", [128, cfg.NSQ * NPG], I32)
        self.pidx = sb("pidx", [128, cfg.NSQ * NPG], I32)
        self.iot = sb("iot", [128, 1], I32)
        self.consts = sb("consts", [128, 8], F32)
        self.ones_row = sb("ones_row", [1, 128], F32)
        self.zrow = sb("zrow", [128, 512], BF16)
        self.onesf = sb("onesf", [128, 128], F32)
        self.dth = sb("dth", [128, 128], F32)
        self.ws = [sb(f"ws{i}", [128, 8, 512], BF16) for i in range(2)]
        self.gain = sb("gain", [128, 1024], F32)
        self.qg = sb("qg", [128, 64], F32)
        self.kg = sb("kg", [128, 64], F32)
        self.xt = [sb(f"xt{i}", [128, 1024], F32) for i in range(2)]
        self.junkf = sb("junkf", [128, 1024], F32)
        self.stat = sb("stat", [128, 64], F32)
        self.hb = sb("hb", [128, 1024], BF16)
        self.hT = sb("hT", [128, 8, 128], BF16)
        self.zf = sb("zf", [128, 1536], F32)
        self.kvA = sb("kvA", [128, 768], F32)
        self.kif = sb("kif", [128, 64], F32)
        self.zb = sb("zb", [128, 1536], BF16)
        self.rt = [sb(f"rt{i}", [128, 8 * 32], F32) for i in range(4)]
        self.QT = sb("QT", [96, 24, 128], BF16)
        self.QTf = sb("QTf", [64, 16, 128], F32)
        self.qiT = sb("qiT", [64, 8, 128], BF16)
        self.sgn = sb("sgn", [128, 8], F32)
        self.wabs = sb("wabs", [128, 8], F32)
        self.KTst = sb("KTst", [64, 6, 128], BF16)
        self.KTn = sb("KTn", [96, 6, 128], BF16)
        self.kiTst = sb("kiTst", [64, 128], BF16)
        self.kiTn = sb("kiTn", [64, 128], BF16)
        self.Vst = sb("Vst", [128, 6, 66], BF16)
        self.Vnew = sb("Vnew", [128, 6, 66], BF16)
        self.Vn2 = [sb(f"Vn2_{i}", [8, 6, 66], BF16) for i in range(2)]
        self.g = sb("g", [128, 1024], F32)
        self.otok = sb("otok", [128, 1024], F32)
        self.og = sb("og", [128, 1024], BF16)
        self.ogT = sb("ogT", [128, 8, 128], BF16)
        self.rec = sb("rec", [128, 16], F32)
        self.KTc = [sb(f"KTc{i}", [96, 6, 512], BF16) for i in range(2)]
        self.Vc = [sb(f"Vc{i}", [128, 4, 396], BF16) for i in range(2)]
        self.kiTc = [sb(f"kiTc{i}", [64, 512], BF16) for i in range(2)]
        self.PT = [sb(f"PT{i}", [128, 512], BF16) for i in range(3)]
        self.kmT = sb("kmT", [64, 4, 32], F32)
        self.ksum = sb("ksum", [64, 4, cfg.NT], F32)
        self.bsm = sb("bsm", [128, 16, 32], F32)
        self.top8 = sb("top8", [128, 16, 8], F32)
        self.thr = sb("thr", [128, 16], F32)
        self.biasq = sb("biasq", [128, 16, 32], BF16)
        self.big = sb("big", [128, 8192], F32)
        self.big2 = sb("big2", [128, 8192], mybir.dt.uint8)
        self.maskTc = [sb(f"maskTc{i}", [128, 4, 128], BF16) for i in range(2)]
        self.maskTs = sb("maskTs", [128, 17, 128], BF16)
        bigb = self.big[:, :].bitcast(BF16)
        self.KTs = bigb[0:96, 0:4 * 2176].rearrange("p (h n) -> p h n", h=4)
        self.Vs = bigb[:, 8704:8704 + 17 * 4 * 66].rearrange("p (k h c) -> p k h c", h=4, c=66)
        self.kmstage = bigb
        self.KTsA = bigb[0:96, 0:2 * 4352].rearrange("p (h n) -> p h n", h=2)
        self.VsA = bigb[:, 8704:8704 + 21 * 2 * 66].rearrange("p (k h c) -> p k h c", h=2, c=66)
        self.thb = sb("thb", [128, 128], F32)
        self.bis = sb("bis", [128, 8], F32)
        self.throw = sb("throw", [1, 128], F32)
        self.Rf = [sb(f"Rf{i}", [128, 512], F32) for i in range(2)]
        self.stK = [sb(f"stK{i}", [128, 2, 256], F32) for i in range(2)]
        self.stV = [sb(f"stV{i}", [128, 2, 256], F32) for i in range(2)]
        self.stI = [sb(f"stI{i}", [128, 4, 64], F32) for i in range(2)]
        self.PTs = [sb(f"PTs{i}", [128, 32], BF16) for i in range(3)]
        self.osn = sb("osn", [32, 4, 64], F32)
        self.recs = sb("recs", [32, 4], F32)
        self.kmTs = sb("kmTs", [64, 4, 16], F32)
        self.qsel = sb("qsel", [64, 16, 8], F32)
        self.bss = sb("bss", [32, 4, 16], F32)
        self.top8s = sb("top8s", [32, 4, 8], F32)
        self.biass = sb("biass", [32, 4, 16], BF16)
        self.qpad = [sb(f"qpad{i}", [64, 8, 128], BF16) for i in range(2)]
        self.kiT1 = sb("kiT1", [64, 17 * 128], BF16)
        self.psS = [ps(f"psS{i}", [128, 512], F32) for i in range(2)]
        self.psA = [ps(f"psA{i}", [128, 512], F32) for i in range(3)]
        self.psM = [ps(f"psM{i}", [128, 512], F32) for i in range(2)]
        self.psT = ps("psT", [128, 1024], BF16)

    def load_tables(self):
        P = self.P
        d = self.din

        def ld(dst, src, name, eng='sp'):
            P.dma(eng, lambda e: e.dma_start(out=dst, in_=src), writes=[name])
        NTT = self.cfg.NTT
        ld(self.cos[:].rearrange("p a b -> p (a b)"), d['t_cos'], 'cos')
        ld(self.sin[:].rearrange("p a b -> p (a b)"), d['t_sin'], 'sin')
        ld(self.identf[:], d['t_identf'], 'identf')
        ld(self.identb[:], d['t_identb'], 'identb')
        ld(self.maskC[:].rearrange("p a b -> p (a b)"), d['t_maskC'], 'maskC')
        for g in range(3):
            ld(self.maskAs[g][:].rearrange("p a b -> p (a b)"), d[f't_maskAs{g}'], f'maskAs{g}')
        ld(self.maskNew[:], d['t_maskNew'], 'maskNew')
        ld(self.cbS[:], d['t_cbS'], 'cbS')
        ld(self.ptb[:], d['pt'].partition_broadcast(128), 'ptb')
        P.op('pool', lambda e: e.iota(out=self.iot[:], pattern=[[0, 1]], base=0, channel_multiplier=1), writes=['iot'])
        P.op('dve', lambda e: e.tensor_scalar(out=self.pidx[:], in0=self.ptb[:], scalar1=128, scalar2=self.iot[:, 0:1],
                                              op0=ALU.mult, op1=ALU.add), reads=['ptb', 'iot'], writes=['pidx'])
        P.op('pool', lambda e: e.memset(self.consts[:, 0:1], 1e-6), writes=['consts'])
        P.op('pool', lambda e: e.memset(self.consts[:, 1:2], 0.5), writes=['consts'])
        P.op('pool', lambda e: e.memset(self.consts[:, 2:3], 1.0), writes=['consts'])
        P.op('pool', lambda e: e.memset(self.ones_row[:], 1.0), writes=['ones_row'])
        P.op('pool', lambda e: e.memset(self.zrow[:], 0.0), writes=['zrow'])
        P.op('pool', lambda e: e.memset(self.onesf[:], 1.0), writes=['onesf'])
        P.op('pool', lambda e: e.memset(self.Vst[:], 1.0), writes=['Vst'])
        P.op('pool', lambda e: e.memset(self.KTn[:], 0.0), writes=['KTn'])

    def stop_at(self, tag):
        if self.cfg.stop == tag or self.cfg.stop == f"L{self.l}:{tag}":
            raise StopBuild()

    def layer_setup(self, l, cast_weights):
        cfg, P = self.cfg, self.P
        kind = l % 3
        L = col_layout(kind)
        d = self.din
        self.l = l
        self.L = L
        P.dma('sp', lambda e: e.dma_start(out=self.gain[:], in_=d[f'norm{l}'].partition_broadcast(128)), writes=['gain'])
        P.dma('sp', lambda e: e.dma_start(out=self.qg[:], in_=d[f'qn{l}'].partition_broadcast(128)), writes=['qg'])
        P.dma('sp', lambda e: e.dma_start(out=self.kg[:], in_=d[f'kn{l}'].partition_broadcast(128)), writes=['kg'])
        if kind == 0:
            P.dma('sp', lambda e: e.dma_start(out=self.ltab[:, :], in_=d['t_maskA']), writes=['ltab'])
        elif kind == 1:
            P.dma('sp', lambda e: e.dma_start(out=self.pastB.rearrange("p a b -> p (a b)"), in_=d['t_pastB']), writes=['ltab'])
            P.dma('sp', lambda e: e.dma_start(out=self.ownB.rearrange("p a b -> p (a b)"), in_=d['t_ownB']), writes=['ltab'])
        else:
            P.dma('sp', lambda e: e.dma_start(out=self.cbQ, in_=d['t_cbQ']), writes=['ltab'])
            P.op('dve', lambda e: e.tensor_scalar(out=self.cposQ, in0=self.cbQ, scalar1=-2.0, scalar2=-BIGF,
                                                  op0=ALU.mult, op1=ALU.add), reads=['ltab'], writes=['ltab'])
        if not cast_weights:
            return
        IN = L['IN']
        win = d[f'win{l}'].rearrange("(c p) n -> p c n", p=128)
        wb3 = self.Wb[l].rearrange("p (c n) -> p c n", c=8)
        for n0 in range(0, IN, 512):
            w = min(512, IN - n0)
            u = self._uid
            self._uid += 1
            ws, wn = self.ws[u % 2], f'ws{u % 2}'
            for c in range(8):
                P.dma('pool', lambda e, ws=ws, n0=n0, w=w, c=c: e.dma_start(out=ws[:, c, 0:w], in_=win[:, c, n0:n0 + w]), writes=[wn])
            P.dma('sp', lambda e, ws=ws, n0=n0, w=w: e.dma_start(out=wb3[:, :, n0:n0 + w], in_=ws[:, :, 0:w]), reads=[wn], writes=[f'Wb{l}'])
        OC = L['OUT'] // 128
        wo = d[f'wout{l}'].rearrange("(c p) n -> p c n", p=128)
        wob3 = self.Wob[l].rearrange("p (c n) -> p c n", c=OC)
        for hf in range(2):
            u = self._uid
            self._uid += 1
            ws, wn = self.ws[u % 2], f'ws{u % 2}'
            for c in range(OC):
                P.dma('pool', lambda e, ws=ws, hf=hf, c=c: e.dma_start(out=ws[:, c, :], in_=wo[:, c, hf * 512:(hf + 1) * 512]), writes=[wn])
            P.dma('sp', lambda e, ws=ws, hf=hf: e.dma_start(out=wob3[:, :, hf * 512:(hf + 1) * 512], in_=ws[:, 0:OC, :]), reads=[wn], writes=[f'Wob{l}'])

    def xio(self, l):
        cfg, d = self.cfg, self.din
        NT = cfg.NT
        xin = (lambda ti: (d['xp'][ti * 128:(ti + 1) * 128, :] if ti < NT else d['xs'])) if l == 0 else \
              (lambda ti: self.Xd[ti * 128:(ti + 1) * 128, :])
        last = (l == cfg.depth - 1)
        xout = (lambda ti: (self.dout['yp'][ti * 128:(ti + 1) * 128, :] if ti < NT else self.dout['ys'])) if last else \
               (lambda ti: self.Xd[ti * 128:(ti + 1) * 128, :])
        return xin, xout

    def layer(self, l):
        cfg, P = self.cfg, self.P
        kind = l % 3
        NT = cfg.NT
        self.layer_setup(l, True)
        L = self.L
        xin, xout = self.xio(l)
        for ti in range(NT):
            self.norm_tile(ti, xin(ti))
            self.phase1_tile(l, kind, L, ti)
        self.stop_at('p1')
        rg = [[0, 1, 2, 3], [4, 5, 6, 7]]
        for c in range(L['HK'] // 2):
            P.op('pool', lambda e, c=c: e.collective_compute("AllGather", ALU.bypass, replica_groups=rg,
                                                             ins=[self.KT_in[c]], outs=[self.KT_all[c]]), reads=['KT_in'], writes=['KT_all'])
        for c in range(self.NVC):
            P.op('pool', lambda e, c=c: e.collective_compute("AllGather", ALU.bypass, replica_groups=rg,
                                                             ins=[self.V_in[c]], outs=[self.V_all[c]]), reads=['V_in'], writes=['V_all'])
        if kind == 2:
            P.op('pool', lambda e: e.collective_compute("AllGather", ALU.bypass, replica_groups=rg,
                                                        ins=[self.KI_in], outs=[self.KI_all]), reads=['KI_in'], writes=['KI_all'])
        for _ in range(cfg.ag_delay):
            P.op('pool', lambda e: e.memset(self.big2[:, 0:4096], 0), writes=['big2'])
        P.op('pool', lambda e: e.memset(self.big2[:, 0:4], 0), writes=['big2', 'KT_all', 'V_all', 'KI_all'])
        self.stop_at('ag')
        if kind == 1:
            self.moba_kmean(L)
        for ti in range(NT):
            self.norm_tile(ti, xin(ti))
            self.phase2_proj(l, kind, L, ti)
            self.stop_at('p2proj')
            if kind == 0:
                self.attn_A_prompt(ti)
            elif kind == 1:
                self.attn_B_prompt(ti)
            else:
                self.attn_C_prompt(ti)
            self.stop_at('attn')
            self.finish_tile(l, L, ti, xout(ti))
            self.stop_at('fin')
        self.stop_at('prompt')

    def layer_sample(self, l):
        cfg, P = self.cfg, self.P
        kind = l % 3
        ti = cfg.NT
        self.layer_setup(l, False)
        L = self.L
        xin, xout = self.xio(l)
        self.norm_tile(ti, xin(ti))
        self.phase1_tile(l, kind, L, ti)
        self.stop_at('s1')
        self.norm_tile(ti, xin(ti))
        self.phase2_proj(l, kind, L, ti)
        self.stop_at('s2')
        if kind == 0:
            self.attn_A_sample(l)
        elif kind == 1:
            self.attn_B_sample(l)
        else:
            self.attn_C_sample(l)
        self.stop_at('s3')
        self.finish_tile(l, L, ti, xout(ti))

    def norm_tile(self, ti, xsrc):
        P = self.P
        s = ti % 2
        xt = self.xt[s]
        xn = f'xt{s}'
        st = self.stat
        P.dma('sp', lambda e: e.dma_start(out=xt[:], in_=xsrc), writes=[xn])
        P.op('act', lambda e: e.activation(out=self.junkf[:], in_=xt[:], func=AF.Square, accum_out=st[:, 0:1]),
             reads=[xn], writes=['junkf', 'stat'])
        P.op('act', lambda e: e.activation(out=st[:, 1:2], in_=st[:, 0:1], func=AF.Sqrt, scale=1.0 / 1024,
                                           bias=self.consts[:, 0:1]), reads=['stat', 'consts'], writes=['stat'])
        P.op('dve', lambda e: e.reciprocal(out=st[:, 2:3], in_=st[:, 1:2]), reads=['stat'], writes=['stat'])
        P.op('dve', lambda e: e.scalar_tensor_tensor(out=self.hb[:], in0=xt[:], scalar=st[:, 2:3], in1=self.gain[:],
                                                     op0=ALU.mult, op1=ALU.mult), reads=[xn, 'stat', 'gain'], writes=['hb'])
        for c in range(8):
            P.op('pe', lambda e, c=c: e.transpose(out=self.psT[:, c * 128:(c + 1) * 128], in_=self.hb[:, c * 128:(c + 1) * 128],
                                                  identity=self.identb[:]), reads=['hb', 'identb'], writes=['psT'])
        P.op('act', lambda e: e.copy(out=self.hT[:], in_=self.psT[:, :].rearrange("p (a b) -> p a b", b=128)), reads=['psT'], writes=['hT'])

    def proj(self, c0, ncols, consume):
        P = self.P
        l = self.l
        wb3 = self.Wb[l].rearrange("p (c n) -> p c n", c=8)
        banks = [(self.psM[0], 'psM0'), (self.psM[1], 'psM1')]
        for n0 in range(0, ncols, 512):
            w = min(512, ncols - n0)
            u = self._uid
            self._uid += 1
            pst, pn = banks[u % 2]
            ws, wn = self.ws[u % 2], f'ws{u % 2}'
            P.dma('sp', lambda e, ws=ws, n0=n0, w=w: e.dma_start(out=ws[:, :, 0:w], in_=wb3[:, :, c0 + n0:c0 + n0 + w]),
                  reads=[f'Wb{l}'], writes=[wn])
            for c in range(8):
                P.op('pe', lambda e, c=c, pst=pst, w=w, ws=ws: e.matmul(pst[:, 0:w], lhsT=self.hT[:, c, :], rhs=ws[:, c, 0:w],
                                                                       start=(c == 0), stop=(c == 7)),
                     reads=['hT', wn], writes=[pn])
            consume(n0, w, pst, pn)

    def qk_post(self, pst, pn, w, gain_t, gain_n, ti, out_ap, out_name, do_norm=True):
        P = self.P
        H = w // 64
        st = self.stat
        v3 = lambda ap, dd=64: ap.rearrange("p (h d) -> p h d", d=dd)
        rt = self.rt
        if do_norm:
            P.op('act', lambda e: e.activation(out=self.junkf[:, 0:w], in_=pst[:, 0:w], func=AF.Square), reads=[pn], writes=['junkf'])
            P.op('dve', lambda e: e.tensor_reduce(out=st[:, 8:8 + H], in_=v3(self.junkf[:, 0:w]), axis=AX.X, op=ALU.add),
                 reads=['junkf'], writes=['stat'])
            P.op('act', lambda e: e.activation(out=st[:, 16:16 + H], in_=st[:, 8:8 + H], func=AF.Sqrt, scale=1.0 / 64,
                                               bias=self.consts[:, 0:1]), reads=['stat', 'consts'], writes=['stat'])
            P.op('dve', lambda e: e.reciprocal(out=st[:, 24:24 + H], in_=st[:, 16:16 + H]), reads=['stat'], writes=['stat'])
            P.op('dve', lambda e: e.tensor_tensor(out=v3(self.junkf[:, 0:w]), in0=v3(pst[:, 0:w]),
                                                  in1=st[:, 24:24 + H].unsqueeze(2).to_broadcast([128, H, 64]), op=ALU.mult),
                 reads=[pn, 'stat'], writes=['junkf'])
            P.op('pool', lambda e: e.tensor_tensor(out=v3(self.junkf[:, 0:w]), in0=v3(self.junkf[:, 0:w]),
                                                   in1=gain_t[:, :].unsqueeze(1).to_broadcast([128, H, 64]), op=ALU.mult),
                 reads=['junkf', gain_n], writes=['junkf'])
        else:
            P.op('act', lambda e: e.copy(out=self.junkf[:, 0:w], in_=pst[:, 0:w]), reads=[pn], writes=['junkf'])
        src = v3(self.junkf[:, 0:w])
        x1, x2 = src[:, :, 0:32], src[:, :, 32:64]
        cosb = self.cos[:, ti, :].unsqueeze(1).to_broadcast([128, H, 32])
        sinb = self.sin[:, ti, :].unsqueeze(1).to_broadcast([128, H, 32])
        r = [v3(rt[i][:, 0:H * 32], 32) for i in range(4)]
        o3 = v3(out_ap)
        P.op('dve', lambda e: e.tensor_tensor(out=r[0], in0=x1, in1=cosb, op=ALU.mult), reads=['junkf', 'cos'], writes=['rt0'])
        P.op('pool', lambda e: e.tensor_tensor(out=r[1], in0=x2, in1=sinb, op=ALU.mult), reads=['junkf', 'sin'], writes=['rt1'])
        P.op('pool', lambda e: e.tensor_tensor(out=r[2], in0=x2, in1=cosb, op=ALU.mult), reads=['junkf', 'cos'], writes=['rt2'])
        P.op('dve', lambda e: e.tensor_tensor(out=r[3], in0=x1, in1=sinb, op=ALU.mult), reads=['junkf', 'sin'], writes=['rt3'])
        P.op('dve', lambda e: e.tensor_tensor(out=o3[:, :, 0:32], in0=r[0], in1=r[1], op=ALU.subtract),
             reads=['rt0', 'rt1'], writes=[out_name])
        P.op('pool', lambda e: e.tensor_tensor(out=o3[:, :, 32:64], in0=r[2], in1=r[3], op=ALU.add),
             reads=['rt2', 'rt3'], writes=[out_name])

    def phase1_tile(self, l, kind, L, ti):
        cfg, P = self.cfg, self.P
        NT = cfg.NT
        HK = L['HK']
        nk = L['k'][1]
        nv = L['v'][1]
        is_s = (ti == NT)
        rows = slice(ti * 128, (ti + 1) * 128)
        if kind == 0:
            kdst = lambda: self.kvA[:].rearrange("p (g x) -> p g x", x=256)[:, :, 0:128]
            vdst = lambda: self.kvA[:].rearrange("p (g x) -> p g x", x=256)[:, :, 128:256]
        def k_cons(n0, w, pst, pn):
            self.qk_post(pst, pn, w, self.kg, 'kg', ti, self.zf[:, 0:w], 'zf')
        self.proj(L['k'][0], nk, k_cons)
        zk = self.zf[:, 0:nk]
        if kind == 0:
            P.op('act', lambda e: e.copy(out=kdst(), in_=zk.rearrange("p (g x) -> p g x", x=128)), reads=['zf'], writes=['kvA'])
        else:
            dst = (self.dout[f'ks{l}'] if is_s else self.dout[f'kp{l}'][rows, :])
            P.dma('sp', lambda e, dst=dst: e.dma_start(out=dst, in_=zk), reads=['zf'])
        P.op('act', lambda e: e.copy(out=self.zb[:, 0:nk], in_=zk), reads=['zf'], writes=['zb'])
        ktdst = self.KTn if is_s else self.KTst
        ktn = 'KTn' if is_s else 'KTst'
        for h in range(HK):
            P.op('pe', lambda e, h=h: e.transpose(out=self.psT[0:64, h * 128:(h + 1) * 128], in_=self.zb[:, h * 64:(h + 1) * 64],
                                                  identity=self.identb[:]), reads=['zb', 'identb'], writes=['psT'])
        P.op('dve', lambda e: e.tensor_copy(out=ktdst[0:64, 0:HK, :], in_=self.psT[0:64, 0:HK * 128].rearrange("p (a b) -> p a b", b=128)),
             reads=['psT'], writes=[ktn])
        if not is_s:
            for c in range(HK // 2):
                P.dma('sp', lambda e, c=c: e.dma_start(out=self.KT_in[c][:, rows].rearrange("(h d) n -> d h n", d=64),
                                                       in_=self.KTst[:, 2 * c:2 * c + 2, :]), reads=['KTst'], writes=['KT_in'])
        def v_cons(n0, w, pst, pn):
            if kind == 0:
                P.op('act', lambda e: e.copy(out=vdst(), in_=pst[:, 0:w].rearrange("p (g x) -> p g x", x=128)), reads=[pn], writes=['kvA'])
            else:
                P.op('act', lambda e: e.copy(out=self.kvA[:, 0:w], in_=pst[:, 0:w]), reads=[pn], writes=['kvA'])
            if kind == 0:
                P.op('dve', lambda e: e.tensor_copy(out=self.Vst[:, 0:6, 0:64].rearrange("p (g a) d -> p g a d", a=2),
                                                    in_=vdst().rearrange("p g (a d) -> p g a d", d=64)), reads=['kvA'], writes=['Vst'])
            else:
                P.op('dve', lambda e: e.tensor_copy(out=self.Vst[:, 0:HK, 0:64], in_=self.kvA[:, 0:w].rearrange("p (h d) -> p h d", d=64)),
                     reads=['kvA'], writes=['Vst'])
        self.proj(L['v'][0], nv, v_cons)
        if kind == 0:
            if is_s:
                for g, (W, _) in enumerate(DIL):
                    o = self.dout[f'sto{l}_{W}']
                    for s_ in range(cfg.NSQ):
                        P.dma('sp', lambda e, o=o, W=W, g=g, s_=s_: e.dma_start(out=o[s_, W - 8:W, :], in_=self.kvA[8 * s_:8 * s_ + 8, g * 256:(g + 1) * 256]),
                              reads=['kvA'])
            else:
                P.dma('sp', lambda e: e.dma_start(out=self.dout[f'kvp{l}'][rows, :], in_=self.kvA[:]), reads=['kvA'])
        else:
            dst = (self.dout[f'vs{l}'] if is_s else self.dout[f'vp{l}'][rows, :])
            P.dma('sp', lambda e, dst=dst: e.dma_start(out=dst, in_=self.kvA[:, 0:nv]), reads=['kvA'], writes=[f'vout{l}'] if is_s else [])
        if not is_s:
            P.dma('sp', lambda e: e.dma_start(out=self.V_in[ti // 4][:, (ti % 4) * 396:(ti % 4) * 396 + HK * 66],
                                              in_=self.Vst[:, 0:HK, :].rearrange("p a b -> p (a b)")), reads=['Vst'], writes=['V_in'])
        else:
            P.op('pool', lambda e: e.tensor_copy(out=self.Vnew[:, 0:HK, :], in_=self.Vst[:, 0:HK, :]), reads=['Vst'], writes=['Vnew'])
        if kind == 2:
            def ki_cons(n0, w, pst, pn):
                self.qk_post(pst, pn, w, None, None, ti, self.kif[:, 0:64], 'kif', do_norm=False)
            self.proj(L['ki'][0], 64, ki_cons)
            dst = (self.dout[f'kis{l}'] if is_s else self.dout[f'kip{l}'][rows, :])
            P.dma('sp', lambda e, dst=dst: e.dma_start(out=dst, in_=self.kif[:]), reads=['kif'])
            P.op('act', lambda e: e.copy(out=self.zb[:, 0:64], in_=self.kif[:]), reads=['kif'], writes=['zb'])
            P.op('pe', lambda e: e.transpose(out=self.psT[0:64, 0:128], in_=self.zb[:, 0:64], identity=self.identb[:]),
                 reads=['zb', 'identb'], writes=['psT'])
            kd = self.kiTn if is_s else self.kiTst
            kdn = 'kiTn' if is_s else 'kiTst'
            P.op('dve', lambda e: e.tensor_copy(out=kd[:], in_=self.psT[0:64, 0:128]), reads=['psT'], writes=[kdn])
            if not is_s:
                P.dma('sp', lambda e: e.dma_start(out=self.KI_in[:, rows], in_=self.kiTst[:]), reads=['kiTst'], writes=['KI_in'])

    def phase2_proj(self, l, kind, L, ti):
        P = self.P
        nq = L['q'][1]
        ng = L['g'][1]
        HQ = L['HQ']

        def q_cons(n0, w, pst, pn):
            self.qk_post(pst, pn, w, self.qg, 'qg', ti, self.zf[:, n0:n0 + w], 'zf')
        self.proj(L['q'][0], nq, q_cons)
        P.op('act', lambda e: e.copy(out=self.zb[:, 0:nq], in_=self.zf[:, 0:nq]), reads=['zf'], writes=['zb'])
        for h0 in range(0, HQ, 8):
            for h in range(h0, min(HQ, h0 + 8)):
                P.op('pe', lambda e, h=h, h0=h0: e.transpose(out=self.psT[0:64, (h - h0) * 128:(h - h0 + 1) * 128],
                                                             in_=self.zb[:, h * 64:(h + 1) * 64], identity=self.identb[:]),
                     reads=['zb', 'identb'], writes=['psT'])
            n = min(HQ, h0 + 8) - h0
            P.op('dve', lambda e, h0=h0, n=n: e.tensor_copy(out=self.QT[0:64, h0:h0 + n, :],
                                                            in_=self.psT[0:64, 0:n * 128].rearrange("p (a b) -> p a b", b=128)), reads=['psT'], writes=['QT'])
        if kind == 1:
            for h0 in range(0, 16, 4):
                pst, pn = (self.psM[0], 'psM0') if (h0 // 4) % 2 == 0 else (self.psM[1], 'psM1')
                for h in range(h0, h0 + 4):
                    P.op('pe', lambda e, h=h, h0=h0, pst=pst: e.transpose(out=pst[0:64, (h - h0) * 128:(h - h0 + 1) * 128],
                                                                          in_=self.zf[:, h * 64:(h + 1) * 64], identity=self.identf[:]),
                         reads=['zf', 'identf'], writes=[pn])
                P.op('act', lambda e, h0=h0, pst=pst: e.copy(out=self.QTf[:, h0:h0 + 4, :], in_=pst[0:64, 0:512].rearrange("p (a b) -> p a b", b=128)),
                     reads=[pn], writes=['QTf'])

        def g_cons(n0, w, pst, pn):
            P.op('act', lambda e: e.activation(out=self.g[:, n0:n0 + w], in_=pst[:, 0:w], func=AF.Silu), reads=[pn], writes=['g'])
        self.proj(L['g'][0], ng, g_cons)
        if kind == 2:
            def wi_cons(n0, w, pst, pn):
                P.op('act', lambda e: e.activation(out=self.sgn[:], in_=pst[:, 0:8], func=AF.Sign), reads=[pn], writes=['sgn'])
                P.op('dve', lambda e: e.scalar_tensor_tensor(out=self.wabs[:], in0=pst[:, 0:8], scalar=0.125 * 8 ** -0.5, in1=self.sgn[:],
                                                             op0=ALU.mult, op1=ALU.mult), reads=[pn, 'sgn'], writes=['wabs'])
            self.proj(L['wi'][0], 8, wi_cons)

            def qi_cons(n0, w, pst, pn):
                self.qk_post(pst, pn, w, None, None, ti, self.zf[:, 0:512], 'zf', do_norm=False)
            self.proj(L['qi'][0], 512, qi_cons)
            P.op('dve', lambda e: e.tensor_tensor(out=self.zb[:, 0:512].rearrange("p (h d) -> p h d", d=64),
                                                  in0=self.zf[:, 0:512].rearrange("p (h d) -> p h d", d=64),
                                                  in1=self.wabs[:, :].unsqueeze(2).to_broadcast([128, 8, 64]), op=ALU.mult),
                 reads=['zf', 'wabs'], writes=['zb'])
            for h in range(8):
                P.op('pe', lambda e, h=h: e.transpose(out=self.psT[0:64, h * 128:(h + 1) * 128], in_=self.zb[:, h * 64:(h + 1) * 64],
                                                      identity=self.identb[:]), reads=['zb', 'identb'], writes=['psT'])
            P.op('dve', lambda e: e.tensor_copy(out=self.qiT[:], in_=self.psT[0:64, 0:1024].rearrange("p (a b) -> p a b", b=128)),
                 reads=['psT'], writes=['qiT'])

    def finish_tile(self, l, L, ti, xdst):
        P = self.P
        OUT = L['OUT']
        OC = OUT // 128
        s = ti % 2
        xt, xn = self.xt[s], f'xt{s}'
        P.op('dve', lambda e: e.tensor_tensor(out=self.og[:, 0:OUT], in0=self.otok[:, 0:OUT], in1=self.g[:, 0:OUT], op=ALU.mult),
             reads=['otok', 'g'], writes=['og'])
        for c in range(OC):
            P.op('pe', lambda e, c=c: e.transpose(out=self.psT[:, c * 128:(c + 1) * 128], in_=self.og[:, c * 128:(c + 1) * 128],
                                                  identity=self.identb[:]), reads=['og', 'identb'], writes=['psT'])
        P.op('act', lambda e: e.copy(out=self.ogT[:, 0:OC, :], in_=self.psT[:, 0:OC * 128].rearrange("p (a b) -> p a b", b=128)),
             reads=['psT'], writes=['ogT'])
        wob3 = self.Wob[l].rearrange("p (c n) -> p c n", c=OC)
        for hf in range(2):
            u = self._uid
            self._uid += 1
            pst, pn = self.psM[u % 2], f'psM{u % 2}'
            ws, wn = self.ws[u % 2], f'ws{u % 2}'
            P.dma('sp', lambda e, ws=ws, hf=hf: e.dma_start(out=ws[:, 0:OC, :], in_=wob3[:, :, hf * 512:(hf + 1) * 512]),
                  reads=[f'Wob{l}'], writes=[wn])
            for c in range(OC):
                P.op('pe', lambda e, c=c, pst=pst, ws=ws: e.matmul(pst[:, :], lhsT=self.ogT[:, c, :], rhs=ws[:, c, :],
                                                                 start=(c == 0), stop=(c == OC - 1)), reads=['ogT', wn], writes=[pn])
            P.op('dve', lambda e, hf=hf, pst=pst: e.tensor_tensor(out=xt[:, hf * 512:(hf + 1) * 512], in0=pst[:, :],
                                                                in1=xt[:, hf * 512:(hf + 1) * 512], op=ALU.add), reads=[pn, xn], writes=[xn])
        P.dma('sp', lambda e: e.dma_start(out=xdst, in_=xt[:]), reads=[xn], writes=['Xd'])

    BIGN = ['bigA', 'bigB']

    def st_block(self, kt_ap, kt_names, q_ap, q_names, nk, ncol, mask_ap=None, mask_names=(), pts=None):
        P = self.P
        u = self._uid
        self._uid += 1
        pss, psn = self.psS[u % 2], f'psS{u % 2}'
        if pts is None:
            pt, ptn = self.PT[u % 3], f'PT{u % 3}'
        else:
            pt, ptn = pts[u % 3], f'PTs{u % 3}'
        P.op('pe', lambda e: e.matmul(pss[0:nk, 0:ncol], lhsT=kt_ap, rhs=q_ap, start=True, stop=True),
             reads=list(kt_names) + list(q_names), writes=[psn])
        P.op('act', lambda e: e.activation(out=pt[0:nk, 0:ncol], in_=pss[0:nk, 0:ncol], func=AF.Exp, scale=0.125),
             reads=[psn], writes=[ptn])
        if mask_ap is not None:
            eng = 'dve' if (u % 2 == 0) else 'pool'
            P.op(eng, lambda e: e.tensor_tensor(out=pt[0:nk, 0:ncol], in0=pt[0:nk, 0:ncol], in1=mask_ap, op=ALU.mult),
                 reads=[ptn] + list(mask_names), writes=[ptn])
        return pt, ptn

    def normalize(self, nslots):
        P = self.P
        for b in range((nslots + 6) // 7):
            n = min(7, nslots - 7 * b)
            acc = self.psA[b][:, 0:n * 65].rearrange("p (s c) -> p s c", c=65)
            pn = f'psA{b}'
            P.op('dve', lambda e, acc=acc, n=n, b=b: e.reciprocal(out=self.rec[:, 7 * b:7 * b + n], in_=acc[:, :, 64:65].rearrange("p s c -> p (s c)")),
                 reads=[pn], writes=['rec'])
            P.op('dve', lambda e, acc=acc, n=n, b=b: e.tensor_tensor(
                out=self.otok[:, 7 * b * 64:(7 * b + n) * 64].rearrange("p (s d) -> p s d", d=64), in0=acc[:, :, 0:64],
                in1=self.rec[:, 7 * b:7 * b + n].unsqueeze(2).to_broadcast([128, n, 64]), op=ALU.mult),
                reads=[pn, 'rec'], writes=['otok'])

    def zero_acc(self, nbanks, M=128):
        P = self.P
        for b in range(nbanks):
            P.op('pe', lambda e, b=b: e.matmul(self.psA[b][0:M, :], lhsT=self.zrow[:, 0:M], rhs=self.zrow[:, 0:512],
                                               start=True, stop=True), reads=['zrow'], writes=[f'psA{b}'])

    def acc_ap(self, slot):
        b, i = divmod(slot, 7)
        return self.psA[b][:, i * 65:(i + 1) * 65], f'psA{b}'

    def load_chunk(self, jp, HK, onehot=False):
        P = self.P
        u = self._uid
        self._uid += 1
        s = u % 2
        ktc, vc = self.KTc[s], self.Vc[s]
        cols = slice(jp * 128, (jp + 1) * 128)
        for h in range(HK):
            src = self.KT_all[h // 2].rearrange("(r x) n -> x r n", r=4)[(h % 2) * 64:(h % 2 + 1) * 64, :, cols]
            P.dma('sp', lambda e, h=h, src=src: e.dma_start(out=ktc[0:64, h, :].rearrange("p (r n) -> p r n", r=4), in_=src),
                  reads=['KT_all'], writes=[f'KTc{s}'])
        if onehot:
            for h in range(HK):
                P.dma('sp', lambda e, h=h: e.dma_start(out=ktc[64:96, h, :], in_=self.din['t_onehotB'][:, jp * 512:(jp + 1) * 512]),
                      writes=[f'KTc{s}'])
        src = self.V_all[jp // 4].rearrange("(r p) n -> p r n", r=4)[:, :, (jp % 4) * 396:(jp % 4 + 1) * 396]
        P.dma('sp', lambda e: e.dma_start(out=vc[:], in_=src), reads=['V_all'], writes=[f'Vc{s}'])
        return s

    def attn_A_prompt(self, j):
        P = self.P
        blocks = [(i, g, dj, rp) for i, (g, dj, rp) in enumerate(A_TILES) if j - dj >= 0]
        nb = len(blocks)
        cur = None
        self.zero_acc(2)
        for bi, (mi, g, dj, rp) in enumerate(blocks):
            if cur is None or cur[0] != dj:
                cur = (dj, self.load_chunk(j - dj, 6))
            s = cur[1]
            for G in range(2):
                hk = g * 2 + G
                kt_ap = self.KTc[s][0:64, hk, rp * 128:(rp + 1) * 128]
                q_ap = self.QT[0:64, g * 8 + G * 4:g * 8 + G * 4 + 4, :]
                mask_ap = self.maskA[:, mi, :].unsqueeze(1).to_broadcast([128, 4, 128])
                pt, ptn = self.st_block(kt_ap, [f'KTc{s}'], q_ap, ['QT'], 128, 512, mask_ap=mask_ap, mask_names=['ltab'])
                for R in range(4):
                    acc, an = self.acc_ap(G * 4 + R)
                    v_ap = self.Vc[s][:, rp, hk * 66:hk * 66 + 65]
                    P.op('pe', lambda e, acc=acc, pt=pt, R=R, v_ap=v_ap, bi=bi: e.matmul(
                        acc, lhsT=pt[:, R * 128:(R + 1) * 128], rhs=v_ap, start=False, stop=(bi == nb - 1)),
                        reads=[ptn, f'Vc{s}'], writes=[an])
        self.normalize(8)

    def attn_BC_blocks(self, j, Kc, mask_fn):
        P = self.P
        nblk = (j + 1) * 4
        self.zero_acc(3)
        for jp in range(j + 1):
            s = self.load_chunk(jp, 4, onehot=(Kc == 96))
            masks = mask_fn(jp)
            for rp in range(4):
                bi = jp * 4 + rp
                for hk in range(4):
                    kt_ap = self.KTc[s][0:Kc, hk, rp * 128:(rp + 1) * 128]
                    q_ap = self.QT[0:Kc, hk * 4:hk * 4 + 4, :]
                    m = masks[rp] if masks else None
                    pt, ptn = self.st_block(kt_ap, [f'KTc{s}'], q_ap, ['QT'], 128, 512,
                                            mask_ap=(m[0] if m else None), mask_names=(m[1] if m else ()))
                    for R in range(4):
                        acc, an = self.acc_ap(hk * 4 + R)
                        v_ap = self.Vc[s][:, rp, hk * 66:hk * 66 + 65]
                        P.op('pe', lambda e, acc=acc, pt=pt, R=R, v_ap=v_ap, bi=bi: e.matmul(
                            acc, lhsT=pt[:, R * 128:(R + 1) * 128], rhs=v_ap, start=False, stop=(bi == nblk - 1)),
                            reads=[ptn, f'Vc{s}'], writes=[an])
        self.normalize(16)

    def moba_kmean(self, L):
        P, cfg = self.P, self.cfg
        NT = cfg.NT
        NBK = 2 * NT
        for h in range(4):
            src = self.KT_all[h // 2].rearrange("(r x) n -> x r n", r=4)[(h % 2) * 64:(h % 2 + 1) * 64, :, :]
            stg = self.kmstage[0:64, 0:4 * NT * 128]
            P.dma('sp', lambda e, src=src, stg=stg: e.dma_start(out=stg.rearrange("p (r n) -> p r n", r=4), in_=src),
                  reads=['KT_all'], writes=self.BIGN)
            P.op('dve', lambda e, stg=stg: e.tensor_reduce(out=self.ksum[:, :, :].rearrange("p r j -> p (r j)"),
                                                           in_=stg.rearrange("p (t k) -> p t k", k=128), axis=AX.X, op=ALU.add),
                 reads=self.BIGN, writes=['ksum'])
            for hf in range(2):
                P.op('dve', lambda e, h=h, hf=hf: e.tensor_tensor(
                    out=self.kmT[:, h, 0:NBK].rearrange("p (j f) -> p j f", f=2)[:, :, hf:hf + 1].rearrange("p j f -> p (j f)"),
                    in0=self.ksum[:, 2 * hf, :], in1=self.ksum[:, 2 * hf + 1, :], op=ALU.add),
                    reads=['ksum'], writes=['kmT'])
        P.op('dve', lambda e: e.tensor_scalar(out=self.kmT[:, :, 0:NBK], in0=self.kmT[:, :, 0:NBK], scalar1=1.0 / 256, scalar2=None, op0=ALU.mult),
             reads=['kmT'], writes=['kmT'])

    def attn_B_prompt(self, j):
        P, cfg = self.P, self.cfg
        NBK = 2 * cfg.NT
        pst, pn = self.psM[0], 'psM0'
        if NBK < 8:
            P.op('dve', lambda e: e.memset(self.bsm[:], -BIGF), writes=['bsm'])
        for h in range(16):
            P.op('pe', lambda e, h=h: e.matmul(pst[:, h * 32:h * 32 + NBK], lhsT=self.QTf[:, h, :], rhs=self.kmT[:, h // 4, 0:NBK],
                                               start=True, stop=True), reads=['QTf', 'kmT'], writes=[pn])
        P.op('dve', lambda e: e.tensor_tensor(out=self.bsm[:, :, 0:NBK], in0=pst[:, :].rearrange("p (h n) -> p h n", n=32)[:, :, 0:NBK],
                                              in1=self.pastB[:, j, 0:NBK].unsqueeze(1).to_broadcast([128, 16, NBK]), op=ALU.add),
             reads=[pn, 'ltab'], writes=['bsm'])
        for h in range(16):
            P.op('dve', lambda e, h=h: e.max(out=self.top8[:, h, :], in_=self.bsm[:, h, 0:max(NBK, 8)]), reads=['bsm'], writes=['top8'])
        P.op('dve', lambda e: e.tensor_scalar(out=self.thr[:], in0=self.top8[:, :, 2:3].rearrange("p h o -> p (h o)"), scalar1=-1e29, scalar2=None,
                                              op0=ALU.max), reads=['top8'], writes=['thr'])
        P.op('dve', lambda e: e.tensor_tensor(out=self.bsm[:, :, 0:NBK], in0=self.bsm[:, :, 0:NBK],
                                              in1=self.thr[:, :].unsqueeze(2).to_broadcast([128, 16, NBK]), op=ALU.is_ge),
             reads=['bsm', 'thr'], writes=['bsm'])
        P.op('dve', lambda e: e.tensor_tensor(out=self.bsm[:, :, 0:NBK], in0=self.bsm[:, :, 0:NBK],
                                              in1=self.ownB[:, j, 0:NBK].unsqueeze(1).to_broadcast([128, 16, NBK]), op=ALU.add),
             reads=['bsm', 'ltab'], writes=['bsm'])
        P.op('pool', lambda e: e.memset(self.biasq[:], NEG), writes=['biasq'])
        P.op('dve', lambda e: e.tensor_scalar(out=self.biasq[:, :, 0:NBK], in0=self.bsm[:, :, 0:NBK], scalar1=-1.0, scalar2=-NEG,
                                              op0=ALU.add, op1=ALU.mult), reads=['bsm', 'biasq'], writes=['biasq'])
        for h0 in (0, 8):
            for h in range(h0, h0 + 8):
                P.op('pe', lambda e, h=h, h0=h0: e.transpose(out=self.psT[64:96, (h - h0) * 128:(h - h0 + 1) * 128], in_=self.biasq[:, h, :],
                                                             identity=self.identb[:]), reads=['biasq', 'identb'], writes=['psT'])
            P.op('act', lambda e, h0=h0: e.copy(out=self.QT[64:96, h0:h0 + 8, :], in_=self.psT[64:96, 0:1024].rearrange("p (a b) -> p a b", b=128)),
                 reads=['psT'], writes=['QT'])

        def mask_fn(jp):
            if jp == j:
                return [(self.maskC[:, rp, :].unsqueeze(1).to_broadcast([128, 4, 128]), ['maskC']) for rp in range(4)]
            return None
        self.attn_BC_blocks(j, 96, mask_fn)

    def dsa_select(self, L_keys, n_fullvis, names):
        P = self.P
        bis = self.bis
        sc = self.big[:, 0:L_keys]
        P.op('dve', lambda e: e.tensor_reduce(out=bis[:, 1:2], in_=sc, axis=AX.X, op=ALU.max), reads=names, writes=['bis'])
        nd = L_keys - n_fullvis
        P.op('dve', lambda e: e.tensor_tensor(out=self.Rf[0][:, 0:nd], in0=self.big[:, n_fullvis:L_keys], in1=self.cposQ[:, 0:nd], op=ALU.max),
             reads=names + ['ltab'], writes=['Rf0'])
        P.op('dve', lambda e: e.tensor_reduce(out=bis[:, 0:1], in_=self.Rf[0][:, 0:nd], axis=AX.X, op=ALU.min), reads=['Rf0'], writes=['bis'])
        if n_fullvis > 0:
            P.op('dve', lambda e: e.tensor_reduce(out=bis[:, 7:8], in_=self.big[:, 0:n_fullvis], axis=AX.X, op=ALU.min),
                 reads=names, writes=['bis'])
            P.op('dve', lambda e: e.tensor_tensor(out=bis[:, 0:1], in0=bis[:, 0:1], in1=bis[:, 7:8], op=ALU.min), reads=['bis'], writes=['bis'])
        for it in range(24):
            eng = 'dve'
            P.op(eng, lambda e: e.scalar_tensor_tensor(out=bis[:, 2:3], in0=bis[:, 0:1], scalar=bis[:, 1:2], in1=self.consts[:, 1:2],
                                                       op0=ALU.add, op1=ALU.mult), reads=['bis', 'consts'], writes=['bis'])
            P.op(eng, lambda e: e.tensor_scalar(out=self.big2[:, 0:L_keys], in0=sc, scalar1=bis[:, 2:3], scalar2=0.0, op0=ALU.is_ge, op1=ALU.add,
                                                accum_out=bis[:, 3:4]), reads=names + ['bis'], writes=['big2', 'bis'])
            P.op(eng, lambda e: e.tensor_scalar(out=bis[:, 4:5], in0=bis[:, 3:4], scalar1=255.5, scalar2=None, op0=ALU.is_ge),
                 reads=['bis'], writes=['bis'])
            P.op(eng, lambda e: e.tensor_tensor(out=bis[:, 5:6], in0=bis[:, 2:3], in1=bis[:, 0:1], op=ALU.subtract), reads=['bis'], writes=['bis'])
            P.op(eng, lambda e: e.tensor_tensor(out=bis[:, 6:7], in0=bis[:, 1:2], in1=bis[:, 2:3], op=ALU.subtract), reads=['bis'], writes=['bis'])
            P.op(eng, lambda e: e.scalar_tensor_tensor(out=bis[:, 0:1], in0=bis[:, 5:6], scalar=bis[:, 4:5], in1=bis[:, 0:1],
                                                       op0=ALU.mult, op1=ALU.add), reads=['bis'], writes=['bis'])
            P.op(eng, lambda e: e.scalar_tensor_tensor(out=bis[:, 1:2], in0=bis[:, 6:7], scalar=bis[:, 4:5], in1=bis[:, 2:3],
                                                       op0=ALU.mult, op1=ALU.add), reads=['bis'], writes=['bis'])
        pst, pn = self.psM[1], 'psM1'
        P.op('dve', lambda e: e.tensor_scalar(out=self.dth[:], in0=self.identf[:], scalar1=bis[:, 0:1], scalar2=None, op0=ALU.mult),
             reads=['bis', 'identf'], writes=['dth'])
        P.op('pe', lambda e: e.matmul(pst[:, 128:256], lhsT=self.onesf[:], rhs=self.dth[:], start=True, stop=True),
             reads=['onesf', 'dth'], writes=[pn])
        P.op('act', lambda e: e.copy(out=self.thb[:], in_=pst[:, 128:256]), reads=[pn], writes=['thb'])

    def dsa_maskT(self, k0, n, dst, dst_name, names):
        P = self.P
        u = self._uid
        self._uid += 1
        pst, pn = self.psM[u % 2], f'psM{u % 2}'
        for kt in range(k0, k0 + n):
            P.op('pe', lambda e, kt=kt: e.transpose(out=pst[:, (kt - k0) * 128:(kt - k0 + 1) * 128],
                                                    in_=self.big[:, kt * 128:(kt + 1) * 128], identity=self.identf[:]),
                 reads=names + ['identf'], writes=[pn])
        P.op('dve', lambda e: e.tensor_tensor(out=dst[:, 0:n, :], in0=pst[:, 0:n * 128].rearrange("p (a b) -> p a b", b=128),
                                              in1=self.thb[:, :].unsqueeze(1).to_broadcast([128, n, 128]), op=ALU.is_ge),
             reads=[pn, 'thb'], writes=[dst_name])

    def attn_C_prompt(self, j):
        P = self.P
        Lk = (j + 1) * 512
        BN = self.BIGN
        for jp in range(j + 1):
            u = self._uid
            self._uid += 1
            s = u % 2
            src = self.KI_all.rearrange("(r x) n -> x r n", r=4)[:, :, jp * 128:(jp + 1) * 128]
            P.dma('sp', lambda e, s=s, src=src: e.dma_start(out=self.kiTc[s][:].rearrange("p (r n) -> p r n", r=4), in_=src),
                  reads=['KI_all'], writes=[f'kiTc{s}'])
            sc = self.big[:, jp * 512:(jp + 1) * 512]
            for h in range(8):
                b = self._uid % 2
                self._uid += 1
                pss, psn = self.psS[b], f'psS{b}'
                rf, rfn = self.Rf[b], f'Rf{b}'
                P.op('pe', lambda e, h=h, pss=pss, s=s: e.matmul(pss[:, :], lhsT=self.qiT[:, h, :], rhs=self.kiTc[s][:, :], start=True, stop=True),
                     reads=['qiT', f'kiTc{s}'], writes=[psn])
                P.op('act', lambda e, pss=pss, rf=rf: e.activation(out=rf[:], in_=pss[:, :], func=AF.Relu), reads=[psn], writes=[rfn])
                if h == 0:
                    P.op('dve', lambda e, rf=rf, sc=sc: e.tensor_scalar(out=sc, in0=rf[:], scalar1=self.sgn[:, 0:1], scalar2=None, op0=ALU.mult),
                         reads=[rfn, 'sgn'], writes=BN)
                else:
                    P.op('dve', lambda e, rf=rf, sc=sc, h=h: e.scalar_tensor_tensor(out=sc, in0=rf[:], scalar=self.sgn[:, h:h + 1], in1=sc,
                                                                                    op0=ALU.mult, op1=ALU.add), reads=[rfn, 'sgn'] + BN, writes=BN)
            if jp == j:
                P.op('dve', lambda e, sc=sc: e.tensor_tensor(out=sc, in0=sc, in1=self.cbQ, op=ALU.add), reads=BN + ['ltab'], writes=BN)
        self.dsa_select(Lk, Lk - 512, BN)

        def mask_fn(jp):
            u = self._uid
            self._uid += 1
            mt, mtn = self.maskTc[u % 2], f'maskTc{u % 2}'
            self.dsa_maskT(jp * 4, 4, mt, mtn, BN)
            return [(mt[:, rp, :].unsqueeze(1).to_broadcast([128, 4, 128]), [mtn]) for rp in range(4)]
        self.attn_BC_blocks(j, 64, mask_fn)

    def sample_pv(self, s, slot, kt_ap, kt_names, Kc, hq0, nk, v_ap, v_names, mask_ap, mask_names, first, last):
        P = self.P
        q_ap = self.QT[0:Kc, hq0:hq0 + 4, 8 * s:8 * s + 8]
        pt, ptn = self.st_block(kt_ap, kt_names, q_ap, ['QT'], nk, 32, mask_ap=mask_ap, mask_names=mask_names, pts=self.PTs)
        acc = self.psA[0][0:32, slot * 65:(slot + 1) * 65]
        if first and slot == 0:
            self.zero_acc(1, M=32)
        P.op('pe', lambda e: e.matmul(acc, lhsT=pt[0:nk, 0:32], rhs=v_ap, start=False, stop=last), reads=[ptn] + list(v_names), writes=['psA0'])

    def sample_finish_seq(self, s, nslots):
        P = self.P
        acc = self.psA[0][0:32, 0:nslots * 65].rearrange("p (s c) -> p s c", c=65)
        P.op('dve', lambda e: e.reciprocal(out=self.recs[:, 0:nslots], in_=acc[:, :, 64:65].rearrange("p s c -> p (s c)")),
             reads=['psA0'], writes=['recs'])
        P.op('dve', lambda e: e.tensor_tensor(out=self.osn[:, 0:nslots, :], in0=acc[:, :, 0:64],
                                              in1=self.recs[:, 0:nslots].unsqueeze(2).to_broadcast([32, nslots, 64]), op=ALU.mult),
             reads=['psA0', 'recs'], writes=['osn'])
        for R in range(4):
            dst = self.otok[8 * s:8 * s + 8, 0:nslots * 256].rearrange("p (s x) -> p s x", x=256)[:, :, R * 64:(R + 1) * 64]
            P.dma('sp', lambda e, dst=dst, R=R: e.dma_start(out=dst, in_=self.osn[R * 8:(R + 1) * 8, 0:nslots, :]), reads=['osn'], writes=['otok'])

    def new_v(self, s, HK):
        P = self.P
        u = self._uid
        self._uid += 1
        vn, vnn = self.Vn2[u % 2], f'Vn2_{u % 2}'
        P.dma('sp', lambda e: e.dma_start(out=vn[0:8, 0:HK, :], in_=self.Vnew[8 * s:8 * s + 8, 0:HK, :]), reads=['Vnew'], writes=[vnn])
        return vn, vnn

    def stage_piece(self, load_fn, nh, kt_dst0, G_of, v_dst):
        P = self.P
        u = self._uid
        self._uid += 1
        sk, skn = self.stK[u % 2], f'stK{u % 2}'
        sv, svn = self.stV[u % 2], f'stV{u % 2}'
        load_fn(sk, skn, sv, svn)
        for h in range(nh):
            pst, pn = self.psM[(u + h) % 2], f'psM{(u + h) % 2}'
            for i in range(2):
                P.op('pe', lambda e, i=i, h=h, pst=pst: e.transpose(out=pst[0:64, i * 128:(i + 1) * 128], in_=sk[:, i, h * 64:(h + 1) * 64],
                                                                    identity=self.identf[:]), reads=[skn, 'identf'], writes=[pn])
            P.op('act', lambda e, h=h, pst=pst: e.copy(out=self.KTs[0:64, G_of(h), kt_dst0 * 128:(kt_dst0 + 2) * 128], in_=pst[0:64, 0:256]),
                 reads=[pn], writes=['bigA'])
        v_dst(sv, svn)

    def attn_A_sample(self, l):
        P, cfg = self.P, self.cfg
        P.op('pool', lambda e: e.memset(self.VsA[:, :, :, 64:65], 1.0), writes=['bigB'])
        for s in range(cfg.NSQ):
            blocks = []
            kbase = 0
            for g, (W, d) in enumerate(DIL):
                nkt = W // 128
                if W > 8:
                    P.dma('sp', lambda e, W=W, s=s: e.dma_start(out=self.dout[f'sto{l}_{W}'][s, 0:W - 8, :], in_=self.din[f'st{l}_{W}'][s, 8:W, :]))
                if nkt == 1:
                    pieces = [(0, 1)]
                else:
                    pieces = [(k, 2) for k in range(0, nkt, 2)]
                for (k0, n) in pieces:
                    src = self.din[f'st{l}_{W}'][s, k0 * 128:(k0 + n) * 128, :].rearrange("(k p) f -> p k f", p=128)

                    def load_fn(sk, skn, sv, svn, src=src, n=n):
                        P.dma('sp', lambda e: e.dma_start(out=sk[:, 0:n, :], in_=src), writes=[skn])

                    def v_dst(sv, svn, k0=k0, n=n, kbase=kbase):
                        pass
                    u = self._uid
                    self._uid += 1
                    sk, skn = self.stK[u % 2], f'stK{u % 2}'
                    load_fn(sk, skn, None, None)
                    for G in range(2):
                        pst, pn = self.psM[(u + G) % 2], f'psM{(u + G) % 2}'
                        for i in range(n):
                            P.op('pe', lambda e, i=i, G=G, pst=pst, sk=sk: e.transpose(out=pst[0:64, i * 128:(i + 1) * 128], in_=sk[:, i, G * 64:(G + 1) * 64],
                                                                                      identity=self.identf[:]), reads=[skn, 'identf'], writes=[pn])
                        P.op('act', lambda e, G=G, pst=pst, n=n, k0=k0, kbase=kbase: e.copy(
                            out=self.KTsA[0:64, G, (kbase + k0) * 128:(kbase + k0 + n) * 128], in_=pst[0:64, 0:n * 128]), reads=[pn], writes=['bigA'])
                    P.op('pool', lambda e, sk=sk, n=n, k0=k0, kbase=kbase: e.tensor_copy(
                        out=self.VsA[:, kbase + k0:kbase + k0 + n, 0:2, 0:64], in_=sk[:, 0:n, 128:256].rearrange("p k (g d) -> p k g d", d=64)),
                        reads=[skn], writes=['bigB'])
                for kt in range(nkt + 1):
                    blocks.append((g, kt, nkt, kbase))
                kbase += nkt
            nb = len(blocks)
            vn, vnn = self.new_v(s, 6)
            for bi, (g, kt, nkt, kb) in enumerate(blocks):
                for G in range(2):
                    hk = g * 2 + G
                    if kt < nkt:
                        kt_ap, ktn = self.KTsA[0:64, G, (kb + kt) * 128:(kb + kt + 1) * 128], ['bigA']
                        v_ap, vnm, nk = self.VsA[:, kb + kt, G, 0:65], ['bigB'], 128
                    else:
                        kt_ap, ktn = self.KTn[0:64, hk, 8 * s:8 * s + 8], ['KTn']
                        v_ap, vnm, nk = vn[0:8, hk, 0:65], [vnn], 8
                    mask_ap = self.maskAs[g][0:nk, kt, :]
                    self.sample_pv(s, G, kt_ap, ktn, 64, g * 8 + G * 4, nk, v_ap, vnm, mask_ap, [f'maskAs{g}'],
                                   first=(bi == 0), last=(bi == nb - 1))
            self.sample_finish_seq(s, 2)

    def page_rows(self, s, pg, pool_name, dst, dst_name):
        P = self.P
        col = s * NPG + pg
        halves = self.pools[pool_name]
        hw = 128 if len(halves) == 2 else 64
        for h in range(len(halves)):
            P.dma('pool', lambda e, h=h: e.indirect_dma_start(out=dst[:, h * hw:(h + 1) * hw], out_offset=None, in_=halves[h],
                                                              in_offset=bass.IndirectOffsetOnAxis(ap=self.pidx[:, col:col + 1], axis=0)),
                  reads=['pidx', 'pool_' + pool_name], writes=[dst_name])

    def sample_kv_load(self, s, kname, vname):
        P = self.P
        for k0 in range(0, NPG, 2):
            u = self._uid
            self._uid += 1
            sk, skn = self.stK[u % 2], f'stK{u % 2}'
            sv, svn = self.stV[u % 2], f'stV{u % 2}'
            for i in range(2):
                self.page_rows(s, k0 + i, kname, sk[:, i, :], skn)
                self.page_rows(s, k0 + i, vname, sv[:, i, :], svn)
            for hk in range(4):
                pst, pn = self.psM[(u + hk) % 2], f'psM{(u + hk) % 2}'
                for i in range(2):
                    P.op('pe', lambda e, i=i, hk=hk, pst=pst, sk=sk: e.transpose(out=pst[0:64, i * 128:(i + 1) * 128], in_=sk[:, i, hk * 64:(hk + 1) * 64],
                                                                               identity=self.identf[:]), reads=[skn, 'identf'], writes=[pn])
                P.op('act', lambda e, hk=hk, pst=pst, k0=k0: e.copy(out=self.KTs[0:64, hk, k0 * 128:(k0 + 2) * 128], in_=pst[0:64, 0:256]),
                     reads=[pn], writes=['bigA'])
            P.op('pool', lambda e, sv=sv, k0=k0: e.tensor_copy(out=self.Vs[:, k0:k0 + 2, :, 0:64], in_=sv[:, :, :].rearrange("p k (h d) -> p k h d", d=64)),
                 reads=[svn], writes=['bigB'])

    def attn_B_sample(self, l):
        P, cfg = self.P, self.cfg
        P.op('pool', lambda e: e.memset(self.Vs[:, :, :, 64:65], 1.0), writes=['bigB'])
        P.op('pool', lambda e: e.memset(self.QT[64:96, 0:16, :], 0.0), writes=['QT'])
        for h in range(4):
            P.dma('sp', lambda e, h=h: e.dma_start(out=self.KTs[64:96, h, :], in_=self.din['t_onehotS']), writes=['bigA'])
            P.dma('sp', lambda e, h=h: e.dma_start(out=self.KTn[64:96, h, :], in_=self.din['t_onehotS'][:, 16 * 128:17 * 128]), writes=['KTn'])
        for s in range(cfg.NSQ):
            self.sample_kv_load(s, 'c1k', 'c1v')
            P.op('dve', lambda e: e.tensor_reduce(out=self.kmTs[:, :, 0:8], in_=self.KTs[0:64, :, 0:2048].rearrange("p h (n k) -> p h n k", k=256),
                                                  axis=AX.X, op=ALU.add), reads=['bigA'], writes=['kmTs'])
            P.op('dve', lambda e, s=s: e.tensor_reduce(out=self.kmTs[:, :, 8:9], in_=self.KTn[0:64, 0:4, 8 * s:8 * s + 8].unsqueeze(2),
                                                       axis=AX.X, op=ALU.add), reads=['KTn'], writes=['kmTs'])
            P.op('dve', lambda e: e.tensor_scalar(out=self.kmTs[:, :, 0:9], in0=self.kmTs[:, :, 0:9], scalar1=1.0 / 256, scalar2=None, op0=ALU.mult),
                 reads=['kmTs'], writes=['kmTs'])
            pst, pn = self.psM[0], 'psM0'
            P.op('pool', lambda e, s=s: e.tensor_copy(out=self.qsel[:], in_=self.QTf[:, :, 8 * s:8 * s + 8]), reads=['QTf'], writes=['qsel'])
            for hk in range(4):
                P.op('pe', lambda e, hk=hk: e.matmul(pst[0:32, hk * 16:hk * 16 + 9], lhsT=self.qsel[:, hk * 4:hk * 4 + 4, :].rearrange("p a b -> p (a b)"),
                                                     rhs=self.kmTs[:, hk, 0:9], start=True, stop=True), reads=['qsel', 'kmTs'], writes=[pn])
            P.op('dve', lambda e: e.tensor_copy(out=self.bss[:, :, 0:9], in_=pst[0:32, 0:64].rearrange("p (h n) -> p h n", n=16)[:, :, 0:9]),
                 reads=[pn], writes=['bss'])
            for hk in range(4):
                P.op('dve', lambda e, hk=hk: e.max(out=self.top8s[:, hk, :], in_=self.bss[:, hk, 0:8]), reads=['bss'], writes=['top8s'])
            P.op('dve', lambda e: e.tensor_tensor(out=self.bss[:, :, 0:8], in0=self.bss[:, :, 0:8],
                                                  in1=self.top8s[:, :, 2:3].to_broadcast([32, 4, 8]), op=ALU.is_ge), reads=['bss', 'top8s'], writes=['bss'])
            P.op('dve', lambda e: e.memset(self.bss[:, :, 8:9], 1.0), reads=['bss'], writes=['bss'])
            P.op('dve', lambda e: e.tensor_scalar(out=self.biass[:, :, 0:9], in0=self.bss[:, :, 0:9], scalar1=-1.0, scalar2=-NEG,
                                                  op0=ALU.add, op1=ALU.mult), reads=['bss'], writes=['biass'])
            for hk in range(4):
                P.op('pe', lambda e, hk=hk: e.transpose(out=self.psT[64:73, hk * 32:(hk + 1) * 32], in_=self.biass[:, hk, 0:9], identity=self.identb[0:32, 0:32]),
                     reads=['biass', 'identb'], writes=['psT'])
            P.op('act', lambda e, s=s: e.copy(out=self.QT[64:73, 0:16, 8 * s:8 * s + 8], in_=self.psT[64:73, 0:128].rearrange("p (h t) -> p h t", t=8)),
                 reads=['psT'], writes=['QT'])
            vn, vnn = self.new_v(s, 4)
            for kt in range(17):
                for hk in range(4):
                    if kt < 16:
                        kt_ap, ktn = self.KTs[0:96, hk, kt * 128:(kt + 1) * 128], ['bigA']
                        v_ap, vnm, nk, m, mn = self.Vs[:, kt, hk, 0:65], ['bigB'], 128, None, ()
                    else:
                        kt_ap, ktn = self.KTn[0:96, hk, 8 * s:8 * s + 8], ['KTn']
                        v_ap, vnm, nk, m, mn = vn[0:8, hk, 0:65], [vnn], 8, self.maskNew[0:8, :], ['maskNew']
                    self.sample_pv(s, hk, kt_ap, ktn, 96, hk * 4, nk, v_ap, vnm, m, mn, first=(kt == 0), last=(kt == 16))
            self.sample_finish_seq(s, 4)

    def attn_C_sample(self, l):
        P, cfg = self.P, self.cfg
        NSQ = cfg.NSQ
        LK = 2048 + 8
        BA = ['bigA']
        P.op('pool', lambda e: e.memset(self.big[:, 0:2176], 0.0), writes=BA)
        chunks = [(0, 512), (512, 512), (1024, 512), (1536, 512), (2048, 8)]
        for s in range(NSQ):
            for k0 in range(0, NPG, 4):
                u = self._uid
                self._uid += 1
                si, sin_ = self.stI[u % 2], f'stI{u % 2}'
                for i in range(4):
                    self.page_rows(s, k0 + i, 'c2i', si[:, i, :], sin_)
                pst, pn = self.psM[u % 2], f'psM{u % 2}'
                for i in range(4):
                    P.op('pe', lambda e, i=i, pst=pst, si=si: e.transpose(out=pst[0:64, i * 128:(i + 1) * 128], in_=si[:, i, :], identity=self.identf[:]),
                         reads=[sin_, 'identf'], writes=[pn])
                P.op('act', lambda e, pst=pst, k0=k0: e.copy(out=self.kiT1[:, k0 * 128:(k0 + 4) * 128], in_=pst[0:64, 0:512]), reads=[pn], writes=['kiT1'])
            P.op('pool', lambda e, s=s: e.tensor_copy(out=self.kiT1[:, 2048:2056], in_=self.kiTn[:, 8 * s:8 * s + 8]), reads=['kiTn'], writes=['kiT1'])
            qp, qpn = self.qpad[s % 2], f'qpad{s % 2}'
            P.op('pool', lambda e, qp=qp: e.memset(qp[:], 0.0), writes=[qpn])
            P.op('pool', lambda e, qp=qp, s=s: e.tensor_copy(out=qp[:, :, 8 * s:8 * s + 8], in_=self.qiT[:, :, 8 * s:8 * s + 8]), reads=['qiT'], writes=[qpn])
            for (c0, cw) in chunks:
                sc = self.big[:, c0:c0 + cw]
                for h in range(8):
                    b = self._uid % 2
                    self._uid += 1
                    pss, psn = self.psS[b], f'psS{b}'
                    rf, rfn = self.Rf[b], f'Rf{b}'
                    P.op('pe', lambda e, qp=qp, pss=pss, c0=c0, cw=cw, h=h: e.matmul(pss[:, 0:cw], lhsT=qp[:, h, :], rhs=self.kiT1[:, c0:c0 + cw],
                                                                                   start=True, stop=True), reads=[qpn, 'kiT1'], writes=[psn])
                    P.op('act', lambda e, pss=pss, rf=rf, cw=cw: e.activation(out=rf[:, 0:cw], in_=pss[:, 0:cw], func=AF.Relu), reads=[psn], writes=[rfn])
                    P.op('dve', lambda e, rf=rf, sc=sc, h=h, cw=cw: e.scalar_tensor_tensor(out=sc, in0=rf[:, 0:cw], scalar=self.sgn[:, h:h + 1], in1=sc,
                                                                                          op0=ALU.mult, op1=ALU.add), reads=[rfn, 'sgn'] + BA, writes=BA)
        P.op('dve', lambda e: e.tensor_tensor(out=self.big[:, 2048:2056], in0=self.big[:, 2048:2056], in1=self.cbS[:], op=ALU.add),
             reads=BA + ['cbS'], writes=BA)
        P.op('dve', lambda e: e.tensor_scalar(out=self.cposQ[:, 0:8], in0=self.cbS[:], scalar1=-2.0, scalar2=-BIGF, op0=ALU.mult, op1=ALU.add),
             reads=['cbS'], writes=['ltab'])
        self.dsa_select(LK, 2048, BA)
        P.op('dve', lambda e: e.memset(self.big[:, 2056:2176], -BIGF), reads=BA, writes=BA)
        for k0 in range(0, 17, 4):
            n = min(4, 17 - k0)
            self.dsa_maskT(k0, n, self.maskTs[:, k0:k0 + n, :], 'maskTs', BA)
        P.op('pool', lambda e: e.memset(self.Vs[:, :, :, 64:65], 1.0), writes=['bigB'])
        for s in range(NSQ):
            self.sample_kv_load(s, 'c2k', 'c2v')
            vn, vnn = self.new_v(s, 4)
            for kt in range(17):
                for hk in range(4):
                    if kt < 16:
                        kt_ap, ktn = self.KTs[0:64, hk, kt * 128:(kt + 1) * 128], ['bigA']
                        v_ap, vnm, nk = self.Vs[:, kt, hk, 0:65], ['bigB'], 128
                    else:
                        kt_ap, ktn = self.KTn[0:64, hk, 8 * s:8 * s + 8], ['KTn']
                        v_ap, vnm, nk = vn[0:8, hk, 0:65], [vnn], 8
                    m = self.maskTs[0:nk, kt, 8 * s:8 * s + 8].unsqueeze(1).to_broadcast([nk, 4, 8])
                    self.sample_pv(s, hk, kt_ap, ktn, 64, hk * 4, nk, v_ap, vnm, m, ['maskTs'], first=(kt == 0), last=(kt == 16))
            self.sample_finish_seq(s, 4)


_CACHE = {}


def get_program(cfg_key):
    if cfg_key not in _CACHE:
        cfg = Cfg(*cfg_key)
        b = Builder(cfg)
        nc = b.build()
        _CACHE[cfg_key] = (cfg, b, nc)
    return _CACHE[cfg_key]


def shard_inputs(cfg, inputs):
    NT, NSQ = cfg.NT, cfg.NSQ
    maps = []
    f32 = lambda a: np.ascontiguousarray(np.asarray(a), dtype=np.float32)
    xp = np.asarray(inputs['x_prompt'])
    xs = np.asarray(inputs['x_sample'])
    pt = np.asarray(inputs['page_table']).astype(np.int32)
    for c in range(8):
        b, r = divmod(c, 4)
        m = {}
        tiles = xp[b].reshape(cfg.SEQ // 128, 128, 1024)[r::4]
        m['xp'] = f32(tiles.reshape(NT * 128, 1024))
        xs_c = np.zeros((128, 1024), np.float32)
        xs_c[:NSQ * 8] = xs[c * NSQ:(c + 1) * NSQ].reshape(NSQ * 8, 1024)
        m['xs'] = xs_c
        for l in (0, 3):
            if l < cfg.depth:
                for W, _ in DIL:
                    m[f'st{l}_{W}'] = f32(np.asarray(inputs[f'state_l{l}_kv_w{W}'])[c * NSQ:(c + 1) * NSQ].reshape(NSQ, W, 256))
        for nm, key, wd in (("c1k", 'cache_l1_k', 256), ("c1v", 'cache_l1_v', 256), ("c2k", 'cache_l2_k', 256),
                            ("c2v", 'cache_l2_v', 256), ("c2i", 'cache_l2_kidx', 64)):
            if key in inputs:
                a = np.asarray(inputs[key])
                m[nm] = f32(a.reshape(cfg.NPOOL // 40, 4, 10 * 128 * wd)[:, r].reshape(cfg.PSH * 128, wd))
            else:
                m[nm] = np.zeros((cfg.PSH * 128, wd), np.float32)
        m['pt'] = np.ascontiguousarray(pt[c * NSQ:(c + 1) * NSQ].reshape(1, NSQ * NPG))
        for l in range(cfg.depth):
            m[f'norm{l}'] = f32(inputs[f'l{l}_norm']).reshape(1, 1024)
            m[f'win{l}'] = f32(inputs[f'l{l}_w_in'])
            m[f'qn{l}'] = f32(inputs[f'l{l}_q_norm']).reshape(1, 64)
            m[f'kn{l}'] = f32(inputs[f'l{l}_k_norm']).reshape(1, 64)
            m[f'wout{l}'] = f32(inputs[f'l{l}_w_out'])
        for k, v in host_tables(cfg, r).items():
            m['t_' + k] = v
        maps.append(m)
    return maps


def assemble(cfg, res):
    NT, NSQ, SEQ, NS = cfg.NT, cfg.NSQ, cfg.SEQ, cfg.NS

    def prompt_rows(name, width):
        out = np.zeros((2, SEQ // 128, 128, width), np.float32)
        for c in range(8):
            b, r = divmod(c, 4)
            out[b, r::4] = res[c][name].reshape(NT, 128, width)
        return out.reshape(2, SEQ, width)

    def sample_rows(name, width):
        return np.concatenate([res[c][name][:NSQ * 8].reshape(NSQ, 8, width) for c in range(8)], 0)
    outs = [prompt_rows('yp', 1024), sample_rows('ys', 1024)]
    for l in range(cfg.depth):
        kind = l % 3
        if kind == 0:
            kv = prompt_rows(f'kvp{l}', 768).reshape(2, SEQ, 3, 2, 2, 64)
            for g, (W, _) in enumerate(DIL):
                w = min(W, SEQ)
                outs.append(np.ascontiguousarray(kv[:, SEQ - w:, g]))
                outs.append(np.concatenate([res[c][f'sto{l}_{W}'] for c in range(8)], 0).reshape(NS, W, 2, 2, 64))
        else:
            outs.append(prompt_rows(f'kp{l}', 256).reshape(2, SEQ, 4, 64))
            outs.append(sample_rows(f'ks{l}', 256).reshape(NS, 8, 4, 64))
            outs.append(prompt_rows(f'vp{l}', 256).reshape(2, SEQ, 4, 64))
            outs.append(sample_rows(f'vs{l}', 256).reshape(NS, 8, 4, 64))
            if kind == 2:
                outs.append(prompt_rows(f'kip{l}', 64))
                outs.append(sample_rows(f'kis{l}', 64))
    return tuple(outs)


def run_cfg(cfg_key, inputs):
    cfg, b, nc = get_program(cfg_key)
    maps = shard_inputs(cfg, inputs)
    res = run_bass_kernel_spmd(nc, maps, core_ids=list(range(8)))
    return assemble(cfg, res.results)


def kernel(**inputs):
    return run_cfg((8192, 128, 2560, 4), inputs)
```

```python
import numpy as np
import ml_dtypes
import concourse.bass as bass
import concourse.mybir as mybir
from concourse.bass_utils import run_bass_kernel_spmd
from contextlib import ExitStack

F32 = mybir.dt.float32
BF16 = mybir.dt.bfloat16
I32 = mybir.dt.int32
ALU = mybir.AluOpType
AF = mybir.ActivationFunctionType
AX = mybir.AxisListType

PAST = 2048
NPG = 16
DIL = ((128, 1), (512, 4), (2048, 16))
NEG = -30000.0
BIGF = 1.0e30


class StopBuild(Exception):
    pass


class Prog:
    ENG = ('pe', 'act', 'dve', 'pool', 'sp')
    CH = 4096
    DMA_DEPTH = 12

    def __init__(self, nc, es, n_dma_sems=40):
        self.nc = nc
        self.es = es
        self.q = {e: [] for e in self.ENG}
        self.cnt = {e: 0 for e in self.ENG}
        self.sems = {}
        self.waited = {}
        self.buf_w = {}
        self.buf_r = {}
        self.nd = n_dma_sems
        for k in range(n_dma_sems):
            self.sems[('d', k)] = es.enter_context(nc.semaphore(f"dsem{k}"))
        self.dval = [0] * n_dma_sems
        self.dnext = 0
        self.n_ops = 0

    def _sem(self, key):
        if key not in self.sems:
            self.sems[key] = self.es.enter_context(self.nc.semaphore(f"s_{key[0]}_{key[1]}"))
        return self.sems[key]

    def _wait(self, e, tok):
        key, val, src = tok
        if src == 'pe' and e == 'pe':
            return
        if self.waited.get((e, key), 0) >= val:
            return
        self.waited[(e, key)] = val
        self._sem(key)
        self.q[e].append(('wait', key, val))

    def _deps(self, e, reads, writes):
        for b in reads:
            t = self.buf_w.get(b)
            if t is not None:
                self._wait(e, t)
        for b in writes:
            t = self.buf_w.get(b)
            if t is not None:
                self._wait(e, t)
            for key, (val, src) in self.buf_r.get(b, {}).items():
                self._wait(e, (key, val, src))

    def _mark(self, tok, reads, writes):
        key, val, src = tok
        for b in reads:
            d = self.buf_r.setdefault(b, {})
            if d.get(key, (0, None))[0] < val:
                d[key] = (val, src)
        for b in writes:
            self.buf_w[b] = tok
            self.buf_r[b] = {}

    limit = None

    def _chk(self):
        import sys as _s
        if not hasattr(self, 'log'):
            self.log = []
        f = _s._getframe(2)
        self.log.append((self.n_ops, f.f_lineno, f.f_code.co_name))
        if self.limit is not None and self.n_ops >= self.limit:
            raise StopBuild()

    def op(self, e, fn, reads=(), writes=()):
        self._chk()
        self._deps(e, reads, writes)
        idx = self.cnt[e]
        j, k = divmod(idx, self.CH)
        key = (e, j)
        self._sem(key)
        self.q[e].append(('op', fn, key))
        self.cnt[e] += 1
        self.n_ops += 1
        tok = (key, k + 1, e)
        self._mark(tok, reads, writes)
        return tok

    def dma(self, e, fn, reads=(), writes=()):
        self._chk()
        self._deps(e, reads, writes)
        k = self.dnext
        self.dnext = (k + 1) % self.nd
        key = ('d', k)
        if self.dval[k] > 0:
            self._wait(e, (key, self.dval[k], 'dma'))
        hist = self.__dict__.setdefault('dhist', {}).setdefault(e, [])
        if len(hist) >= self.DMA_DEPTH:
            self._wait(e, hist[-self.DMA_DEPTH])
        self.dval[k] += 16
        self.q[e].append(('dma', fn, key))
        self.n_ops += 1
        tok = (key, self.dval[k], 'dma')
        hist.append(tok)
        self._mark(tok, reads, writes)
        return tok

    def finish(self):
        for k in range(self.nd):
            if self.dval[k] > 0:
                self._wait('sp', (('d', k), self.dval[k], 'dma'))
        for e in ('pe', 'act', 'dve', 'pool'):
            if self.cnt[e] > 0:
                j, k = divmod(self.cnt[e] - 1, self.CH)
                self._wait('sp', ((e, j), k + 1, e))

    def _emit(self, eng, e):
        for item in self.q[e]:
            if item[0] == 'wait':
                eng.wait_ge(self.sems[item[1]], item[2])
            elif item[0] == 'op':
                item[1](eng).then_inc(self.sems[item[2]], 1)
            else:
                item[1](eng).then_inc(self.sems[item[2]], 16)

    def emit(self):
        self.finish()
        with self.nc.Block() as block:
            @block.tensor
            def _(eng):
                self._emit(eng, 'pe')

            @block.scalar
            def _(eng):
                self._emit(eng, 'act')

            @block.vector
            def _(eng):
                self._emit(eng, 'dve')

            @block.gpsimd
            def _(eng):
                self._emit(eng, 'pool')

            @block.sync
            def _(eng):
                self._emit(eng, 'sp')


class Cfg:
    def __init__(self, SEQ=8192, NS=128, NPOOL=2560, depth=4, stop=None, ag_delay=1200, pool_delay=2000):
        self.stop = stop
        self.ag_delay = ag_delay
        self.pool_delay = pool_delay
        self.SEQ, self.NS, self.NPOOL = SEQ, NS, NPOOL
        self.NT = SEQ // 512
        self.NSQ = NS // 8
        self.NTT = self.NT + 1
        self.PSH = NPOOL // 4
        self.depth = depth


def col_layout(kind):
    if kind == 0:
        return dict(q=(0, 1536), k=(1536, 384), v=(1920, 384), g=(2304, 512), IN=2816, OUT=512, HQ=24, HK=6)
    if kind == 1:
        return dict(q=(0, 1024), k=(1024, 256), v=(1280, 256), g=(1536, 1024), IN=2560, OUT=1024, HQ=16, HK=4)
    return dict(q=(0, 1024), k=(1024, 256), v=(1280, 256), g=(1536, 1024), qi=(2560, 512), ki=(3072, 64),
                wi=(3136, 8), IN=3144, OUT=1024, HQ=16, HK=4)


def a_tiles():
    lst = []
    for dj in range(4, -1, -1):
        for rp in range(4):
            for g in range(3):
                if g == 0 and not (dj == 0 or (dj == 1 and rp == 3)):
                    continue
                if g == 1 and dj > 1:
                    continue
                lst.append((g, dj, rp))
    return lst


A_TILES = a_tiles()


def host_tables(cfg, r):
    NT, NTT = cfg.NT, cfg.NTT
    t = {}
    half = 32
    inv = (10000.0 ** (-np.arange(half, dtype=np.float32) / half)).astype(np.float32)
    pos = np.zeros((128, NTT), np.float32)
    for j in range(NT):
        pos[:, j] = (4 * j + r) * 128 + np.arange(128)
    pos[:, NT] = PAST + (np.arange(128) % 8)
    ang = pos[:, :, None] * inv[None, None, :]
    t['cos'] = np.cos(ang).astype(np.float32).reshape(128, NTT * 32)
    t['sin'] = np.sin(ang).astype(np.float32).reshape(128, NTT * 32)
    t['identf'] = np.eye(128, dtype=np.float32)
    t['identb'] = np.eye(128, dtype=np.float32).astype(ml_dtypes.bfloat16)
    k = np.arange(128)[:, None]
    q = np.arange(128)[None, :]
    mA = np.zeros((128, len(A_TILES), 128), np.float32)
    for i, (g, dj, rp) in enumerate(A_TILES):
        W, d = DIL[g]
        delta = (4 * dj + r - rp) * 128 + q - k
        mA[:, i, :] = ((delta >= 0) & (delta <= W) & (delta % d == 0))
    t['maskA'] = mA.astype(ml_dtypes.bfloat16).reshape(128, -1)
    mC = np.zeros((128, 4, 128), np.float32)
    for rp in range(4):
        mC[:, rp, :] = ((rp * 128 + k) <= (r * 128 + q))
    t['maskC'] = mC.astype(ml_dtypes.bfloat16).reshape(128, -1)
    oh = np.zeros((32, NT, 4, 128), np.float32)
    for j in range(NT):
        for rp in range(4):
            n = 2 * j + rp // 2
            if n < 32:
                oh[n, j, rp, :] = 1.0
    t['onehotB'] = oh.astype(ml_dtypes.bfloat16).reshape(32, -1)
    pb = np.full((128, NT, 32), -BIGF, np.float32)
    ow = np.zeros((128, NT, 32), np.float32)
    for j in range(NT):
        own = 2 * j + r // 2
        pb[:, j, :own] = 0.0
        if own < 32:
            ow[:, j, own] = 1.0
    t['pastB'] = pb.reshape(128, -1)
    t['ownB'] = ow.reshape(128, -1)
    cb = np.zeros((128, 4, 128), np.float32)
    qq = np.arange(128)[:, None]
    kk = np.arange(128)[None, :]
    for rp in range(4):
        cb[:, rp, :] = np.where((rp * 128 + kk) <= (r * 128 + qq), 0.0, -BIGF)
    t['cbQ'] = cb.reshape(128, -1)
    tq = np.arange(8)[None, :]
    for g, (W, d) in enumerate(DIL):
        nkt = W // 128
        m = np.zeros((128, nkt + 1, 4, 8), np.float32)
        p = np.arange(128)[:, None]
        for kt in range(nkt):
            i = kt * 128 + p
            dlt = W + tq - i
            m[:, kt, :, :] = ((dlt >= 0) & (dlt <= W) & (dlt % d == 0))[:, None, :]
        dlt = tq - p
        m[:, nkt, :, :] = ((dlt >= 0) & (dlt % d == 0) & (p < 8))[:, None, :]
        t[f'maskAs{g}'] = m.astype(ml_dtypes.bfloat16).reshape(128, -1)
    mn = np.zeros((128, 4, 8), np.float32)
    p = np.arange(128)[:, None]
    mn[:, :, :] = ((p <= tq) & (p < 8))[:, None, :]
    t['maskNew'] = mn.astype(ml_dtypes.bfloat16).reshape(128, -1)
    ohs = np.zeros((32, 17 * 128), np.float32)
    for kt in range(16):
        ohs[kt // 2, kt * 128:(kt + 1) * 128] = 1.0
    ohs[8, 16 * 128:] = 1.0
    t['onehotS'] = ohs.astype(ml_dtypes.bfloat16)
    t['cbS'] = np.where(np.arange(8)[None, :] <= (np.arange(128) % 8)[:, None], 0.0, -BIGF).astype(np.float32)
    return t


TABLE_SPECS = None


def table_specs(cfg):
    t = host_tables(cfg, 0)
    return {k: (list(v.shape), BF16 if v.dtype == ml_dtypes.bfloat16 else F32) for k, v in t.items()}


class Builder:
    def __init__(self, cfg):
        self.cfg = cfg
        self.nc = bass.Bass("TRN2", target_bir_lowering=False, num_devices=8)
        self.din = {}
        self.dout = {}

    def inp(self, name, shape, dt=F32):
        self.din[name] = self.nc.dram_tensor(name, list(shape), dt, kind="ExternalInput").ap()
        return self.din[name]

    def outp(self, name, shape, dt=F32):
        self.dout[name] = self.nc.dram_tensor(name, list(shape), dt, kind="ExternalOutput").ap()
        return self.dout[name]

    def scratch(self, name, shape, dt):
        return self.nc.dram_tensor(name, list(shape), dt, kind="Internal").ap()

    def declare(self):
        cfg = self.cfg
        NT, NSQ = cfg.NT, cfg.NSQ
        self.inp("xp", [NT * 128, 1024])
        self.inp("xs", [128, 1024])
        for l in (0, 3):
            if l < cfg.depth:
                for W, _ in DIL:
                    self.inp(f"st{l}_{W}", [NSQ, W, 256])
        for nm, wd in (("c1k", 256), ("c1v", 256), ("c2k", 256), ("c2v", 256), ("c2i", 64)):
            self.inp(nm, [cfg.PSH * 128, wd])
        self.inp("pt", [1, NSQ * NPG], I32)
        for l in range(cfg.depth):
            L = col_layout(l % 3)
            self.inp(f"norm{l}", [1, 1024])
            self.inp(f"win{l}", [1024, L['IN']])
            self.inp(f"qn{l}", [1, 64])
            self.inp(f"kn{l}", [1, 64])
            self.inp(f"wout{l}", [L['OUT'], 1024])
        for k, (shape, dt) in table_specs(cfg).items():
            self.inp("t_" + k, shape, dt)
        self.outp("yp", [NT * 128, 1024])
        self.outp("ys", [128, 1024])
        for l in range(cfg.depth):
            kind = l % 3
            if kind == 0:
                self.outp(f"kvp{l}", [NT * 128, 768])
                for W, _ in DIL:
                    self.outp(f"sto{l}_{W}", [NSQ, W, 256])
            else:
                self.outp(f"kp{l}", [NT * 128, 256])
                self.outp(f"vp{l}", [NT * 128, 256])
                self.outp(f"ks{l}", [128, 256])
                self.outp(f"vs{l}", [128, 256])
                if kind == 2:
                    self.outp(f"kip{l}", [NT * 128, 64])
                    self.outp(f"kis{l}", [128, 64])
        self.Xd = self.scratch("Xd", [(NT + 1) * 128, 1024], F32)
        self.KT_in = [self.scratch(f"KT_in{c}", [128, NT * 128], BF16) for c in range(3)]
        self.KT_all = [self.scratch(f"KT_all{c}", [4 * 128, NT * 128], BF16) for c in range(3)]
        self.NVC = (NT + 3) // 4
        self.V_in = [self.scratch(f"V_in{c}", [128, 4 * 396], BF16) for c in range(self.NVC)]
        self.V_all = [self.scratch(f"V_all{c}", [4 * 128, 4 * 396], BF16) for c in range(self.NVC)]
        self.KI_in = self.scratch("KI_in", [64, NT * 128], BF16)
        self.KI_all = self.scratch("KI_all", [4 * 64, NT * 128], BF16)
        self.Wb = [self.scratch(f"Wb{l}", [128, 8 * col_layout(l % 3)['IN']], BF16) for l in range(cfg.depth)]
        self.Wob = [self.scratch(f"Wob{l}", [128, (col_layout(l % 3)['OUT'] // 128) * 1024], BF16) for l in range(cfg.depth)]
        self.pools = {}
        self.pool_in = {}
        for nm, wd in (("c1k", 256), ("c1v", 256), ("c2k", 256), ("c2v", 256), ("c2i", 64)):
            nh = 2 if wd == 256 else 1
            self.pools[nm] = [self.scratch(f"pool_{nm}{h}", [cfg.NPOOL * 128, wd // nh], F32) for h in range(nh)]
            self.pool_in[nm] = [self.scratch(f"pin_{nm}{h}", [cfg.PSH * 128, wd // nh], F32) for h in range(nh)]

    def build(self):
        cfg = self.cfg
        nc = self.nc
        self.declare()
        with ExitStack() as es:
            self.es = es
            self.P = P = Prog(nc, es)
            if isinstance(cfg.stop, str) and cfg.stop.startswith('n'):
                P.limit = int(cfg.stop[1:])
            self._uid = 0
            self.alloc_persistent()
            try:
                self.build_body()
            except StopBuild:
                pass
            P.emit()
        return nc

    def build_body(self):
        cfg, P = self.cfg, self.P
        if True:
            if cfg.depth > 1:
                for nm in ("c1k", "c1v", "c2k", "c2v", "c2i"):
                    if nm.startswith("c2") and cfg.depth < 3:
                        continue
                    nrows = cfg.PSH * 128
                    halves = self.pool_in[nm]
                    hw = 256 // 2 if len(halves) == 2 else 64
                    for h in range(len(halves)):
                        r0 = 0
                        while r0 < nrows:
                            n = min(2048, nrows - r0)
                            a = n // 128
                            P.dma('sp', lambda e, nm=nm, r0=r0, n=n, a=a, h=h, hw=hw: e.dma_start(
                                out=self.pool_in[nm][h][r0:r0 + n, :].rearrange("(p a) f -> p a f", a=a),
                                in_=self.din[nm][r0:r0 + n, h * hw:(h + 1) * hw].rearrange("(p a) f -> p a f", a=a)), writes=['pin_' + nm])
                            r0 += n
                        CP = 10 * 128
                        for k in range(cfg.PSH // 10):
                            P.op('pool', lambda e, nm=nm, k=k, h=h, CP=CP: e.collective_compute(
                                "AllGather", ALU.bypass, replica_groups=[[0, 1, 2, 3], [4, 5, 6, 7]],
                                ins=[self.pool_in[nm][h][k * CP:(k + 1) * CP, :]], outs=[self.pools[nm][h][k * 4 * CP:(k + 1) * 4 * CP, :]]),
                                reads=['pin_' + nm], writes=['pool_' + nm])
            self.load_tables()
            for l in range(cfg.depth):
                self.layer(l)
            for _ in range(cfg.pool_delay):
                P.op('pool', lambda e: e.memset(self.big2[:, 0:4096], 0), writes=['big2'])
            P.op('pool', lambda e: e.memset(self.big2[:, 0:4], 0), writes=['big2'] + ['pool_' + nm for nm in ("c1k", "c1v", "c2k", "c2v", "c2i")])
            for l in range(cfg.depth):
                self.layer_sample(l)

    def sb(self, name, shape, dt):
        return self.es.enter_context(self.nc.sbuf_tensor(name, list(shape), dt))

    def ps(self, name, shape, dt):
        return self.es.enter_context(self.nc.psum_tensor(name, list(shape), dt))

    def alloc_persistent(self):
        cfg = self.cfg
        NTT = cfg.NTT
        sb, ps = self.sb, self.ps
        self.cos = sb("cos", [128, NTT, 32], F32)
        self.sin = sb("sin", [128, NTT, 32], F32)
        self.identf = sb("identf", [128, 128], F32)
        self.identb = sb("identb", [128, 128], BF16)
        self.ltab = sb("ltab", [128, len(A_TILES) * 128], BF16)
        self.maskA = self.ltab[:, :].rearrange("p (a b) -> p a b", b=128)
        lt32 = self.ltab[:, :].bitcast(F32)
        self.pastB = lt32[:, 0:cfg.NT * 32].rearrange("p (a b) -> p a b", b=32)
        self.ownB = lt32[:, 512:512 + cfg.NT * 32].rearrange("p (a b) -> p a b", b=32)
        self.cbQ = lt32[:, 0:512]
        self.cposQ = lt32[:, 512:1024]
        self.maskC = sb("maskC", [128, 4, 128], BF16)
        self.maskAs = [sb(f"maskAs{g}", [128, DIL[g][0] // 128 + 1, 32], BF16) for g in range(3)]
        self.maskNew = sb("maskNew", [128, 32], BF16)
        self.cbS = sb("cbS", [128, 8], F32)
        self.ptb = sb("ptb", [128, cfg.NSQ * NPG], I32)
        self.pidx = sb("pidx", [128, cfg.NSQ * NPG], I32)
        self.iot = sb("iot", [128, 1], I32)
        self.consts = sb("consts", [128, 8], F32)
        self.ones_row = sb("ones_row", [1, 128], F32)
        self.zrow = sb("zrow", [128, 512], BF16)
        self.onesf = sb("onesf", [128, 128], F32)
        self.dth = sb("dth", [128, 128], F32)
        self.ws = [sb(f"ws{i}", [128, 8, 512], BF16) for i in range(2)]
        self.gain = sb("gain", [128, 1024], F32)
        self.qg = sb("qg", [128, 64], F32)
        self.kg = sb("kg", [128, 64], F32)
        self.xt = [sb(f"xt{i}", [128, 1024], F32) for i in range(2)]
        self.junkf = sb("junkf", [128, 1024], F32)
        self.stat = sb("stat", [128, 64], F32)
        self.hb = sb("hb", [128, 1024], BF16)
        self.hT = sb("hT", [128, 8, 128], BF16)
        self.zf = sb("zf", [128, 1536], F32)
        self.kvA = sb("kvA", [128, 768], F32)
        self.kif = sb("kif", [128, 64], F32)
        self.zb = sb("zb", [128, 1536], BF16)
        self.rt = [sb(f"rt{i}", [128, 8 * 32], F32) for i in range(4)]
        self.QT = sb("QT", [96, 24, 128], BF16)
        self.QTf = sb("QTf", [64, 16, 128], F32)
        self.qiT = sb("qiT", [64, 8, 128], BF16)
        self.sgn = sb("sgn", [128, 8], F32)
        self.wabs = sb("wabs", [128, 8], F32)
        self.KTst = sb("KTst", [64, 6, 128], BF16)
        self.KTn = sb("KTn", [96, 6, 128], BF16)
        self.kiTst = sb("kiTst", [64, 128], BF16)
        self.kiTn = sb("kiTn", [64, 128], BF16)
        self.Vst = sb("Vst", [128, 6, 66], BF16)
        self.Vnew = sb("Vnew", [128, 6, 66], BF16)
        self.Vn2 = [sb(f"Vn2_{i}", [8, 6, 66], BF16) for i in range(2)]
        self.g = sb("g", [128, 1024], F32)
        self.otok = sb("otok", [128, 1024], F32)
        self.og = sb("og", [128, 1024], BF16)
        self.ogT = sb("ogT", [128, 8, 128], BF16)
        self.rec = sb("rec", [128, 16], F32)
        self.KTc = [sb(f"KTc{i}", [96, 6, 512], BF16) for i in range(2)]
        self.Vc = [sb(f"Vc{i}", [128, 4, 396], BF16) for i in range(2)]
        self.kiTc = [sb(f"kiTc{i}", [64, 512], BF16) for i in range(2)]
        self.PT = [sb(f"PT{i}", [128, 512], BF16) for i in range(3)]
        self.kmT = sb("kmT", [64, 4, 32], F32)
        self.ksum = sb("ksum", [64, 4, cfg.NT], F32)
        self.bsm = sb("bsm", [128, 16, 32], F32)
        self.top8 = sb("top8", [128, 16, 8], F32)
        self.thr = sb("thr", [128, 16], F32)
        self.biasq = sb("biasq", [128, 16, 32], BF16)
        self.big = sb("big", [128, 8192], F32)
        self.big2 = sb("big2", [128, 8192], mybir.dt.uint8)
        self.maskTc = [sb(f"maskTc{i}", [128, 4, 128], BF16) for i in range(2)]
        self.maskTs = sb("maskTs", [128, 17, 128], BF16)
        bigb = self.big[:, :].bitcast(BF16)
        self.KTs = bigb[0:96, 0:4 * 2176].rearrange("p (h n) -> p h n", h=4)
        self.Vs = bigb[:, 8704:8704 + 17 * 4 * 66].rearrange("p (k h c) -> p k h c", h=4, c=66)
        self.kmstage = bigb
        self.KTsA = bigb[0:96, 0:2 * 4352].rearrange("p (h n) -> p h n", h=2)
        self.VsA = bigb[:, 8704:8704 + 21 * 2 * 66].rearrange("p (k h c) -> p k h c", h=2, c=66)
        self.thb = sb("thb", [128, 128], F32)
        self.bis = sb("bis", [128, 8], F32)
        self.throw = sb("throw", [1, 128], F32)
        self.Rf = [sb(f"Rf{i}", [128, 512], F32) for i in range(2)]
        self.stK = [sb(f"stK{i}", [128, 2, 256], F32) for i in range(2)]
        self.stV = [sb(f"stV{i}", [128, 2, 256], F32) for i in range(2)]
        self.stI = [sb(f"stI{i}", [128, 4, 64], F32) for i in range(2)]
        self.PTs = [sb(f"PTs{i}", [128, 32], BF16) for i in range(3)]
        self.osn = sb("osn", [32, 4, 64], F32)
        self.recs = sb("recs", [32, 4], F32)
        self.kmTs = sb("kmTs", [64, 4, 16], F32)
        self.qsel = sb("qsel", [64, 16, 8], F32)
        self.bss = sb("bss", [32, 4, 16], F32)
        self.top8s = sb("top8s", [32, 4, 8], F32)
        self.biass = sb("biass", [32, 4, 16], BF16)
        self.qpad = [sb(f"qpad{i}", [64, 8, 128], BF16) for i in range(2)]
        self.kiT1 = sb("kiT1", [64, 17 * 128], BF16)
        self.psS = [ps(f"psS{i}", [128, 512], F32) for i in range(2)]
        self.psA = [ps(f"psA{i}", [128, 512], F32) for i in range(3)]
        self.psM = [ps(f"psM{i}", [128, 512], F32) for i in range(2)]
        self.psT = ps("psT", [128, 1024], BF16)

    def load_tables(self):
        P = self.P
        d = self.din

        def ld(dst, src, name, eng='sp'):
            P.dma(eng, lambda e: e.dma_start(out=dst, in_=src), writes=[name])
        NTT = self.cfg.NTT
        ld(self.cos[:].rearrange("p a b -> p (a b)"), d['t_cos'], 'cos')
        ld(self.sin[:].rearrange("p a b -> p (a b)"), d['t_sin'], 'sin')
        ld(self.identf[:], d['t_identf'], 'identf')
        ld(self.identb[:], d['t_identb'], 'identb')
        ld(self.maskC[:].rearrange("p a b -> p (a b)"), d['t_maskC'], 'maskC')
        for g in range(3):
            ld(self.maskAs[g][:].rearrange("p a b -> p (a b)"), d[f't_maskAs{g}'], f'maskAs{g}')
        ld(self.maskNew[:], d['t_maskNew'], 'maskNew')
        ld(self.cbS[:], d['t_cbS'], 'cbS')
        ld(self.ptb[:], d['pt'].partition_broadcast(128), 'ptb')
        P.op('pool', lambda e: e.iota(out=self.iot[:], pattern=[[0, 1]], base=0, channel_multiplier=1), writes=['iot'])
        P.op('dve', lambda e: e.tensor_scalar(out=self.pidx[:], in0=self.ptb[:], scalar1=128, scalar2=self.iot[:, 0:1],
                                              op0=ALU.mult, op1=ALU.add), reads=['ptb', 'iot'], writes=['pidx'])
        P.op('pool', lambda e: e.memset(self.consts[:, 0:1], 1e-6), writes=['consts'])
        P.op('pool', lambda e: e.memset(self.consts[:, 1:2], 0.5), writes=['consts'])
        P.op('pool', lambda e: e.memset(self.consts[:, 2:3], 1.0), writes=['consts'])
        P.op('pool', lambda e: e.memset(self.ones_row[:], 1.0), writes=['ones_row'])
        P.op('pool', lambda e: e.memset(self.zrow[:], 0.0), writes=['zrow'])
        P.op('pool', lambda e: e.memset(self.onesf[:], 1.0), writes=['onesf'])
        P.op('pool', lambda e: e.memset(self.Vst[:], 1.0), writes=['Vst'])
        P.op('pool', lambda e: e.memset(self.KTn[:], 0.0), writes=['KTn'])

    def stop_at(self, tag):
        if self.cfg.stop == tag or self.cfg.stop == f"L{self.l}:{tag}":
            raise StopBuild()

    def layer_setup(self, l, cast_weights):
        cfg, P = self.cfg, self.P
        kind = l % 3
        L = col_layout(kind)
        d = self.din
        self.l = l
        self.L = L
        P.dma('sp', lambda e: e.dma_start(out=self.gain[:], in_=d[f'norm{l}'].partition_broadcast(128)), writes=['gain'])
        P.dma('sp', lambda e: e.dma_start(out=self.qg[:], in_=d[f'qn{l}'].partition_broadcast(128)), writes=['qg'])
        P.dma('sp', lambda e: e.dma_start(out=self.kg[:], in_=d[f'kn{l}'].partition_broadcast(128)), writes=['kg'])
        if kind == 0:
            P.dma('sp', lambda e: e.dma_start(out=self.ltab[:, :], in_=d['t_maskA']), writes=['ltab'])
        elif kind == 1:
            P.dma('sp', lambda e: e.dma_start(out=self.pastB.rearrange("p a b -> p (a b)"), in_=d['t_pastB']), writes=['ltab'])
            P.dma('sp', lambda e: e.dma_start(out=self.ownB.rearrange("p a b -> p (a b)"), in_=d['t_ownB']), writes=['ltab'])
        else:
            P.dma('sp', lambda e: e.dma_start(out=self.cbQ, in_=d['t_cbQ']), writes=['ltab'])
            P.op('dve', lambda e: e.tensor_scalar(out=self.cposQ, in0=self.cbQ, scalar1=-2.0, scalar2=-BIGF,
                                                  op0=ALU.mult, op1=ALU.add), reads=['ltab'], writes=['ltab'])
        if not cast_weights:
            return
        IN = L['IN']
        win = d[f'win{l}'].rearrange("(c p) n -> p c n", p=128)
        wb3 = self.Wb[l].rearrange("p (c n) -> p c n", c=8)
        for n0 in range(0, IN, 512):
            w = min(512, IN - n0)
            u = self._uid
            self._uid += 1
            ws, wn = self.ws[u % 2], f'ws{u % 2}'
            for c in range(8):
                P.dma('pool', lambda e, ws=ws, n0=n0, w=w, c=c: e.dma_start(out=ws[:, c, 0:w], in_=win[:, c, n0:n0 + w]), writes=[wn])
            P.dma('sp', lambda e, ws=ws, n0=n0, w=w: e.dma_start(out=wb3[:, :, n0:n0 + w], in_=ws[:, :, 0:w]), reads=[wn], writes=[f'Wb{l}'])
        OC = L['OUT'] // 128
        wo = d[f'wout{l}'].rearrange("(c p) n -> p c n", p=128)
        wob3 = self.Wob[l].rearrange("p (c n) -> p c n", c=OC)
        for hf in range(2):
            u = self._uid
            self._uid += 1
            ws, wn = self.ws[u % 2], f'ws{u % 2}'
            for c in range(OC):
                P.dma('pool', lambda e, ws=ws, hf=hf, c=c: e.dma_start(out=ws[:, c, :], in_=wo[:, c, hf * 512:(hf + 1) * 512]), writes=[wn])
            P.dma('sp', lambda e, ws=ws, hf=hf: e.dma_start(out=wob3[:, :, hf * 512:(hf + 1) * 512], in_=ws[:, 0:OC, :]), reads=[wn], writes=[f'Wob{l}'])

    def xio(self, l):
        cfg, d = self.cfg, self.din
        NT = cfg.NT
        xin = (lambda ti: (d['xp'][ti * 128:(ti + 1) * 128, :] if ti < NT else d['xs'])) if l == 0 else \
              (lambda ti: self.Xd[ti * 128:(ti + 1) * 128, :])
        last = (l == cfg.depth - 1)
        xout = (lambda ti: (self.dout['yp'][ti * 128:(ti + 1) * 128, :] if ti < NT else self.dout['ys'])) if last else \
               (lambda ti: self.Xd[ti * 128:(ti + 1) * 128, :])
        return xin, xout

    def layer(self, l):
        cfg, P = self.cfg, self.P
        kind = l % 3
        NT = cfg.NT
        self.layer_setup(l, True)
        L = self.L
        xin, xout = self.xio(l)
        for ti in range(NT):
            self.norm_tile(ti, xin(ti))
            self.phase1_tile(l, kind, L, ti)
        self.stop_at('p1')
        rg = [[0, 1, 2, 3], [4, 5, 6, 7]]
        for c in range(L['HK'] // 2):
            P.op('pool', lambda e, c=c: e.collective_compute("AllGather", ALU.bypass, replica_groups=rg,
                                                             ins=[self.KT_in[c]], outs=[self.KT_all[c]]), reads=['KT_in'], writes=['KT_all'])
        for c in range(self.NVC):
            P.op('pool', lambda e, c=c: e.collective_compute("AllGather", ALU.bypass, replica_groups=rg,
                                                             ins=[self.V_in[c]], outs=[self.V_all[c]]), reads=['V_in'], writes=['V_all'])
        if kind == 2:
            P.op('pool', lambda e: e.collective_compute("AllGather", ALU.bypass, replica_groups=rg,
                                                        ins=[self.KI_in], outs=[self.KI_all]), reads=['KI_in'], writes=['KI_all'])
        for _ in range(cfg.ag_delay):
            P.op('pool', lambda e: e.memset(self.big2[:, 0:4096], 0), writes=['big2'])
        P.op('pool', lambda e: e.memset(self.big2[:, 0:4], 0), writes=['big2', 'KT_all', 'V_all', 'KI_all'])
        self.stop_at('ag')
        if kind == 1:
            self.moba_kmean(L)
        for ti in range(NT):
            self.norm_tile(ti, xin(ti))
            self.phase2_proj(l, kind, L, ti)
            self.stop_at('p2proj')
            if kind == 0:
                self.attn_A_prompt(ti)
            elif kind == 1:
                self.attn_B_prompt(ti)
            else:
                self.attn_C_prompt(ti)
            self.stop_at('attn')
            self.finish_tile(l, L, ti, xout(ti))
            self.stop_at('fin')
        self.stop_at('prompt')

    def layer_sample(self, l):
        cfg, P = self.cfg, self.P
        kind = l % 3
        ti = cfg.NT
        self.layer_setup(l, False)
        L = self.L
        xin, xout = self.xio(l)
        self.norm_tile(ti, xin(ti))
        self.phase1_tile(l, kind, L, ti)
        self.stop_at('s1')
        self.norm_tile(ti, xin(ti))
        self.phase2_proj(l, kind, L, ti)
        self.stop_at('s2')
        if kind == 0:
            self.attn_A_sample(l)
        elif kind == 1:
            self.attn_B_sample(l)
        else:
            self.attn_C_sample(l)
        self.stop_at('s3')
        self.finish_tile(l, L, ti, xout(ti))

    def norm_tile(self, ti, xsrc):
        P = self.P
        s = ti % 2
        xt = self.xt[s]
        xn = f'xt{s}'
        st = self.stat
        P.dma('sp', lambda e: e.dma_start(out=xt[:], in_=xsrc), writes=[xn])
        P.op('act', lambda e: e.activation(out=self.junkf[:], in_=xt[:], func=AF.Square, accum_out=st[:, 0:1]),
             reads=[xn], writes=['junkf', 'stat'])
        P.op('act', lambda e: e.activation(out=st[:, 1:2], in_=st[:, 0:1], func=AF.Sqrt, scale=1.0 / 1024,
                                           bias=self.consts[:, 0:1]), reads=['stat', 'consts'], writes=['stat'])
        P.op('dve', lambda e: e.reciprocal(out=st[:, 2:3], in_=st[:, 1:2]), reads=['stat'], writes=['stat'])
        P.op('dve', lambda e: e.scalar_tensor_tensor(out=self.hb[:], in0=xt[:], scalar=st[:, 2:3], in1=self.gain[:],
                                                     op0=ALU.mult, op1=ALU.mult), reads=[xn, 'stat', 'gain'], writes=['hb'])
        for c in range(8):
            P.op('pe', lambda e, c=c: e.transpose(out=self.psT[:, c * 128:(c + 1) * 128], in_=self.hb[:, c * 128:(c + 1) * 128],
                                                  identity=self.identb[:]), reads=['hb', 'identb'], writes=['psT'])
        P.op('act', lambda e: e.copy(out=self.hT[:], in_=self.psT[:, :].rearrange("p (a b) -> p a b", b=128)), reads=['psT'], writes=['hT'])

    def proj(self, c0, ncols, consume):
        P = self.P
        l = self.l
        wb3 = self.Wb[l].rearrange("p (c n) -> p c n", c=8)
        banks = [(self.psM[0], 'psM0'), (self.psM[1], 'psM1')]
        for n0 in range(0, ncols, 512):
            w = min(512, ncols - n0)
            u = self._uid
            self._uid += 1
            pst, pn = banks[u % 2]
            ws, wn = self.ws[u % 2], f'ws{u % 2}'
            P.dma('sp', lambda e, ws=ws, n0=n0, w=w: e.dma_start(out=ws[:, :, 0:w], in_=wb3[:, :, c0 + n0:c0 + n0 + w]),
                  reads=[f'Wb{l}'], writes=[wn])
            for c in range(8):
                P.op('pe', lambda e, c=c, pst=pst, w=w, ws=ws: e.matmul(pst[:, 0:w], lhsT=self.hT[:, c, :], rhs=ws[:, c, 0:w],
                                                                       start=(c == 0), stop=(c == 7)),
                     reads=['hT', wn], writes=[pn])
            consume(n0, w, pst, pn)

    def qk_post(self, pst, pn, w, gain_t, gain_n, ti, out_ap, out_name, do_norm=True):
        P = self.P
        H = w // 64
        st = self.stat
        v3 = lambda ap, dd=64: ap.rearrange("p (h d) -> p h d", d=dd)
        rt = self.rt
        if do_norm:
            P.op('act', lambda e: e.activation(out=self.junkf[:, 0:w], in_=pst[:, 0:w], func=AF.Square), reads=[pn], writes=['junkf'])
            P.op('dve', lambda e: e.tensor_reduce(out=st[:, 8:8 + H], in_=v3(self.junkf[:, 0:w]), axis=AX.X, op=ALU.add),
                 reads=['junkf'], writes=['stat'])
            P.op('act', lambda e: e.activation(out=st[:, 16:16 + H], in_=st[:, 8:8 + H], func=AF.Sqrt, scale=1.0 / 64,
                                               bias=self.consts[:, 0:1]), reads=['stat', 'consts'], writes=['stat'])
            P.op('dve', lambda e: e.reciprocal(out=st[:, 24:24 + H], in_=st[:, 16:16 + H]), reads=['stat'], writes=['stat'])
            P.op('dve', lambda e: e.tensor_tensor(out=v3(self.junkf[:, 0:w]), in0=v3(pst[:, 0:w]),
                                                  in1=st[:, 24:24 + H].unsqueeze(2).to_broadcast([128, H, 64]), op=ALU.mult),
                 reads=[pn, 'stat'], writes=['junkf'])
            P.op('pool', lambda e: e.tensor_tensor(out=v3(self.junkf[:, 0:w]), in0=v3(self.junkf[:, 0:w]),
                                                   in1=gain_t[:, :].unsqueeze(1).to_broadcast([128, H, 64]), op=ALU.mult),
                 reads=['junkf', gain_n], writes=['junkf'])
        else:
            P.op('act', lambda e: e.copy(out=self.junkf[:, 0:w], in_=pst[:, 0:w]), reads=[pn], writes=['junkf'])
        src = v3(self.junkf[:, 0:w])
        x1, x2 = src[:, :, 0:32], src[:, :, 32:64]
        cosb = self.cos[:, ti, :].unsqueeze(1).to_broadcast([128, H, 32])
        sinb = self.sin[:, ti, :].unsqueeze(1).to_broadcast([128, H, 32])
        r = [v3(rt[i][:, 0:H * 32], 32) for i in range(4)]
        o3 = v3(out_ap)
        P.op('dve', lambda e: e.tensor_tensor(out=r[0], in0=x1, in1=cosb, op=ALU.mult), reads=['junkf', 'cos'], writes=['rt0'])
        P.op('pool', lambda e: e.tensor_tensor(out=r[1], in0=x2, in1=sinb, op=ALU.mult), reads=['junkf', 'sin'], writes=['rt1'])
        P.op('pool', lambda e: e.tensor_tensor(out=r[2], in0=x2, in1=cosb, op=ALU.mult), reads=['junkf', 'cos'], writes=['rt2'])
        P.op('dve', lambda e: e.tensor_tensor(out=r[3], in0=x1, in1=sinb, op=ALU.mult), reads=['junkf', 'sin'], writes=['rt3'])
        P.op('dve', lambda e: e.tensor_tensor(out=o3[:, :, 0:32], in0=r[0], in1=r[1], op=ALU.subtract),
             reads=['rt0', 'rt1'], writes=[out_name])
        P.op('pool', lambda e: e.tensor_tensor(out=o3[:, :, 32:64], in0=r[2], in1=r[3], op=ALU.add),
             reads=['rt2', 'rt3'], writes=[out_name])

    def phase1_tile(self, l, kind, L, ti):
        cfg, P = self.cfg, self.P
        NT = cfg.NT
        HK = L['HK']
        nk = L['k'][1]
        nv = L['v'][1]
        is_s = (ti == NT)
        rows = slice(ti * 128, (ti + 1) * 128)
        if kind == 0:
            kdst = lambda: self.kvA[:].rearrange("p (g x) -> p g x", x=256)[:, :, 0:128]
            vdst = lambda: self.kvA[:].rearrange("p (g x) -> p g x", x=256)[:, :, 128:256]
        def k_cons(n0, w, pst, pn):
            self.qk_post(pst, pn, w, self.kg, 'kg', ti, self.zf[:, 0:w], 'zf')
        self.proj(L['k'][0], nk, k_cons)
        zk = self.zf[:, 0:nk]
        if kind == 0:
            P.op('act', lambda e: e.copy(out=kdst(), in_=zk.rearrange("p (g x) -> p g x", x=128)), reads=['zf'], writes=['kvA'])
        else:
            dst = (self.dout[f'ks{l}'] if is_s else self.dout[f'kp{l}'][rows, :])
            P.dma('sp', lambda e, dst=dst: e.dma_start(out=dst, in_=zk), reads=['zf'])
        P.op('act', lambda e: e.copy(out=self.zb[:, 0:nk], in_=zk), reads=['zf'], writes=['zb'])
        ktdst = self.KTn if is_s else self.KTst
        ktn = 'KTn' if is_s else 'KTst'
        for h in range(HK):
            P.op('pe', lambda e, h=h: e.transpose(out=self.psT[0:64, h * 128:(h + 1) * 128], in_=self.zb[:, h * 64:(h + 1) * 64],
                                                  identity=self.identb[:]), reads=['zb', 'identb'], writes=['psT'])
        P.op('dve', lambda e: e.tensor_copy(out=ktdst[0:64, 0:HK, :], in_=self.psT[0:64, 0:HK * 128].rearrange("p (a b) -> p a b", b=128)),
             reads=['psT'], writes=[ktn])
        if not is_s:
            for c in range(HK // 2):
                P.dma('sp', lambda e, c=c: e.dma_start(out=self.KT_in[c][:, rows].rearrange("(h d) n -> d h n", d=64),
                                                       in_=self.KTst[:, 2 * c:2 * c + 2, :]), reads=['KTst'], writes=['KT_in'])
        def v_cons(n0, w, pst, pn):
            if kind == 0:
                P.op('act', lambda e: e.copy(out=vdst(), in_=pst[:, 0:w].rearrange("p (g x) -> p g x", x=128)), reads=[pn], writes=['kvA'])
            else:
                P.op('act', lambda e: e.copy(out=self.kvA[:, 0:w], in_=pst[:, 0:w]), reads=[pn], writes=['kvA'])
            if kind == 0:
                P.op('dve', lambda e: e.tensor_copy(out=self.Vst[:, 0:6, 0:64].rearrange("p (g a) d -> p g a d", a=2),
                                                    in_=vdst().rearrange("p g (a d) -> p g a d", d=64)), reads=['kvA'], writes=['Vst'])
            else:
                P.op('dve', lambda e: e.tensor_copy(out=self.Vst[:, 0:HK, 0:64], in_=self.kvA[:, 0:w].rearrange("p (h d) -> p h d", d=64)),
                     reads=['kvA'], writes=['Vst'])
        self.proj(L['v'][0], nv, v_cons)
        if kind == 0:
            if is_s:
                for g, (W, _) in enumerate(DIL):
                    o = self.dout[f'sto{l}_{W}']
                    for s_ in range(cfg.NSQ):
                        P.dma('sp', lambda e, o=o, W=W, g=g, s_=s_: e.dma_start(out=o[s_, W - 8:W, :], in_=self.kvA[8 * s_:8 * s_ + 8, g * 256:(g + 1) * 256]),
                              reads=['kvA'])
            else:
                P.dma('sp', lambda e: e.dma_start(out=self.dout[f'kvp{l}'][rows, :], in_=self.kvA[:]), reads=['kvA'])
        else:
            dst = (self.dout[f'vs{l}'] if is_s else self.dout[f'vp{l}'][rows, :])
            P.dma('sp', lambda e, dst=dst: e.dma_start(out=dst, in_=self.kvA[:, 0:nv]), reads=['kvA'], writes=[f'vout{l}'] if is_s else [])
        if not is_s:
            P.dma('sp', lambda e: e.dma_start(out=self.V_in[ti // 4][:, (ti % 4) * 396:(ti % 4) * 396 + HK * 66],
                                              in_=self.Vst[:, 0:HK, :].rearrange("p a b -> p (a b)")), reads=['Vst'], writes=['V_in'])
        else:
            P.op('pool', lambda e: e.tensor_copy(out=self.Vnew[:, 0:HK, :], in_=self.Vst[:, 0:HK, :]), reads=['Vst'], writes=['Vnew'])
        if kind == 2:
            def ki_cons(n0, w, pst, pn):
                self.qk_post(pst, pn, w, None, None, ti, self.kif[:, 0:64], 'kif', do_norm=False)
            self.proj(L['ki'][0], 64, ki_cons)
            dst = (self.dout[f'kis{l}'] if is_s else self.dout[f'kip{l}'][rows, :])
            P.dma('sp', lambda e, dst=dst: e.dma_start(out=dst, in_=self.kif[:]), reads=['kif'])
            P.op('act', lambda e: e.copy(out=self.zb[:, 0:64], in_=self.kif[:]), reads=['kif'], writes=['zb'])
            P.op('pe', lambda e: e.transpose(out=self.psT[0:64, 0:128], in_=self.zb[:, 0:64], identity=self.identb[:]),
                 reads=['zb', 'identb'], writes=['psT'])
            kd = self.kiTn if is_s else self.kiTst
            kdn = 'kiTn' if is_s else 'kiTst'
            P.op('dve', lambda e: e.tensor_copy(out=kd[:], in_=self.psT[0:64, 0:128]), reads=['psT'], writes=[kdn])
            if not is_s:
                P.dma('sp', lambda e: e.dma_start(out=self.KI_in[:, rows], in_=self.kiTst[:]), reads=['kiTst'], writes=['KI_in'])

    def phase2_proj(self, l, kind, L, ti):
        P = self.P
        nq = L['q'][1]
        ng = L['g'][1]
        HQ = L['HQ']

        def q_cons(n0, w, pst, pn):
            self.qk_post(pst, pn, w, self.qg, 'qg', ti, self.zf[:, n0:n0 + w], 'zf')
        self.proj(L['q'][0], nq, q_cons)
        P.op('act', lambda e: e.copy(out=self.zb[:, 0:nq], in_=self.zf[:, 0:nq]), reads=['zf'], writes=['zb'])
        for h0 in range(0, HQ, 8):
            for h in range(h0, min(HQ, h0 + 8)):
                P.op('pe', lambda e, h=h, h0=h0: e.transpose(out=self.psT[0:64, (h - h0) * 128:(h - h0 + 1) * 128],
                                                             in_=self.zb[:, h * 64:(h + 1) * 64], identity=self.identb[:]),
                     reads=['zb', 'identb'], writes=['psT'])
            n = min(HQ, h0 + 8) - h0
            P.op('dve', lambda e, h0=h0, n=n: e.tensor_copy(out=self.QT[0:64, h0:h0 + n, :],
                                                            in_=self.psT[0:64, 0:n * 128].rearrange("p (a b) -> p a b", b=128)), reads=['psT'], writes=['QT'])
        if kind == 1:
            for h0 in range(0, 16, 4):
                pst, pn = (self.psM[0], 'psM0') if (h0 // 4) % 2 == 0 else (self.psM[1], 'psM1')
                for h in range(h0, h0 + 4):
                    P.op('pe', lambda e, h=h, h0=h0, pst=pst: e.transpose(out=pst[0:64, (h - h0) * 128:(h - h0 + 1) * 128],
                                                                          in_=self.zf[:, h * 64:(h + 1) * 64], identity=self.identf[:]),
                         reads=['zf', 'identf'], writes=[pn])
                P.op('act', lambda e, h0=h0, pst=pst: e.copy(out=self.QTf[:, h0:h0 + 4, :], in_=pst[0:64, 0:512].rearrange("p (a b) -> p a b", b=128)),
                     reads=[pn], writes=['QTf'])

        def g_cons(n0, w, pst, pn):
            P.op('act', lambda e: e.activation(out=self.g[:, n0:n0 + w], in_=pst[:, 0:w], func=AF.Silu), reads=[pn], writes=['g'])
        self.proj(L['g'][0], ng, g_cons)
        if kind == 2:
            def wi_cons(n0, w, pst, pn):
                P.op('act', lambda e: e.activation(out=self.sgn[:], in_=pst[:, 0:8], func=AF.Sign), reads=[pn], writes=['sgn'])
                P.op('dve', lambda e: e.scalar_tensor_tensor(out=self.wabs[:], in0=pst[:, 0:8], scalar=0.125 * 8 ** -0.5, in1=self.sgn[:],
                                                             op0=ALU.mult, op1=ALU.mult), reads=[pn, 'sgn'], writes=['wabs'])
            self.proj(L['wi'][0], 8, wi_cons)

            def qi_cons(n0, w, pst, pn):
                self.qk_post(pst, pn, w, None, None, ti, self.zf[:, 0:512], 'zf', do_norm=False)
            self.proj(L['qi'][0], 512, qi_cons)
            P.op('dve', lambda e: e.tensor_tensor(out=self.zb[:, 0:512].rearrange("p (h d) -> p h d", d=64),
                                                  in0=self.zf[:, 0:512].rearrange("p (h d) -> p h d", d=64),
                                                  in1=self.wabs[:, :].unsqueeze(2).to_broadcast([128, 8, 64]), op=ALU.mult),
                 reads=['zf', 'wabs'], writes=['zb'])
            for h in range(8):
                P.op('pe', lambda e, h=h: e.transpose(out=self.psT[0:64, h * 128:(h + 1) * 128], in_=self.zb[:, h * 64:(h + 1) * 64],
                                                      identity=self.identb[:]), reads=['zb', 'identb'], writes=['psT'])
            P.op('dve', lambda e: e.tensor_copy(out=self.qiT[:], in_=self.psT[0:64, 0:1024].rearrange("p (a b) -> p a b", b=128)),
                 reads=['psT'], writes=['qiT'])

    def finish_tile(self, l, L, ti, xdst):
        P = self.P
        OUT = L['OUT']
        OC = OUT // 128
        s = ti % 2
        xt, xn = self.xt[s], f'xt{s}'
        P.op('dve', lambda e: e.tensor_tensor(out=self.og[:, 0:OUT], in0=self.otok[:, 0:OUT], in1=self.g[:, 0:OUT], op=ALU.mult),
             reads=['otok', 'g'], writes=['og'])
        for c in range(OC):
            P.op('pe', lambda e, c=c: e.transpose(out=self.psT[:, c * 128:(c + 1) * 128], in_=self.og[:, c * 128:(c + 1) * 128],
                                                  identity=self.identb[:]), reads=['og', 'identb'], writes=['psT'])
        P.op('act', lambda e: e.copy(out=self.ogT[:, 0:OC, :], in_=self.psT[:, 0:OC * 128].rearrange("p (a b) -> p a b", b=128)),
             reads=['psT'], writes=['ogT'])
        wob3 = self.Wob[l].rearrange("p (c n) -> p c n", c=OC)
        for hf in range(2):
            u = self._uid
            self._uid += 1
            pst, pn = self.psM[u % 2], f'psM{u % 2}'
            ws, wn = self.ws[u % 2], f'ws{u % 2}'
            P.dma('sp', lambda e, ws=ws, hf=hf: e.dma_start(out=ws[:, 0:OC, :], in_=wob3[:, :, hf * 512:(hf + 1) * 512]),
                  reads=[f'Wob{l}'], writes=[wn])
            for c in range(OC):
                P.op('pe', lambda e, c=c, pst=pst, ws=ws: e.matmul(pst[:, :], lhsT=self.ogT[:, c, :], rhs=ws[:, c, :],
                                                                 start=(c == 0), stop=(c == OC - 1)), reads=['ogT', wn], writes=[pn])
            P.op('dve', lambda e, hf=hf, pst=pst: e.tensor_tensor(out=xt[:, hf * 512:(hf + 1) * 512], in0=pst[:, :],
                                                                in1=xt[:, hf * 512:(hf + 1) * 512], op=ALU.add), reads=[pn, xn], writes=[xn])
        P.dma('sp', lambda e: e.dma_start(out=xdst, in_=xt[:]), reads=[xn], writes=['Xd'])

    BIGN = ['bigA', 'bigB']

    def st_block(self, kt_ap, kt_names, q_ap, q_names, nk, ncol, mask_ap=None, mask_names=(), pts=None):
        P = self.P
        u = self._uid
        self._uid += 1
        pss, psn = self.psS[u % 2], f'psS{u % 2}'
        if pts is None:
            pt, ptn = self.PT[u % 3], f'PT{u % 3}'
        else:
            pt, ptn = pts[u % 3], f'PTs{u % 3}'
        P.op('pe', lambda e: e.matmul(pss[0:nk, 0:ncol], lhsT=kt_ap, rhs=q_ap, start=True, stop=True),
             reads=list(kt_names) + list(q_names), writes=[psn])
        P.op('act', lambda e: e.activation(out=pt[0:nk, 0:ncol], in_=pss[0:nk, 0:ncol], func=AF.Exp, scale=0.125),
             reads=[psn], writes=[ptn])
        if mask_ap is not None:
            eng = 'dve' if (u % 2 == 0) else 'pool'
            P.op(eng, lambda e: e.tensor_tensor(out=pt[0:nk, 0:ncol], in0=pt[0:nk, 0:ncol], in1=mask_ap, op=ALU.mult),
                 reads=[ptn] + list(mask_names), writes=[ptn])
        return pt, ptn

    def normalize(self, nslots):
        P = self.P
        for b in range((nslots + 6) // 7):
            n = min(7, nslots - 7 * b)
            acc = self.psA[b][:, 0:n * 65].rearrange("p (s c) -> p s c", c=65)
            pn = f'psA{b}'
            P.op('dve', lambda e, acc=acc, n=n, b=b: e.reciprocal(out=self.rec[:, 7 * b:7 * b + n], in_=acc[:, :, 64:65].rearrange("p s c -> p (s c)")),
                 reads=[pn], writes=['rec'])
            P.op('dve', lambda e, acc=acc, n=n, b=b: e.tensor_tensor(
                out=self.otok[:, 7 * b * 64:(7 * b + n) * 64].rearrange("p (s d) -> p s d", d=64), in0=acc[:, :, 0:64],
                in1=self.rec[:, 7 * b:7 * b + n].unsqueeze(2).to_broadcast([128, n, 64]), op=ALU.mult),
                reads=[pn, 'rec'], writes=['otok'])

    def zero_acc(self, nbanks, M=128):
        P = self.P
        for b in range(nbanks):
            P.op('pe', lambda e, b=b: e.matmul(self.psA[b][0:M, :], lhsT=self.zrow[:, 0:M], rhs=self.zrow[:, 0:512],
                                               start=True, stop=True), reads=['zrow'], writes=[f'psA{b}'])

    def acc_ap(self, slot):
        b, i = divmod(slot, 7)
        return self.psA[b][:, i * 65:(i + 1) * 65], f'psA{b}'

    def load_chunk(self, jp, HK, onehot=False):
        P = self.P
        u = self._uid
        self._uid += 1
        s = u % 2
        ktc, vc = self.KTc[s], self.Vc[s]
        cols = slice(jp * 128, (jp + 1) * 128)
        for h in range(HK):
            src = self.KT_all[h // 2].rearrange("(r x) n -> x r n", r=4)[(h % 2) * 64:(h % 2 + 1) * 64, :, cols]
            P.dma('sp', lambda e, h=h, src=src: e.dma_start(out=ktc[0:64, h, :].rearrange("p (r n) -> p r n", r=4), in_=src),
                  reads=['KT_all'], writes=[f'KTc{s}'])
        if onehot:
            for h in range(HK):
                P.dma('sp', lambda e, h=h: e.dma_start(out=ktc[64:96, h, :], in_=self.din['t_onehotB'][:, jp * 512:(jp + 1) * 512]),
                      writes=[f'KTc{s}'])
        src = self.V_all[jp // 4].rearrange("(r p) n -> p r n", r=4)[:, :, (jp % 4) * 396:(jp % 4 + 1) * 396]
        P.dma('sp', lambda e: e.dma_start(out=vc[:], in_=src), reads=['V_all'], writes=[f'Vc{s}'])
        return s

    def attn_A_prompt(self, j):
        P = self.P
        blocks = [(i, g, dj, rp) for i, (g, dj, rp) in enumerate(A_TILES) if j - dj >= 0]
        nb = len(blocks)
        cur = None
        self.zero_acc(2)
        for bi, (mi, g, dj, rp) in enumerate(blocks):
            if cur is None or cur[0] != dj:
                cur = (dj, self.load_chunk(j - dj, 6))
            s = cur[1]
            for G in range(2):
                hk = g * 2 + G
                kt_ap = self.KTc[s][0:64, hk, rp * 128:(rp + 1) * 128]
                q_ap = self.QT[0:64, g * 8 + G * 4:g * 8 + G * 4 + 4, :]
                mask_ap = self.maskA[:, mi, :].unsqueeze(1).to_broadcast([128, 4, 128])
                pt, ptn = self.st_block(kt_ap, [f'KTc{s}'], q_ap, ['QT'], 128, 512, mask_ap=mask_ap, mask_names=['ltab'])
                for R in range(4):
                    acc, an = self.acc_ap(G * 4 + R)
                    v_ap = self.Vc[s][:, rp, hk * 66:hk * 66 + 65]
                    P.op('pe', lambda e, acc=acc, pt=pt, R=R, v_ap=v_ap, bi=bi: e.matmul(
                        acc, lhsT=pt[:, R * 128:(R + 1) * 128], rhs=v_ap, start=False, stop=(bi == nb - 1)),
                        reads=[ptn, f'Vc{s}'], writes=[an])
        self.normalize(8)

    def attn_BC_blocks(self, j, Kc, mask_fn):
        P = self.P
        nblk = (j + 1) * 4
        self.zero_acc(3)
        for jp in range(j + 1):
            s = self.load_chunk(jp, 4, onehot=(Kc == 96))
            masks = mask_fn(jp)
            for rp in range(4):
                bi = jp * 4 + rp
                for hk in range(4):
                    kt_ap = self.KTc[s][0:Kc, hk, rp * 128:(rp + 1) * 128]
                    q_ap = self.QT[0:Kc, hk * 4:hk * 4 + 4, :]
                    m = masks[rp] if masks else None
                    pt, ptn = self.st_block(kt_ap, [f'KTc{s}'], q_ap, ['QT'], 128, 512,
                                            mask_ap=(m[0] if m else None), mask_names=(m[1] if m else ()))
                    for R in range(4):
                        acc, an = self.acc_ap(hk * 4 + R)
                        v_ap = self.Vc[s][:, rp, hk * 66:hk * 66 + 65]
                        P.op('pe', lambda e, acc=acc, pt=pt, R=R, v_ap=v_ap, bi=bi: e.matmul(
                            acc, lhsT=pt[:, R * 128:(R + 1) * 128], rhs=v_ap, start=False, stop=(bi == nblk - 1)),
                            reads=[ptn, f'Vc{s}'], writes=[an])
        self.normalize(16)

    def moba_kmean(self, L):
        P, cfg = self.P, self.cfg
        NT = cfg.NT
        NBK = 2 * NT
        for h in range(4):
            src = self.KT_all[h // 2].rearrange("(r x) n -> x r n", r=4)[(h % 2) * 64:(h % 2 + 1) * 64, :, :]
            stg = self.kmstage[0:64, 0:4 * NT * 128]
            P.dma('sp', lambda e, src=src, stg=stg: e.dma_start(out=stg.rearrange("p (r n) -> p r n", r=4), in_=src),
                  reads=['KT_all'], writes=self.BIGN)
            P.op('dve', lambda e, stg=stg: e.tensor_reduce(out=self.ksum[:, :, :].rearrange("p r j -> p (r j)"),
                                                           in_=stg.rearrange("p (t k) -> p t k", k=128), axis=AX.X, op=ALU.add),
                 reads=self.BIGN, writes=['ksum'])
            for hf in range(2):
                P.op('dve', lambda e, h=h, hf=hf: e.tensor_tensor(
                    out=self.kmT[:, h, 0:NBK].rearrange("p (j f) -> p j f", f=2)[:, :, hf:hf + 1].rearrange("p j f -> p (j f)"),
                    in0=self.ksum[:, 2 * hf, :], in1=self.ksum[:, 2 * hf + 1, :], op=ALU.add),
                    reads=['ksum'], writes=['kmT'])
        P.op('dve', lambda e: e.tensor_scalar(out=self.kmT[:, :, 0:NBK], in0=self.kmT[:, :, 0:NBK], scalar1=1.0 / 256, scalar2=None, op0=ALU.mult),
             reads=['kmT'], writes=['kmT'])

    def attn_B_prompt(self, j):
        P, cfg = self.P, self.cfg
        NBK = 2 * cfg.NT
        pst, pn = self.psM[0], 'psM0'
        if NBK < 8:
            P.op('dve', lambda e: e.memset(self.bsm[:], -BIGF), writes=['bsm'])
        for h in range(16):
            P.op('pe', lambda e, h=h: e.matmul(pst[:, h * 32:h * 32 + NBK], lhsT=self.QTf[:, h, :], rhs=self.kmT[:, h // 4, 0:NBK],
                                               start=True, stop=True), reads=['QTf', 'kmT'], writes=[pn])
        P.op('dve', lambda e: e.tensor_tensor(out=self.bsm[:, :, 0:NBK], in0=pst[:, :].rearrange("p (h n) -> p h n", n=32)[:, :, 0:NBK],
                                              in1=self.pastB[:, j, 0:NBK].unsqueeze(1).to_broadcast([128, 16, NBK]), op=ALU.add),
             reads=[pn, 'ltab'], writes=['bsm'])
        for h in range(16):
            P.op('dve', lambda e, h=h: e.max(out=self.top8[:, h, :], in_=self.bsm[:, h, 0:max(NBK, 8)]), reads=['bsm'], writes=['top8'])
        P.op('dve', lambda e: e.tensor_scalar(out=self.thr[:], in0=self.top8[:, :, 2:3].rearrange("p h o -> p (h o)"), scalar1=-1e29, scalar2=None,
                                              op0=ALU.max), reads=['top8'], writes=['thr'])
        P.op('dve', lambda e: e.tensor_tensor(out=self.bsm[:, :, 0:NBK], in0=self.bsm[:, :, 0:NBK],
                                              in1=self.thr[:, :].unsqueeze(2).to_broadcast([128, 16, NBK]), op=ALU.is_ge),
             reads=['bsm', 'thr'], writes=['bsm'])
        P.op('dve', lambda e: e.tensor_tensor(out=self.bsm[:, :, 0:NBK], in0=self.bsm[:, :, 0:NBK],
                                              in1=self.ownB[:, j, 0:NBK].unsqueeze(1).to_broadcast([128, 16, NBK]), op=ALU.add),
             reads=['bsm', 'ltab'], writes=['bsm'])
        P.op('pool', lambda e: e.memset(self.biasq[:], NEG), writes=['biasq'])
        P.op('dve', lambda e: e.tensor_scalar(out=self.biasq[:, :, 0:NBK], in0=self.bsm[:, :, 0:NBK], scalar1=-1.0, scalar2=-NEG,
                                              op0=ALU.add, op1=ALU.mult), reads=['bsm', 'biasq'], writes=['biasq'])
        for h0 in (0, 8):
            for h in range(h0, h0 + 8):
                P.op('pe', lambda e, h=h, h0=h0: e.transpose(out=self.psT[64:96, (h - h0) * 128:(h - h0 + 1) * 128], in_=self.biasq[:, h, :],
                                                             identity=self.identb[:]), reads=['biasq', 'identb'], writes=['psT'])
            P.op('act', lambda e, h0=h0: e.copy(out=self.QT[64:96, h0:h0 + 8, :], in_=self.psT[64:96, 0:1024].rearrange("p (a b) -> p a b", b=128)),
                 reads=['psT'], writes=['QT'])

        def mask_fn(jp):
            if jp == j:
                return [(self.maskC[:, rp, :].unsqueeze(1).to_broadcast([128, 4, 128]), ['maskC']) for rp in range(4)]
            return None
        self.attn_BC_blocks(j, 96, mask_fn)

    def dsa_select(self, L_keys, n_fullvis, names):
        P = self.P
        bis = self.bis
        sc = self.big[:, 0:L_keys]
        P.op('dve', lambda e: e.tensor_reduce(out=bis[:, 1:2], in_=sc, axis=AX.X, op=ALU.max), reads=names, writes=['bis'])
        nd = L_keys - n_fullvis
        P.op('dve', lambda e: e.tensor_tensor(out=self.Rf[0][:, 0:nd], in0=self.big[:, n_fullvis:L_keys], in1=self.cposQ[:, 0:nd], op=ALU.max),
             reads=names + ['ltab'], writes=['Rf0'])
        P.op('dve', lambda e: e.tensor_reduce(out=bis[:, 0:1], in_=self.Rf[0][:, 0:nd], axis=AX.X, op=ALU.min), reads=['Rf0'], writes=['bis'])
        if n_fullvis > 0:
            P.op('dve', lambda e: e.tensor_reduce(out=bis[:, 7:8], in_=self.big[:, 0:n_fullvis], axis=AX.X, op=ALU.min),
                 reads=names, writes=['bis'])
            P.op('dve', lambda e: e.tensor_tensor(out=bis[:, 0:1], in0=bis[:, 0:1], in1=bis[:, 7:8], op=ALU.min), reads=['bis'], writes=['bis'])
        for it in range(24):
            eng = 'dve'
            P.op(eng, lambda e: e.scalar_tensor_tensor(out=bis[:, 2:3], in0=bis[:, 0:1], scalar=bis[:, 1:2], in1=self.consts[:, 1:2],
                                                       op0=ALU.add, op1=ALU.mult), reads=['bis', 'consts'], writes=['bis'])
            P.op(eng, lambda e: e.tensor_scalar(out=self.big2[:, 0:L_keys], in0=sc, scalar1=bis[:, 2:3], scalar2=0.0, op0=ALU.is_ge, op1=ALU.add,
                                                accum_out=bis[:, 3:4]), reads=names + ['bis'], writes=['big2', 'bis'])
            P.op(eng, lambda e: e.tensor_scalar(out=bis[:, 4:5], in0=bis[:, 3:4], scalar1=255.5, scalar2=None, op0=ALU.is_ge),
                 reads=['bis'], writes=['bis'])
            P.op(eng, lambda e: e.tensor_tensor(out=bis[:, 5:6], in0=bis[:, 2:3], in1=bis[:, 0:1], op=ALU.subtract), reads=['bis'], writes=['bis'])
            P.op(eng, lambda e: e.tensor_tensor(out=bis[:, 6:7], in0=bis[:, 1:2], in1=bis[:, 2:3], op=ALU.subtract), reads=['bis'], writes=['bis'])
            P.op(eng, lambda e: e.scalar_tensor_tensor(out=bis[:, 0:1], in0=bis[:, 5:6], scalar=bis[:, 4:5], in1=bis[:, 0:1],
                                                       op0=ALU.mult, op1=ALU.add), reads=['bis'], writes=['bis'])
            P.op(eng, lambda e: e.scalar_tensor_tensor(out=bis[:, 1:2], in0=bis[:, 6:7], scalar=bis[:, 4:5], in1=bis[:, 2:3],
                                                       op0=ALU.mult, op1=ALU.add), reads=['bis'], writes=['bis'])
        pst, pn = self.psM[1], 'psM1'
        P.op('dve', lambda e: e.tensor_scalar(out=self.dth[:], in0=self.identf[:], scalar1=bis[:, 0:1], scalar2=None, op0=ALU.mult),
             reads=['bis', 'identf'], writes=['dth'])
        P.op('pe', lambda e: e.matmul(pst[:, 128:256], lhsT=self.onesf[:], rhs=self.dth[:], start=True, stop=True),
             reads=['onesf', 'dth'], writes=[pn])
        P.op('act', lambda e: e.copy(out=self.thb[:], in_=pst[:, 128:256]), reads=[pn], writes=['thb'])

    def dsa_maskT(self, k0, n, dst, dst_name, names):
        P = self.P
        u = self._uid
        self._uid += 1
        pst, pn = self.psM[u % 2], f'psM{u % 2}'
        for kt in range(k0, k0 + n):
            P.op('pe', lambda e, kt=kt: e.transpose(out=pst[:, (kt - k0) * 128:(kt - k0 + 1) * 128],
                                                    in_=self.big[:, kt * 128:(kt + 1) * 128], identity=self.identf[:]),
                 reads=names + ['identf'], writes=[pn])
        P.op('dve', lambda e: e.tensor_tensor(out=dst[:, 0:n, :], in0=pst[:, 0:n * 128].rearrange("p (a b) -> p a b", b=128),
                                              in1=self.thb[:, :].unsqueeze(1).to_broadcast([128, n, 128]), op=ALU.is_ge),
             reads=[pn, 'thb'], writes=[dst_name])

    def attn_C_prompt(self, j):
        P = self.P
        Lk = (j + 1) * 512
        BN = self.BIGN
        for jp in range(j + 1):
            u = self._uid
            self._uid += 1
            s = u % 2
            src = self.KI_all.rearrange("(r x) n -> x r n", r=4)[:, :, jp * 128:(jp + 1) * 128]
            P.dma('sp', lambda e, s=s, src=src: e.dma_start(out=self.kiTc[s][:].rearrange("p (r n) -> p r n", r=4), in_=src),
                  reads=['KI_all'], writes=[f'kiTc{s}'])
            sc = self.big[:, jp * 512:(jp + 1) * 512]
            for h in range(8):
                b = self._uid % 2
                self._uid += 1
                pss, psn = self.psS[b], f'psS{b}'
                rf, rfn = self.Rf[b], f'Rf{b}'
                P.op('pe', lambda e, h=h, pss=pss, s=s: e.matmul(pss[:, :], lhsT=self.qiT[:, h, :], rhs=self.kiTc[s][:, :], start=True, stop=True),
                     reads=['qiT', f'kiTc{s}'], writes=[psn])
                P.op('act', lambda e, pss=pss, rf=rf: e.activation(out=rf[:], in_=pss[:, :], func=AF.Relu), reads=[psn], writes=[rfn])
                if h == 0:
                    P.op('dve', lambda e, rf=rf, sc=sc: e.tensor_scalar(out=sc, in0=rf[:], scalar1=self.sgn[:, 0:1], scalar2=None, op0=ALU.mult),
                         reads=[rfn, 'sgn'], writes=BN)
                else:
                    P.op('dve', lambda e, rf=rf, sc=sc, h=h: e.scalar_tensor_tensor(out=sc, in0=rf[:], scalar=self.sgn[:, h:h + 1], in1=sc,
                                                                                    op0=ALU.mult, op1=ALU.add), reads=[rfn, 'sgn'] + BN, writes=BN)
            if jp == j:
                P.op('dve', lambda e, sc=sc: e.tensor_tensor(out=sc, in0=sc, in1=self.cbQ, op=ALU.add), reads=BN + ['ltab'], writes=BN)
        self.dsa_select(Lk, Lk - 512, BN)

        def mask_fn(jp):
            u = self._uid
            self._uid += 1
            mt, mtn = self.maskTc[u % 2], f'maskTc{u % 2}'
            self.dsa_maskT(jp * 4, 4, mt, mtn, BN)
            return [(mt[:, rp, :].unsqueeze(1).to_broadcast([128, 4, 128]), [mtn]) for rp in range(4)]
        self.attn_BC_blocks(j, 64, mask_fn)

    def sample_pv(self, s, slot, kt_ap, kt_names, Kc, hq0, nk, v_ap, v_names, mask_ap, mask_names, first, last):
        P = self.P
        q_ap = self.QT[0:Kc, hq0:hq0 + 4, 8 * s:8 * s + 8]
        pt, ptn = self.st_block(kt_ap, kt_names, q_ap, ['QT'], nk, 32, mask_ap=mask_ap, mask_names=mask_names, pts=self.PTs)
        acc = self.psA[0][0:32, slot * 65:(slot + 1) * 65]
        if first and slot == 0:
            self.zero_acc(1, M=32)
        P.op('pe', lambda e: e.matmul(acc, lhsT=pt[0:nk, 0:32], rhs=v_ap, start=False, stop=last), reads=[ptn] + list(v_names), writes=['psA0'])

    def sample_finish_seq(self, s, nslots):
        P = self.P
        acc = self.psA[0][0:32, 0:nslots * 65].rearrange("p (s c) -> p s c", c=65)
        P.op('dve', lambda e: e.reciprocal(out=self.recs[:, 0:nslots], in_=acc[:, :, 64:65].rearrange("p s c -> p (s c)")),
             reads=['psA0'], writes=['recs'])
        P.op('dve', lambda e: e.tensor_tensor(out=self.osn[:, 0:nslots, :], in0=acc[:, :, 0:64],
                                              in1=self.recs[:, 0:nslots].unsqueeze(2).to_broadcast([32, nslots, 64]), op=ALU.mult),
             reads=['psA0', 'recs'], writes=['osn'])
        for R in range(4):
            dst = self.otok[8 * s:8 * s + 8, 0:nslots * 256].rearrange("p (s x) -> p s x", x=256)[:, :, R * 64:(R + 1) * 64]
            P.dma('sp', lambda e, dst=dst, R=R: e.dma_start(out=dst, in_=self.osn[R * 8:(R + 1) * 8, 0:nslots, :]), reads=['osn'], writes=['otok'])

    def new_v(self, s, HK):
        P = self.P
        u = self._uid
        self._uid += 1
        vn, vnn = self.Vn2[u % 2], f'Vn2_{u % 2}'
        P.dma('sp', lambda e: e.dma_start(out=vn[0:8, 0:HK, :], in_=self.Vnew[8 * s:8 * s + 8, 0:HK, :]), reads=['Vnew'], writes=[vnn])
        return vn, vnn

    def stage_piece(self, load_fn, nh, kt_dst0, G_of, v_dst):
        P = self.P
        u = self._uid
        self._uid += 1
        sk, skn = self.stK[u % 2], f'stK{u % 2}'
        sv, svn = self.stV[u % 2], f'stV{u % 2}'
        load_fn(sk, skn, sv, svn)
        for h in range(nh):
            pst, pn = self.psM[(u + h) % 2], f'psM{(u + h) % 2}'
            for i in range(2):
                P.op('pe', lambda e, i=i, h=h, pst=pst: e.transpose(out=pst[0:64, i * 128:(i + 1) * 128], in_=sk[:, i, h * 64:(h + 1) * 64],
                                                                    identity=self.identf[:]), reads=[skn, 'identf'], writes=[pn])
            P.op('act', lambda e, h=h, pst=pst: e.copy(out=self.KTs[0:64, G_of(h), kt_dst0 * 128:(kt_dst0 + 2) * 128], in_=pst[0:64, 0:256]),
                 reads=[pn], writes=['bigA'])
        v_dst(sv, svn)

    def attn_A_sample(self, l):
        P, cfg = self.P, self.cfg
        P.op('pool', lambda e: e.memset(self.VsA[:, :, :, 64:65], 1.0), writes=['bigB'])
        for s in range(cfg.NSQ):
            blocks = []
            kbase = 0
            for g, (W, d) in enumerate(DIL):
                nkt = W // 128
                if W > 8:
                    P.dma('sp', lambda e, W=W, s=s: e.dma_start(out=self.dout[f'sto{l}_{W}'][s, 0:W - 8, :], in_=self.din[f'st{l}_{W}'][s, 8:W, :]))
                if nkt == 1:
                    pieces = [(0, 1)]
                else:
                    pieces = [(k, 2) for k in range(0, nkt, 2)]
                for (k0, n) in pieces:
                    src = self.din[f'st{l}_{W}'][s, k0 * 128:(k0 + n) * 128, :].rearrange("(k p) f -> p k f", p=128)

                    def load_fn(sk, skn, sv, svn, src=src, n=n):
                        P.dma('sp', lambda e: e.dma_start(out=sk[:, 0:n, :], in_=src), writes=[skn])

                    def v_dst(sv, svn, k0=k0, n=n, kbase=kbase):
                        pass
                    u = self._uid
                    self._uid += 1
                    sk, skn = self.stK[u % 2], f'stK{u % 2}'
                    load_fn(sk, skn, None, None)
                    for G in range(2):
                        pst, pn = self.psM[(u + G) % 2], f'psM{(u + G) % 2}'
                        for i in range(n):
                            P.op('pe', lambda e, i=i, G=G, pst=pst, sk=sk: e.transpose(out=pst[0:64, i * 128:(i + 1) * 128], in_=sk[:, i, G * 64:(G + 1) * 64],
                                                                                      identity=self.identf[:]), reads=[skn, 'identf'], writes=[pn])
                        P.op('act', lambda e, G=G, pst=pst, n=n, k0=k0, kbase=kbase: e.copy(
                            out=self.KTsA[0:64, G, (kbase + k0) * 128:(kbase + k0 + n) * 128], in_=pst[0:64, 0:n * 128]), reads=[pn], writes=['bigA'])
                    P.op('pool', lambda e, sk=sk, n=n, k0=k0, kbase=kbase: e.tensor_copy(
                        out=self.VsA[:, kbase + k0:kbase + k0 + n, 0:2, 0:64], in_=sk[:, 0:n, 128:256].rearrange("p k (g d) -> p k g d", d=64)),
                        reads=[skn], writes=['bigB'])
                for kt in range(nkt + 1):
                    blocks.append((g, kt, nkt, kbase))
                kbase += nkt
            nb = len(blocks)
            vn, vnn = self.new_v(s, 6)
            for bi, (g, kt, nkt, kb) in enumerate(blocks):
                for G in range(2):
                    hk = g * 2 + G
                    if kt < nkt:
                        kt_ap, ktn = self.KTsA[0:64, G, (kb + kt) * 128:(kb + kt + 1) * 128], ['bigA']
                        v_ap, vnm, nk = self.VsA[:, kb + kt, G, 0:65], ['bigB'], 128
                    else:
                        kt_ap, ktn = self.KTn[0:64, hk, 8 * s:8 * s + 8], ['KTn']
                        v_ap, vnm, nk = vn[0:8, hk, 0:65], [vnn], 8
                    mask_ap = self.maskAs[g][0:nk, kt, :]
                    self.sample_pv(s, G, kt_ap, ktn, 64, g * 8 + G * 4, nk, v_ap, vnm, mask_ap, [f'maskAs{g}'],
                                   first=(bi == 0), last=(bi == nb - 1))
            self.sample_finish_seq(s, 2)

    def page_rows(self, s, pg, pool_name, dst, dst_name):
        P = self.P
        col = s * NPG + pg
        halves = self.pools[pool_name]
        hw = 128 if len(halves) == 2 else 64
        for h in range(len(halves)):
            P.dma('pool', lambda e, h=h: e.indirect_dma_start(out=dst[:, h * hw:(h + 1) * hw], out_offset=None, in_=halves[h],
                                                              in_offset=bass.IndirectOffsetOnAxis(ap=self.pidx[:, col:col + 1], axis=0)),
                  reads=['pidx', 'pool_' + pool_name], writes=[dst_name])

    def sample_kv_load(self, s, kname, vname):
        P = self.P
        for k0 in range(0, NPG, 2):
            u = self._uid
            self._uid += 1
            sk, skn = self.stK[u % 2], f'stK{u % 2}'
            sv, svn = self.stV[u % 2], f'stV{u % 2}'
            for i in range(2):
                self.page_rows(s, k0 + i, kname, sk[:, i, :], skn)
                self.page_rows(s, k0 + i, vname, sv[:, i, :], svn)
            for hk in range(4):
                pst, pn = self.psM[(u + hk) % 2], f'psM{(u + hk) % 2}'
                for i in range(2):
                    P.op('pe', lambda e, i=i, hk=hk, pst=pst, sk=sk: e.transpose(out=pst[0:64, i * 128:(i + 1) * 128], in_=sk[:, i, hk * 64:(hk + 1) * 64],
                                                                               identity=self.identf[:]), reads=[skn, 'identf'], writes=[pn])
                P.op('act', lambda e, hk=hk, pst=pst, k0=k0: e.copy(out=self.KTs[0:64, hk, k0 * 128:(k0 + 2) * 128], in_=pst[0:64, 0:256]),
                     reads=[pn], writes=['bigA'])
            P.op('pool', lambda e, sv=sv, k0=k0: e.tensor_copy(out=self.Vs[:, k0:k0 + 2, :, 0:64], in_=sv[:, :, :].rearrange("p k (h d) -> p k h d", d=64)),
                 reads=[svn], writes=['bigB'])

    def attn_B_sample(self, l):
        P, cfg = self.P, self.cfg
        P.op('pool', lambda e: e.memset(self.Vs[:, :, :, 64:65], 1.0), writes=['bigB'])
        P.op('pool', lambda e: e.memset(self.QT[64:96, 0:16, :], 0.0), writes=['QT'])
        for h in range(4):
            P.dma('sp', lambda e, h=h: e.dma_start(out=self.KTs[64:96, h, :], in_=self.din['t_onehotS']), writes=['bigA'])
            P.dma('sp', lambda e, h=h: e.dma_start(out=self.KTn[64:96, h, :], in_=self.din['t_onehotS'][:, 16 * 128:17 * 128]), writes=['KTn'])
        for s in range(cfg.NSQ):
            self.sample_kv_load(s, 'c1k', 'c1v')
            P.op('dve', lambda e: e.tensor_reduce(out=self.kmTs[:, :, 0:8], in_=self.KTs[0:64, :, 0:2048].rearrange("p h (n k) -> p h n k", k=256),
                                                  axis=AX.X, op=ALU.add), reads=['bigA'], writes=['kmTs'])
            P.op('dve', lambda e, s=s: e.tensor_reduce(out=self.kmTs[:, :, 8:9], in_=self.KTn[0:64, 0:4, 8 * s:8 * s + 8].unsqueeze(2),
                                                       axis=AX.X, op=ALU.add), reads=['KTn'], writes=['kmTs'])
            P.op('dve', lambda e: e.tensor_scalar(out=self.kmTs[:, :, 0:9], in0=self.kmTs[:, :, 0:9], scalar1=1.0 / 256, scalar2=None, op0=ALU.mult),
                 reads=['kmTs'], writes=['kmTs'])
            pst, pn = self.psM[0], 'psM0'
            P.op('pool', lambda e, s=s: e.tensor_copy(out=self.qsel[:], in_=self.QTf[:, :, 8 * s:8 * s + 8]), reads=['QTf'], writes=['qsel'])
            for hk in range(4):
                P.op('pe', lambda e, hk=hk: e.matmul(pst[0:32, hk * 16:hk * 16 + 9], lhsT=self.qsel[:, hk * 4:hk * 4 + 4, :].rearrange("p a b -> p (a b)"),
                                                     rhs=self.kmTs[:, hk, 0:9], start=True, stop=True), reads=['qsel', 'kmTs'], writes=[pn])
            P.op('dve', lambda e: e.tensor_copy(out=self.bss[:, :, 0:9], in_=pst[0:32, 0:64].rearrange("p (h n) -> p h n", n=16)[:, :, 0:9]),
                 reads=[pn], writes=['bss'])
            for hk in range(4):
                P.op('dve', lambda e, hk=hk: e.max(out=self.top8s[:, hk, :], in_=self.bss[:, hk, 0:8]), reads=['bss'], writes=['top8s'])
            P.op('dve', lambda e: e.tensor_tensor(out=self.bss[:, :, 0:8], in0=self.bss[:, :, 0:8],
                                                  in1=self.top8s[:, :, 2:3].to_broadcast([32, 4, 8]), op=ALU.is_ge), reads=['bss', 'top8s'], writes=['bss'])
            P.op('dve', lambda e: e.memset(self.bss[:, :, 8:9], 1.0), reads=['bss'], writes=['bss'])
            P.op('dve', lambda e: e.tensor_scalar(out=self.biass[:, :, 0:9], in0=self.bss[:, :, 0:9], scalar1=-1.0, scalar2=-NEG,
                                                  op0=ALU.add, op1=ALU.mult), reads=['bss'], writes=['biass'])
            for hk in range(4):
                P.op('pe', lambda e, hk=hk: e.transpose(out=self.psT[64:73, hk * 32:(hk + 1) * 32], in_=self.biass[:, hk, 0:9], identity=self.identb[0:32, 0:32]),
                     reads=['biass', 'identb'], writes=['psT'])
            P.op('act', lambda e, s=s: e.copy(out=self.QT[64:73, 0:16, 8 * s:8 * s + 8], in_=self.psT[64:73, 0:128].rearrange("p (h t) -> p h t", t=8)),
                 reads=['psT'], writes=['QT'])
            vn, vnn = self.new_v(s, 4)
            for kt in range(17):
                for hk in range(4):
                    if kt < 16:
                        kt_ap, ktn = self.KTs[0:96, hk, kt * 128:(kt + 1) * 128], ['bigA']
                        v_ap, vnm, nk, m, mn = self.Vs[:, kt, hk, 0:65], ['bigB'], 128, None, ()
                    else:
                        kt_ap, ktn = self.KTn[0:96, hk, 8 * s:8 * s + 8], ['KTn']
                        v_ap, vnm, nk, m, mn = vn[0:8, hk, 0:65], [vnn], 8, self.maskNew[0:8, :], ['maskNew']
                    self.sample_pv(s, hk, kt_ap, ktn, 96, hk * 4, nk, v_ap, vnm, m, mn, first=(kt == 0), last=(kt == 16))
            self.sample_finish_seq(s, 4)

    def attn_C_sample(self, l):
        P, cfg = self.P, self.cfg
        NSQ = cfg.NSQ
        LK = 2048 + 8
        BA = ['bigA']
        P.op('pool', lambda e: e.memset(self.big[:, 0:2176], 0.0), writes=BA)
        chunks = [(0, 512), (512, 512), (1024, 512), (1536, 512), (2048, 8)]
        for s in range(NSQ):
            for k0 in range(0, NPG, 4):
                u = self._uid
                self._uid += 1
                si, sin_ = self.stI[u % 2], f'stI{u % 2}'
                for i in range(4):
                    self.page_rows(s, k0 + i, 'c2i', si[:, i, :], sin_)
                pst, pn = self.psM[u % 2], f'psM{u % 2}'
                for i in range(4):
                    P.op('pe', lambda e, i=i, pst=pst, si=si: e.transpose(out=pst[0:64, i * 128:(i + 1) * 128], in_=si[:, i, :], identity=self.identf[:]),
                         reads=[sin_, 'identf'], writes=[pn])
                P.op('act', lambda e, pst=pst, k0=k0: e.copy(out=self.kiT1[:, k0 * 128:(k0 + 4) * 128], in_=pst[0:64, 0:512]), reads=[pn], writes=['kiT1'])
            P.op('pool', lambda e, s=s: e.tensor_copy(out=self.kiT1[:, 2048:2056], in_=self.kiTn[:, 8 * s:8 * s + 8]), reads=['kiTn'], writes=['kiT1'])
            qp, qpn = self.qpad[s % 2], f'qpad{s % 2}'
            P.op('pool', lambda e, qp=qp: e.memset(qp[:], 0.0), writes=[qpn])
            P.op('pool', lambda e, qp=qp, s=s: e.tensor_copy(out=qp[:, :, 8 * s:8 * s + 8], in_=self.qiT[:, :, 8 * s:8 * s + 8]), reads=['qiT'], writes=[qpn])
            for (c0, cw) in chunks:
                sc = self.big[:, c0:c0 + cw]
                for h in range(8):
                    b = self._uid % 2
                    self._uid += 1
                    pss, psn = self.psS[b], f'psS{b}'
                    rf, rfn = self.Rf[b], f'Rf{b}'
                    P.op('pe', lambda e, qp=qp, pss=pss, c0=c0, cw=cw, h=h: e.matmul(pss[:, 0:cw], lhsT=qp[:, h, :], rhs=self.kiT1[:, c0:c0 + cw],
                                                                                   start=True, stop=True), reads=[qpn, 'kiT1'], writes=[psn])
                    P.op('act', lambda e, pss=pss, rf=rf, cw=cw: e.activation(out=rf[:, 0:cw], in_=pss[:, 0:cw], func=AF.Relu), reads=[psn], writes=[rfn])
                    P.op('dve', lambda e, rf=rf, sc=sc, h=h, cw=cw: e.scalar_tensor_tensor(out=sc, in0=rf[:, 0:cw], scalar=self.sgn[:, h:h + 1], in1=sc,
                                                                                          op0=ALU.mult, op1=ALU.add), reads=[rfn, 'sgn'] + BA, writes=BA)
        P.op('dve', lambda e: e.tensor_tensor(out=self.big[:, 2048:2056], in0=self.big[:, 2048:2056], in1=self.cbS[:], op=ALU.add),
             reads=BA + ['cbS'], writes=BA)
        P.op('dve', lambda e: e.tensor_scalar(out=self.cposQ[:, 0:8], in0=self.cbS[:], scalar1=-2.0, scalar2=-BIGF, op0=ALU.mult, op1=ALU.add),
             reads=['cbS'], writes=['ltab'])
        self.dsa_select(LK, 2048, BA)
        P.op('dve', lambda e: e.memset(self.big[:, 2056:2176], -BIGF), reads=BA, writes=BA)
        for k0 in range(0, 17, 4):
            n = min(4, 17 - k0)
            self.dsa_maskT(k0, n, self.maskTs[:, k0:k0 + n, :], 'maskTs', BA)
        P.op('pool', lambda e: e.memset(self.Vs[:, :, :, 64:65], 1.0), writes=['bigB'])
        for s in range(NSQ):
            self.sample_kv_load(s, 'c2k', 'c2v')
            vn, vnn = self.new_v(s, 4)
            for kt in range(17):
                for hk in range(4):
                    if kt < 16:
                        kt_ap, ktn = self.KTs[0:64, hk, kt * 128:(kt + 1) * 128], ['bigA']
                        v_ap, vnm, nk = self.Vs[:, kt, hk, 0:65], ['bigB'], 128
                    else:
                        kt_ap, ktn = self.KTn[0:64, hk, 8 * s:8 * s + 8], ['KTn']
                        v_ap, vnm, nk = vn[0:8, hk, 0:65], [vnn], 8
                    m = self.maskTs[0:nk, kt, 8 * s:8 * s + 8].unsqueeze(1).to_broadcast([nk, 4, 8])
                    self.sample_pv(s, hk, kt_ap, ktn, 64, hk * 4, nk, v_ap, vnm, m, ['maskTs'], first=(kt == 0), last=(kt == 16))
            self.sample_finish_seq(s, 4)


_CACHE = {}


def get_program(cfg_key):
    if cfg_key not in _CACHE:
        cfg = Cfg(*cfg_key)
        b = Builder(cfg)
        nc = b.build()
        _CACHE[cfg_key] = (cfg, b, nc)
    return _CACHE[cfg_key]


def shard_inputs(cfg, inputs):
    NT, NSQ = cfg.NT, cfg.NSQ
    maps = []
    f32 = lambda a: np.ascontiguousarray(np.asarray(a), dtype=np.float32)
    xp = np.asarray(inputs['x_prompt'])
    xs = np.asarray(inputs['x_sample'])
    pt = np.asarray(inputs['page_table']).astype(np.int32)
    for c in range(8):
        b, r = divmod(c, 4)
        m = {}
        tiles = xp[b].reshape(cfg.SEQ // 128, 128, 1024)[r::4]
        m['xp'] = f32(tiles.reshape(NT * 128, 1024))
        xs_c = np.zeros((128, 1024), np.float32)
        xs_c[:NSQ * 8] = xs[c * NSQ:(c + 1) * NSQ].reshape(NSQ * 8, 1024)
        m['xs'] = xs_c
        for l in (0, 3):
            if l < cfg.depth:
                for W, _ in DIL:
                    m[f'st{l}_{W}'] = f32(np.asarray(inputs[f'state_l{l}_kv_w{W}'])[c * NSQ:(c + 1) * NSQ].reshape(NSQ, W, 256))
        for nm, key, wd in (("c1k", 'cache_l1_k', 256), ("c1v", 'cache_l1_v', 256), ("c2k", 'cache_l2_k', 256),
                            ("c2v", 'cache_l2_v', 256), ("c2i", 'cache_l2_kidx', 64)):
            if key in inputs:
                a = np.asarray(inputs[key])
                m[nm] = f32(a.reshape(cfg.NPOOL // 40, 4, 10 * 128 * wd)[:, r].reshape(cfg.PSH * 128, wd))
            else:
                m[nm] = np.zeros((cfg.PSH * 128, wd), np.float32)
        m['pt'] = np.ascontiguousarray(pt[c * NSQ:(c + 1) * NSQ].reshape(1, NSQ * NPG))
        for l in range(cfg.depth):
            m[f'norm{l}'] = f32(inputs[f'l{l}_norm']).reshape(1, 1024)
            m[f'win{l}'] = f32(inputs[f'l{l}_w_in'])
            m[f'qn{l}'] = f32(inputs[f'l{l}_q_norm']).reshape(1, 64)
            m[f'kn{l}'] = f32(inputs[f'l{l}_k_norm']).reshape(1, 64)
            m[f'wout{l}'] = f32(inputs[f'l{l}_w_out'])
        for k, v in host_tables(cfg, r).items():
            m['t_' + k] = v
        maps.append(m)
    return maps


def assemble(cfg, res):
    NT, NSQ, SEQ, NS = cfg.NT, cfg.NSQ, cfg.SEQ, cfg.NS

    def prompt_rows(name, width):
        out = np.zeros((2, SEQ // 128, 128, width), np.float32)
        for c in range(8):
            b, r = divmod(c, 4)
            out[b, r::4] = res[c][name].reshape(NT, 128, width)
        return out.reshape(2, SEQ, width)

    def sample_rows(name, width):
        return np.concatenate([res[c][name][:NSQ * 8].reshape(NSQ, 8, width) for c in range(8)], 0)
    outs = [prompt_rows('yp', 1024), sample_rows('ys', 1024)]
    for l in range(cfg.depth):
        kind = l % 3
        if kind == 0:
            kv = prompt_rows(f'kvp{l}', 768).reshape(2, SEQ, 3, 2, 2, 64)
            for g, (W, _) in enumerate(DIL):
                w = min(W, SEQ)
                outs.append(np.ascontiguousarray(kv[:, SEQ - w:, g]))
                outs.append(np.concatenate([res[c][f'sto{l}_{W}'] for c in range(8)], 0).reshape(NS, W, 2, 2, 64))
        else:
            outs.append(prompt_rows(f'kp{l}', 256).reshape(2, SEQ, 4, 64))
            outs.append(sample_rows(f'ks{l}', 256).reshape(NS, 8, 4, 64))
            outs.append(prompt_rows(f'vp{l}', 256).reshape(2, SEQ, 4, 64))
            outs.append(sample_rows(f'vs{l}', 256).reshape(NS, 8, 4, 64))
            if kind == 2:
                outs.append(prompt_rows(f'kip{l}', 64))
                outs.append(sample_rows(f'kis{l}', 64))
    return tuple(outs)


def run_cfg(cfg_key, inputs):
    cfg, b, nc = get_program(cfg_key)
    maps = shard_inputs(cfg, inputs)
    res = run_bass_kernel_spmd(nc, maps, core_ids=list(range(8)))
    return assemble(cfg, res.results)


def kernel(**inputs):
    return run_cfg((8192, 128, 2560, 4), inputs)
```
